# Optimizing a Trainium2 kernel written in Bass

```python
import math
import jax
import jax.numpy as jnp
from jax import lax
import numpy as np

D_MODEL = 1024
BATCH = 8
SEQ = 4096
DEPTH = 4

GDN_HEADS = 4
GDN_DK = 128
GDN_DV = 128
GDN_CONV = 4
GDN_CHUNK = 64
DSA_HEADS = 4
DSA_HEAD_DIM = 64
DSA_Q_RANK = 256
DSA_KV_RANK = 128
IDX_HEADS = 8
IDX_DIM = 32
IDX_TOPK = 256
DSA_QBLOCK = 128
MLSTM_HEADS = 4
MLSTM_DQK = 64
MLSTM_DV = 64
MLSTM_CHUNK = 64
FFN_HIDDEN = (8 * D_MODEL + 3 * 256 - 1) // (3 * 256) * 256
NORM_EPS = 1e-6

IN_WIDTHS = (
    GDN_HEADS * GDN_DK, GDN_HEADS * GDN_DK, GDN_HEADS * GDN_DV, GDN_HEADS * GDN_DV,
    GDN_HEADS, GDN_HEADS,
    DSA_Q_RANK, DSA_KV_RANK, IDX_DIM, IDX_HEADS,
    MLSTM_HEADS * MLSTM_DQK, MLSTM_HEADS * MLSTM_DQK, MLSTM_HEADS * MLSTM_DV,
    MLSTM_HEADS * MLSTM_DV, MLSTM_HEADS, MLSTM_HEADS,
)
IN_DIM = sum(IN_WIDTHS)
MIX_WIDTH = GDN_HEADS * GDN_DV + DSA_HEADS * DSA_HEAD_DIM + MLSTM_HEADS * MLSTM_DV

kernel_name = "hybrid_gdn_dsa_mlstm_trunk"


def rms_norm(x, g):
    xf = x.astype(jnp.float32)
    y = xf * lax.rsqrt(jnp.mean(xf * xf, axis=-1, keepdims=True) + NORM_EPS)
    return (y * g.astype(jnp.float32)).astype(x.dtype)


def l2_normalize(x):
    return x * lax.rsqrt(jnp.sum(x * x, axis=-1, keepdims=True) + NORM_EPS)


def to_heads(x, n_heads):
    b, t, _ = x.shape
    return x.reshape(b, t, n_heads, -1).transpose(0, 2, 1, 3).astype(jnp.float32)


def causal_depthwise_conv(x, w):
    width, t = w.shape[0], x.shape[1]
    xp = jnp.pad(x, ((0, 0), (width - 1, 0), (0, 0)))
    return sum(xp[:, j:j + t] * w[j] for j in range(width))


def split_points(widths):
    pts, acc = [], 0
    for w in widths[:-1]:
        acc += w
        pts.append(acc)
    return pts


def chunk_gated_delta(q, k, v, g, beta):
    b, h, t, dk = q.shape
    dv = v.shape[-1]
    c = GDN_CHUNK
    n = t // c
    q = q * dk ** -0.5
    rs = lambda a: a.reshape(b, h, n, c, *a.shape[3:])
    q, k, v, g, beta = rs(q), rs(k), rs(v), rs(g), rs(beta)
    gc = jnp.cumsum(g, axis=-1)
    incl = jnp.tril(jnp.ones((c, c), dtype=bool))
    strict = jnp.tril(jnp.ones((c, c), dtype=bool), -1)
    decay = jnp.exp(jnp.where(incl, gc[..., :, None] - gc[..., None, :], -jnp.inf))
    kb = k * beta[..., None]
    a_mat = jnp.where(strict, jnp.einsum("bhnid,bhnjd->bhnij", kb, k) * decay, 0.0)
    eye = jnp.eye(c, dtype=jnp.float32)
    t_mat = lax.linalg.triangular_solve(eye + a_mat, jnp.broadcast_to(eye, a_mat.shape),
                                        left_side=True, lower=True)
    u = t_mat @ (v * beta[..., None])
    w = t_mat @ (kb * jnp.exp(gc)[..., None])
    qk = jnp.einsum("bhnid,bhnjd->bhnij", q, k) * decay
    q_dec = q * jnp.exp(gc)[..., None]
    g_last = gc[..., -1]
    k_dec = k * jnp.exp(g_last[..., None] - gc)[..., None]

    def step(s, xs):
        u_c, w_c, qk_c, qd_c, kd_c, gl_c = xs
        v_new = u_c - jnp.einsum("bhck,bhkv->bhcv", w_c, s)
        o = jnp.einsum("bhck,bhkv->bhcv", qd_c, s) + jnp.einsum("bhij,bhjv->bhiv", qk_c, v_new)
        s = s * jnp.exp(gl_c)[..., None, None] + jnp.einsum("bhck,bhcv->bhkv", kd_c, v_new)
        return s, o

    xs = tuple(jnp.moveaxis(a, 2, 0) for a in (u, w, qk, q_dec, k_dec, g_last))
    s0 = jnp.zeros((b, h, dk, dv), jnp.float32)
    _, o = lax.scan(step, s0, xs)
    return jnp.moveaxis(o, 0, 2).reshape(b, h, t, dv)


def gated_deltanet(q, k, v, z, a, bt, conv_w, a_log, dt_bias, out_g):
    bsz, t, _ = q.shape
    qkv = jax.nn.silu(causal_depthwise_conv(jnp.concatenate([q, k, v], axis=-1), conv_w))
    q, k, v = jnp.split(qkv, [GDN_HEADS * GDN_DK, 2 * GDN_HEADS * GDN_DK], axis=-1)
    q = l2_normalize(to_heads(q, GDN_HEADS))
    k = l2_normalize(to_heads(k, GDN_HEADS))
    v = to_heads(v, GDN_HEADS)
    g = -jnp.exp(a_log.astype(jnp.float32)) * jax.nn.softplus(a.astype(jnp.float32) + dt_bias.astype(jnp.float32))
    beta = jax.nn.sigmoid(bt.astype(jnp.float32))
    o = chunk_gated_delta(q, k, v, g.transpose(0, 2, 1), beta.transpose(0, 2, 1))
    o = o.transpose(0, 2, 1, 3)
    zf = z.reshape(bsz, t, GDN_HEADS, GDN_DV).astype(jnp.float32)
    o = rms_norm(o, out_g) * jax.nn.silu(zf)
    return o.reshape(bsz, t, -1)


def dsa_sparse_attention(cq, ckv, k_idx, w_idx, q_norm, kv_norm, w_uq, w_qidx, w_uk, w_uv):
    f32 = jnp.float32
    b, t, _ = cq.shape
    cq = rms_norm(cq, q_norm)
    c_kv = rms_norm(ckv, kv_norm).astype(f32)
    q = (cq @ w_uq).reshape(b, t, DSA_HEADS, DSA_HEAD_DIM).astype(f32)
    q_lat = jnp.einsum("bthd,hcd->bthc", q, w_uk.astype(f32)) * DSA_HEAD_DIM ** -0.5
    q_idx = (cq @ w_qidx).reshape(b, t, IDX_HEADS, IDX_DIM).astype(f32)
    k_i = k_idx.astype(f32)
    w_i = w_idx.astype(f32) * (IDX_HEADS ** -0.5 * IDX_DIM ** -0.5)
    n_sel = min(IDX_TOPK, t // 4)
    key_pos = jnp.arange(t)

    def query_block(blk):
        start = blk * DSA_QBLOCK
        sl = lambda arr: lax.dynamic_slice_in_dim(arr, start, DSA_QBLOCK, axis=1)
        qi, wi, ql = sl(q_idx), sl(w_i), sl(q_lat)
        q_pos = start + jnp.arange(DSA_QBLOCK)
        score = jax.nn.relu(jnp.einsum("bqhd,bsd->bqsh", qi, k_i))
        score = jnp.einsum("bqsh,bqh->bqs", score, wi)
        score = jnp.where(key_pos[None, None, :] <= q_pos[None, :, None], score, -jnp.inf)
        _, sel = lax.top_k(score, n_sel)
        c_sel = jax.vmap(lambda c, ix: c[ix])(c_kv, sel)
        valid = sel <= q_pos[None, :, None]
        logits = jnp.einsum("bqhc,bqkc->bqhk", ql, c_sel)
        logits = jnp.where(valid[:, :, None, :], logits, -jnp.inf)
        p = jax.nn.softmax(logits, axis=-1)
        return jnp.einsum("bqhk,bqkc->bqhc", p, c_sel)

    o_lat = lax.map(query_block, jnp.arange(t // DSA_QBLOCK))
    o_lat = jnp.moveaxis(o_lat, 0, 1).reshape(b, t, DSA_HEADS, DSA_KV_RANK)
    o = jnp.einsum("bthc,hcd->bthd", o_lat, w_uv.astype(f32))
    return o.reshape(b, t, -1)


def chunk_mlstm(q, k, v, li, lf):
    b, h, t, dk = q.shape
    dv = v.shape[-1]
    L = MLSTM_CHUNK
    n = t // L
    ch = lambda a: jnp.moveaxis(a.reshape(b, h, n, L, *a.shape[3:]), 2, 0)
    qc, kc, vc, lic = ch(q), ch(k), ch(v), ch(li)
    bc = jnp.cumsum(ch(lf), axis=-1)
    incl = jnp.tril(jnp.ones((L, L), dtype=bool))

    def step(carry, xs):
        c_st, n_st, m_st = carry
        qx, kx, vx, bx, ix = xs
        d_log = jnp.where(incl, bx[..., :, None] - bx[..., None, :] + ix[..., None, :], -jnp.inf)
        a_log = bx + m_st[..., None]
        m_t = jnp.maximum(a_log, jnp.max(d_log, axis=-1))
        dw = jnp.exp(d_log - m_t[..., None])
        aw = jnp.exp(a_log - m_t)
        s = jnp.einsum("bhid,bhjd->bhij", qx, kx) * dw
        num = aw[..., None] * jnp.einsum("bhid,bhdv->bhiv", qx, c_st) + s @ vx
        den = aw * jnp.einsum("bhid,bhd->bhi", qx, n_st) + jnp.sum(s, axis=-1)
        h_out = num / jnp.maximum(jnp.abs(den), jnp.exp(-m_t))[..., None]
        m_new = m_t[..., -1]
        kw = jnp.exp(bx[..., -1:] - bx + ix - m_new[..., None])
        dec = jnp.exp(bx[..., -1] + m_st - m_new)
        kx_w = kx * kw[..., None]
        c_st = dec[..., None, None] * c_st + jnp.einsum("bhld,bhlv->bhdv", kx_w, vx)
        n_st = dec[..., None] * n_st + jnp.sum(kx_w, axis=2)
        return (c_st, n_st, m_new), h_out

    init = (jnp.zeros((b, h, dk, dv), jnp.float32), jnp.zeros((b, h, dk), jnp.float32),
            jnp.zeros((b, h), jnp.float32))
    _, hs = lax.scan(step, init, (qc, kc, vc, bc, lic))
    return jnp.moveaxis(hs, 0, 2).reshape(b, h, t, dv)


def mlstm(q, k, v, o_pre, i_pre, f_pre, i_bias, f_bias, out_g):
    bsz, t, _ = q.shape
    q = to_heads(q, MLSTM_HEADS)
    k = to_heads(k, MLSTM_HEADS) * MLSTM_DQK ** -0.5
    v = to_heads(v, MLSTM_HEADS)
    li = (i_pre.astype(jnp.float32) + i_bias.astype(jnp.float32)).transpose(0, 2, 1)
    lf = jax.nn.log_sigmoid(f_pre.astype(jnp.float32) + f_bias.astype(jnp.float32)).transpose(0, 2, 1)
    hh = chunk_mlstm(q, k, v, li, lf).transpose(0, 2, 1, 3)
    o_gate = jax.nn.sigmoid(o_pre.reshape(bsz, t, MLSTM_HEADS, MLSTM_DV).astype(jnp.float32))
    return (o_gate * rms_norm(hh, out_g)).reshape(bsz, t, -1)


def setup_inputs(seed: int = 0) -> dict:
    key = jax.random.key(seed)
    ks = jax.random.split(key, 32)
    f32 = jnp.float32
    nrm = lambda k, shape, fan_in: jax.random.normal(k, shape, f32) * fan_in ** -0.5
    gain = lambda k, shape: 1.0 + 0.02 * jax.random.normal(k, shape, f32)
    res_scale = (2 * DEPTH) ** -0.5
    dt = jnp.exp(jax.random.uniform(ks[5], (DEPTH, GDN_HEADS), f32, math.log(1e-3), math.log(1e-1)))
    return {
        "x": jax.random.normal(ks[0], (BATCH, SEQ, D_MODEL), f32),
        "attn_norm": gain(ks[1], (DEPTH, D_MODEL)),
        "w_in": nrm(ks[2], (DEPTH, D_MODEL, IN_DIM), D_MODEL),
        "gdn_conv": nrm(ks[3], (DEPTH, GDN_CONV, 2 * GDN_HEADS * GDN_DK + GDN_HEADS * GDN_DV), GDN_CONV),
        "gdn_a_log": jnp.log(jax.random.uniform(ks[4], (DEPTH, GDN_HEADS), f32, 1.0, 16.0)),
        "gdn_dt_bias": dt + jnp.log(-jnp.expm1(-dt)),
        "gdn_out_norm": gain(ks[6], (DEPTH, GDN_DV)),
        "dsa_q_norm": gain(ks[7], (DEPTH, DSA_Q_RANK)),
        "dsa_kv_norm": gain(ks[8], (DEPTH, DSA_KV_RANK)),
        "dsa_w_uq": nrm(ks[9], (DEPTH, DSA_Q_RANK, DSA_HEADS * DSA_HEAD_DIM), DSA_Q_RANK),
        "dsa_w_qidx": nrm(ks[10], (DEPTH, DSA_Q_RANK, IDX_HEADS * IDX_DIM), DSA_Q_RANK),
        "dsa_w_uk": nrm(ks[11], (DEPTH, DSA_HEADS, DSA_KV_RANK, DSA_HEAD_DIM), DSA_HEAD_DIM),
        "dsa_w_uv": nrm(ks[12], (DEPTH, DSA_HEADS, DSA_KV_RANK, DSA_HEAD_DIM), DSA_KV_RANK),
        "mlstm_i_bias": 0.1 * jax.random.normal(ks[13], (DEPTH, MLSTM_HEADS), f32),
        "mlstm_f_bias": jax.random.uniform(ks[14], (DEPTH, MLSTM_HEADS), f32, 3.0, 6.0),
        "mlstm_out_norm": gain(ks[15], (DEPTH, MLSTM_DV)),
        "w_out": nrm(ks[16], (DEPTH, MIX_WIDTH, D_MODEL), MIX_WIDTH) * res_scale,
        "ffn_norm": gain(ks[17], (DEPTH, D_MODEL)),
        "w_gate": nrm(ks[18], (DEPTH, D_MODEL, FFN_HIDDEN), D_MODEL),
        "w_up": nrm(ks[19], (DEPTH, D_MODEL, FFN_HIDDEN), D_MODEL),
        "w_down": nrm(ks[20], (DEPTH, FFN_HIDDEN, D_MODEL), FFN_HIDDEN) * res_scale,
        "final_norm": gain(ks[21], (D_MODEL,)),
    }


def reference(x, attn_norm, w_in, gdn_conv, gdn_a_log, gdn_dt_bias, gdn_out_norm,
              dsa_q_norm, dsa_kv_norm, dsa_w_uq, dsa_w_qidx, dsa_w_uk, dsa_w_uv,
              mlstm_i_bias, mlstm_f_bias, mlstm_out_norm, w_out, ffn_norm,
              w_gate, w_up, w_down, final_norm):
    pts = split_points(IN_WIDTHS)
    for l in range(DEPTH):
        h = rms_norm(x, attn_norm[l])
        proj = h @ w_in[l]
        (gq, gk, gv, gz, ga, gb, cq, ckv, ik, iw,
         mq, mk, mv, mo, mi, mf) = jnp.split(proj, pts, axis=-1)
        y_a = gated_deltanet(gq, gk, gv, gz, ga, gb, gdn_conv[l], gdn_a_log[l],
                             gdn_dt_bias[l], gdn_out_norm[l])
        y_b = dsa_sparse_attention(cq, ckv, ik, iw, dsa_q_norm[l], dsa_kv_norm[l], dsa_w_uq[l],
                                   dsa_w_qidx[l], dsa_w_uk[l], dsa_w_uv[l])
        y_c = mlstm(mq, mk, mv, mo, mi, mf, mlstm_i_bias[l], mlstm_f_bias[l], mlstm_out_norm[l])
        mix = jnp.concatenate([y_a, y_b, y_c], axis=-1).astype(x.dtype)
        x = x + mix @ w_out[l]
        h = rms_norm(x, ffn_norm[l])
        x = x + (jax.nn.silu(h @ w_gate[l]) * (h @ w_up[l])) @ w_down[l]
    return rms_norm(x, final_norm)
```

```python
from contextlib import ExitStack
import os
import threading
import numpy as np
import concourse.bass as bass
import concourse.mybir as mybir
from concourse.bass_utils import run_bass_kernel_spmd

F32 = mybir.dt.float32
BF16 = mybir.dt.bfloat16
ALU = mybir.AluOpType
AF = mybir.ActivationFunctionType
AX = mybir.AxisListType

D = 1024
KC = 8
FF = 2816
NJ = 22
IN_DIM = 3512
GQ, GK, GV, GZ, GA, GB_, CQ, CKV, IK, IW, MQ, MK, MV, MO, MI, MF = (
    0, 512, 1024, 1536, 2048, 2052, 2056, 2312, 2440, 2472, 2480, 2736, 2992, 3248, 3504, 3508)
EPS = 1e-6
NEG = -30000.0
NBIS = 12


class Buf:
    __slots__ = ("name", "w", "r", "sem", "cnt", "al", "psum")

    def __init__(self, name):
        self.name = name
        self.w = None
        self.r = {}
        self.sem = None
        self.cnt = 0
        self.al = []
        self.psum = False


class Prog:
    ENGS = ("pe", "act", "dve", "pool", "sp")

    def __init__(self, nc, stack):
        self.nc = nc
        self.stack = stack
        self.stream = {e: [] for e in self.ENGS}
        self.cnt = {e: 0 for e in self.ENGS}
        self.waited = {e: {} for e in self.ENGS}
        self.semobj = {e: stack.enter_context(nc.semaphore("s_" + e)) for e in self.ENGS}
        self.nbuf = 0
        self.dsems = []
        self.ilv = None

    def buf(self, name):
        return Buf(name)

    def _need(self, eng, tok):
        if tok is None:
            return
        key, val = tok
        if key == eng:
            if eng == "pe":
                return
            if val < self.cnt[eng] - 1:
                return
        if self.waited[eng].get(key, 0) >= val:
            return
        self.waited[eng][key] = val
        self.stream[eng].append(("w", key, val))

    def _deps(self, eng, reads, writes):
        for b in reads:
            self._need(eng, b.w)
            if b.psum:
                for t in b.r.items():
                    if t[0] != eng:
                        self._need(eng, t)
        for b in writes:
            for x in [b] + b.al:
                self._need(eng, x.w)
                for t in x.r.items():
                    self._need(eng, t)

    def _mark(self, tok, reads, writes):
        for b in reads:
            b.r[tok[0]] = tok[1]
        for b in writes:
            b.w = tok
            b.r = {}

    def op(self, eng, fn, reads=(), writes=()):
        if self.ilv is not None:
            self.ilv()
        self._deps(eng, reads, writes)
        self.cnt[eng] += 1
        tok = (eng, self.cnt[eng])
        self.stream[eng].append(("o", fn, eng, 1))
        self._mark(tok, reads, writes)

    def v(self, eng, meth, reads, writes, **kw):
        self.op(eng, lambda e: getattr(e, meth)(**kw), reads, writes)

    def dma(self, q, out_ap, in_ap, reads, wbuf, **kw):
        self._deps(q, reads, (wbuf,))
        if wbuf.sem is None:
            wbuf.sem = {}
            wbuf.cnt = {}
        if q not in wbuf.sem:
            self.nbuf += 1
            key = ("d", self.nbuf)
            wbuf.sem[q] = key
            wbuf.cnt[q] = 0
            self.dsems.append((wbuf, q))
            self.semobj[key] = self.stack.enter_context(self.nc.semaphore("d%d" % self.nbuf))
        wbuf.cnt[q] += 16
        key = wbuf.sem[q]
        tok = (key, wbuf.cnt[q])
        self.stream[q].append(("o", lambda e: e.dma_start(out=out_ap, in_=in_ap, **kw), key, 16))
        self._mark(tok, reads, (wbuf,))

    def barrier(self):
        toks = [(e, self.cnt[e]) for e in self.ENGS if self.cnt[e] > 0]
        toks += [(b.sem[q], b.cnt[q]) for (b, q) in self.dsems]
        for e in self.ENGS:
            for t in toks:
                if t[0] != e:
                    self._need(e, t)

    def final_wait(self, eng, bufs):
        for b in bufs:
            self._need(eng, b.w)

    def emit(self):
        nc = self.nc
        with nc.Block() as block:
            def mk(ename):
                def body(e):
                    for it in self.stream[ename]:
                        if it[0] == "w":
                            e.wait_ge(self.semobj[it[1]], it[2])
                        else:
                            it[1](e).then_inc(self.semobj[it[2]], it[3])
                return body
            block.tensor(mk("pe"))
            block.scalar(mk("act"))
            block.vector(mk("dve"))
            block.gpsimd(mk("pool"))
            block.sync(mk("sp"))


def bc(ap, shape):
    return ap.to_broadcast(list(shape))


class Builder:
    def __init__(self, T, layers, final, ksel, dbg=None, first=True):
        self.T, self.layers, self.final, self.ksel, self.dbg = T, layers, final, ksel, dbg
        self.NT = T // 128
        nc = self.nc = bass.Bass("TRN2", target_bir_lowering=False)
        L = len(layers)
        di = lambda n, s: nc.dram_tensor(n, list(s), F32, kind="ExternalInput").ap()
        self.x_in = di("x", (T, D))
        self.prm = dict(
            attn_norm=di("attn_norm", (L, D)), w_in=di("w_in", (L, D, IN_DIM)),
            gdn_conv=di("gdn_conv", (L, 4, 1536)), gdn_a_log=di("gdn_a_log", (L, 4)),
            gdn_dt_bias=di("gdn_dt_bias", (L, 4)), gdn_out_norm=di("gdn_out_norm", (L, 128)),
            dsa_q_norm=di("dsa_q_norm", (L, 256)), dsa_kv_norm=di("dsa_kv_norm", (L, 128)),
            dsa_w_uq=di("dsa_w_uq", (L, 256, 256)), dsa_w_qidx=di("dsa_w_qidx", (L, 256, 256)),
            dsa_w_uk=di("dsa_w_uk", (L, 4, 128, 64)), dsa_w_uv=di("dsa_w_uv", (L, 4, 128, 64)),
            mlstm_i_bias=di("mlstm_i_bias", (L, 4)), mlstm_f_bias=di("mlstm_f_bias", (L, 4)),
            mlstm_out_norm=di("mlstm_out_norm", (L, 64)), w_out=di("w_out", (L, D, D)),
            ffn_norm=di("ffn_norm", (L, D)), w_gate=di("w_gate", (L, D, FF)),
            w_up=di("w_up", (L, D, FF)), w_down=di("w_down", (L, FF, D)),
            final_norm=di("final_norm", (D,)))
        self.y_out = nc.dram_tensor("y", [T, D], F32, kind="ExternalOutput").ap()
        self.xres = nc.dram_tensor("xres", [T, D], F32).ap()
        if dbg:
            self.dbg_out = nc.dram_tensor("dbg", [T, D], F32, kind="ExternalOutput").ap()
        with ExitStack() as st:
            self.st = st
            self.P = Prog(nc, st)
            self.pre_alloc()
            self.alloc()
            self.consts()
            for li in range(L):
                self.layer(li)
            self.P.final_wait("sp", [self.b_y] + ([self.b_dbg] if dbg else []))
            self.P.final_wait("pool", [self.b_y])
            self.P.emit()

    def sb(self, name, shape, dt=F32):
        t = self.nc.alloc_sbuf_tensor(name, list(shape), dt)
        b = self.P.buf(name)
        return t, b

    def frame_alloc(self, name, shape, dt=F32, part=128):
        n = 1
        for d in shape[1:]:
            n *= d
        units = n * (2 if dt == F32 else 1)
        units += units % 2
        self.fo = (self.fo + 31) // 32 * 32
        assert self.fo + units <= self.FN, (name, self.fo, units, self.FN)
        ap = self.F[:, self.fo:self.fo + units]
        self.fo += units
        self.fmax = max(self.fmax, self.fo)
        if dt == F32:
            ap = ap.bitcast(F32)
        ap = ap[:, 0:n]
        if len(shape) == 3:
            ap = ap.rearrange("p (a b) -> p a b", a=shape[1])
        if shape[0] != 128:
            ap = ap[0:shape[0]]
        return ap, self.P.buf(name)

    def pre_alloc(self):
        self.xt = [self.sb("xt%d" % i, (128, D)) for i in range(2)]
        self.xti = 0
        self.junk, self.b_junk = self.sb("junk", (128, 1536))
        self.st1 = [self.sb("st1_%d" % i, (128, 4)) for i in range(2)]
        self.xn, self.b_xn = self.sb("xn", (128, D), BF16)
        self.hT, self.b_hT = self.sb("hT", (128, KC, 128), BF16)
        self.gA, _ = self.sb("gA", (128, KC))
        self.gF, _ = self.sb("gF", (128, KC))

    def alloc(self):
        P, nc, T = self.P, self.nc, self.T
        self.FN = 94000
        self.F = nc.alloc_sbuf_tensor("F", [128, self.FN], BF16)
        self.fo = 0
        self.fmax = 0
        fa = self.frame_alloc
        self.Wg, self.b_Wg = fa("Wg", (128, KC, FF), BF16)
        self.Wu, self.b_Wu = fa("Wu", (128, KC, FF), BF16)
        self.Wd, self.b_Wd = fa("Wd", (128, NJ, D), BF16)
        self.actT, self.b_actT = fa("actT", (128, NJ, 128), BF16)
        self.sg, self.b_sg = fa("sg", (128, 128))
        self.gN, self.b_gN = fa("gN", (128, D))
        self.fo = 0
        self.Wi, self.b_Wi = fa("Wi", (128, KC, IN_DIM), BF16)
        self.Wo, self.b_Wo = fa("Wo", (128, KC, D), BF16)
        self.kidx, self.b_kidx = fa("kidx", (128, T), BF16)
        self.ckvT, self.b_ckvT = fa("ckvT", (128, T), BF16)
        self.ckvK, self.b_ckvK = fa("ckvK", (128, T // 128, 128), BF16)
        arena0 = self.fo
        self.score, self.b_score = fa("score", (128, T))
        self.Pm, self.b_Pm = fa("Pm", (128, T), BF16)
        self.maskb, self.b_maskb = fa("maskb", (128, T), BF16)
        arena1 = max(self.fo, arena0 + 16384)
        big = [self.b_score, self.b_Pm, self.b_maskb]
        self.fo = arena0
        f4 = (128, 4, 128)
        ov = []

        def fo_(name, shape, dt=F32):
            ap, b = fa(name, shape, dt)
            ov.append(b)
            return ap, b
        self.cbuf, self.b_cbuf = fo_("cbuf", (128, 12, 131))
        self.cact, self.b_cact = fo_("cact", (128, 12, 128))
        self.rn, self.b_rn = fo_("rn", (128, 8, 128))
        self.Qm, self.b_Qm = fo_("Qm", f4)
        self.G4, self.b_G4 = fo_("G4", f4)
        self.dcT, self.b_dcT = fo_("dcT", f4)
        self.egcB, self.b_egcB = fo_("egcB", f4)
        self.qT, self.b_qT = fo_("qT", f4, BF16)
        self.kT, self.b_kT = fo_("kT", f4, BF16)
        self.qdT, self.b_qdT = fo_("qdT", f4, BF16)
        self.kbg, self.b_kbg = fo_("kbg", f4, BF16)
        self.kd0, self.b_kd0 = fo_("kd0", f4, BF16)
        self.kd1, self.b_kd1 = fo_("kd1", f4, BF16)
        self.vb, self.b_vb = fo_("vb", f4, BF16)
        assert self.fo <= arena1, (self.fo, arena1)
        for b_ in ov:
            b_.al = list(big)
        for b_ in big:
            b_.al = list(ov)
        self.fo = arena1
        self.mixer_alloc()
        self.ps = []
        for i in range(8):
            t = self.st.enter_context(nc.psum_tensor("ps%d" % i, [128, 512], F32))
            pb_ = P.buf("ps%d" % i)
            pb_.psum = True
            self.ps.append((t, pb_))
        self.psi = 0
        self._tl = threading.local()
        self.b_y = P.buf("y")
        self.b_dbg = P.buf("dbg")
        self.b_xres = P.buf("xres")
        self.b_prm = P.buf("prm")

    def bank(self, excl=()):
        grp = getattr(self._tl, "grp", None)
        if grp is not None:
            lst, st = grp
            t, b = self.ps[lst[st[0] % len(lst)]]
            st[0] += 1
            return t, b
        while True:
            t, b = self.ps[self.psi % 8]
            self.psi += 1
            if not any(t is x for x in excl):
                return t, b

    def interleave(self, fa, fb, banks_a, banks_b):
        P = self.P
        sa, sb_ = threading.Semaphore(0), threading.Semaphore(0)
        done = {"a": False, "b": False}
        err = []

        def run(me, other, f, banks, s_me, s_other):
            self._tl.grp = (banks, [0])
            self._tl.me = me
            s_me.acquire()
            try:
                f()
            except BaseException as ex:
                err.append(ex)
            done[me] = True
            s_other.release()

        def hook():
            me = getattr(self._tl, "me", None)
            if me is None:
                return
            other = "b" if me == "a" else "a"
            if not done[other]:
                (sb_ if me == "a" else sa).release()
                (sa if me == "a" else sb_).acquire()
        ta = threading.Thread(target=run, args=("a", "b", fa, banks_a, sa, sb_))
        tb_ = threading.Thread(target=run, args=("b", "a", fb, banks_b, sb_, sa))
        P.ilv = hook
        ta.start()
        tb_.start()
        sa.release()
        ta.join()
        tb_.join()
        P.ilv = None
        if err:
            raise err[0]

    def consts(self):
        P = self.P
        sel = lambda out, pat, cmp, fill, base, cm, bufs: P.v(
            "pool", "affine_select", bufs, bufs, out=out, in_=out, pattern=pat, compare_op=cmp,
            fill=fill, base=base, channel_multiplier=cm)

        def chunkify(t, b, fill):
            sel(t[:, 0:64], [[0, 64]], ALU.is_ge, fill, 63, -1, [b])
            sel(t[:, 64:128], [[0, 64]], ALU.is_ge, fill, -64, 1, [b])
        self.identf, self.b_c = self.sb("identf", (128, 128))
        bcst = [self.b_c]
        P.v("pool", "memset", [], bcst, ap=self.identf[:], constant=1.0)
        sel(self.identf[:], [[-1, 128]], ALU.is_equal, 0.0, 0, 1, bcst)
        self.identb, _ = self.sb("identb", (128, 128), BF16)
        P.v("pool", "tensor_copy", bcst, bcst, out=self.identb[:], in_=self.identf[:])
        self.onesf, _ = self.sb("onesf", (128, 128))
        P.v("pool", "memset", [], bcst, ap=self.onesf[:], constant=1.0)
        self.onesb, _ = self.sb("onesb", (128, 128), BF16)
        P.v("pool", "memset", [], bcst, ap=self.onesb[:], constant=1.0)
        self.ones128b, _ = self.sb("ones128b", (128, 128), BF16)
        P.v("pool", "memset", [], bcst, ap=self.ones128b[:], constant=128.0)
        self.utriC, _ = self.sb("utriC", (128, 128))
        P.v("pool", "memset", [], bcst, ap=self.utriC[:], constant=1.0)
        sel(self.utriC[:], [[1, 128]], ALU.is_ge, 0.0, 0, -1, bcst)
        chunkify(self.utriC, self.b_c, 0.0)
        self.bonesC, _ = self.sb("bonesC", (128, 128))
        P.v("pool", "memset", [], bcst, ap=self.bonesC[:], constant=1.0)
        chunkify(self.bonesC, self.b_c, 0.0)
        self.mnegT, _ = self.sb("mnegT", (128, 128))
        P.v("pool", "memset", [], bcst, ap=self.mnegT[:], constant=0.0)
        sel(self.mnegT[:], [[1, 128]], ALU.is_ge, NEG, 0, -1, bcst)
        chunkify(self.mnegT, self.b_c, NEG)
        self.mposL, _ = self.sb("mposL", (128, 128))
        P.v("pool", "memset", [], bcst, ap=self.mposL[:], constant=0.0)
        sel(self.mposL[:], [[-1, 128]], ALU.is_ge, -NEG, -1, 1, bcst)
        chunkify(self.mposL, self.b_c, -NEG)
        self.caus, _ = self.sb("caus", (128, 128))
        P.v("pool", "memset", [], bcst, ap=self.caus[:], constant=0.0)
        sel(self.caus[:], [[-1, 128]], ALU.is_ge, -1e30, 0, 1, bcst)
        self.causb, _ = self.sb("causb", (128, 128), BF16)
        P.v("pool", "memset", [], bcst, ap=self.causb[:], constant=0.0)
        sel(self.causb[:], [[-1, 128]], ALU.is_ge, NEG, 0, 1, bcst)
        self.rowm, _ = self.sb("rowm", (128, 2))
        P.v("pool", "memset", [], bcst, ap=self.rowm[:], constant=1.0)
        sel(self.rowm[:, 0:1], [[0, 1]], ALU.is_ge, 0.0, 63, -1, bcst)
        sel(self.rowm[:, 1:2], [[0, 1]], ALU.is_ge, 0.0, -64, 1, bcst)
        self.epsc, _ = self.sb("epsc", (128, 2))
        P.v("pool", "memset", [], bcst, ap=self.epsc[:, 0:1], constant=EPS)
        P.v("pool", "memset", [], bcst, ap=self.epsc[:, 1:2], constant=128.0 * EPS)
        self.pows, _ = self.sb("pows", (128, NBIS + 1))
        for k in range(NBIS + 1):
            P.v("pool", "memset", [], bcst, ap=self.pows[:, k:k + 1], constant=float(2.0 ** -(k + 1)))

    def rstd(self, src_ap, src_bufs, n, st, b_st, col, eps_col=0, post_scale=None):
        P = self.P
        P.v("act", "activation", src_bufs, [self.b_junk, b_st], out=self.junk[:, 0:n], in_=src_ap,
            func=AF.Square, accum_out=st[:, col:col + 1])
        P.v("act", "activation", [b_st, self.b_c], [b_st], out=st[:, col:col + 1], in_=st[:, col:col + 1],
            func=AF.Ln, scale=1.0 / n, bias=self.epsc[:, eps_col:eps_col + 1])
        P.v("act", "activation", [b_st], [b_st], out=st[:, col:col + 1], in_=st[:, col:col + 1],
            func=AF.Exp, scale=-0.5)

    def norm_T(self, xt, b_xt, gain, ncols_off, width=128):
        P = self.P
        st, b_st = self.st1[self.psi % 2]
        self.rstd(xt[:], [b_xt], D, st, b_st, 0)
        P.v("act", "activation", [b_xt, b_st], [self.b_xn], out=self.xn[:], in_=xt[:], func=AF.Copy,
            scale=st[:, 0:1])
        pt, b_pt = self.bank()
        ptb = pt[:].bitcast(BF16)
        xn, identb = self.xn, self.identb

        def tr(e):
            for k in range(KC):
                i = e.transpose(out=ptb[:, k * 128:(k + 1) * 128], in_=xn[:, k * 128:(k + 1) * 128],
                                identity=identb[:])
            return i
        P.op("pe", tr, [self.b_xn, self.b_c], [b_pt])
        P.v("dve", "tensor_tensor", [b_pt, self.b_prm], [self.b_hT],
            out=self.hT[:, :, ncols_off:ncols_off + 128],
            in0=ptb.rearrange("p (k t) -> p k t", k=KC),
            in1=bc(gain[:].unsqueeze(2), (128, KC, 128)), op=ALU.mult)

    def load_x(self, li, ti):
        P = self.P
        xt, b_xt = self.xt[self.xti % 2]
        self.xti += 1
        if li == 0:
            P.dma("sp", xt[:], self.x_in[ti * 128:(ti + 1) * 128, :], [], b_xt)
        else:
            P.dma("sp", xt[:], self.xres[ti * 128:(ti + 1) * 128, :], [self.b_xres], b_xt)
        return xt, b_xt

    def layer(self, li):
        P, prm = self.P, self.prm
        l = li
        P.barrier()
        P.dma("pool", self.Wi, prm["w_in"][l].rearrange("(k p) n -> p k n", p=128), [], self.b_Wi)
        P.dma("pool", self.Wo, prm["w_out"][l].rearrange("(k p) n -> p k n", p=128), [], self.b_Wo)
        P.dma("sp", self.gA[:], prm["attn_norm"][l].rearrange("(k p) -> p k", p=128), [], self.b_prm,
              allow_slow_non_contiguous=True)
        P.dma("sp", self.gF[:], prm["ffn_norm"][l].rearrange("(k p) -> p k", p=128), [], self.b_prm,
              allow_slow_non_contiguous=True)
        self.mixer_setup(li)
        for ti in range(self.NT):
            self.mixer_tile(li, ti)
        P.barrier()
        P.dma("pool", self.Wg, prm["w_gate"][l].rearrange("(k p) n -> p k n", p=128), [], self.b_Wg)
        P.dma("pool", self.Wu, prm["w_up"][l].rearrange("(k p) n -> p k n", p=128), [], self.b_Wu)
        P.dma("pool", self.Wd, prm["w_down"][l].rearrange("(k p) n -> p k n", p=128), [], self.b_Wd)
        last = self.final and li == len(self.layers) - 1
        if last:
            P.dma("sp", self.gN[:], prm["final_norm"].partition_broadcast(128), [], self.b_gN)
        for ti in range(self.NT):
            self.ffn_tile(li, ti, last)

    def ffn_tile(self, li, ti, last):
        P = self.P
        xt, b_xt = self.xt[self.xti % 2]
        self.xti += 1
        P.dma("sp", xt[:], self.xres[ti * 128:(ti + 1) * 128, :], [self.b_xres], b_xt)
        self.norm_T(xt, b_xt, self.gF, 0)
        hT, Wg, Wu, Wd, actT = self.hT, self.Wg, self.Wu, self.Wd, self.actT
        for j in range(NJ):
            pg, b_pg = self.bank()

            def mm(e, j=j, pg=pg):
                for (W, c0) in ((Wg, 0), (Wu, 128)):
                    for k in range(KC):
                        i = e.matmul(pg[:, c0:c0 + 128], lhsT=W[:, k, j * 128:(j + 1) * 128], rhs=hT[:, k, :],
                                     start=(k == 0), stop=(k == KC - 1))
                return i
            P.op("pe", mm, [self.b_hT, self.b_Wg, self.b_Wu], [b_pg])
            P.v("act", "activation", [b_pg], [self.b_sg], out=self.sg[:], in_=pg[:, 0:128], func=AF.Silu)
            P.v("dve", "tensor_tensor", [self.b_sg, b_pg], [self.b_actT], out=actT[:, j, :], in0=self.sg[:],
                in1=pg[:, 128:256], op=ALU.mult)
        for n in range(2):
            pd, b_pd = self.bank()

            def mm2(e, n=n, pd=pd):
                for j in range(NJ):
                    i = e.matmul(pd[:], lhsT=actT[:, j, :], rhs=Wd[:, j, n * 512:(n + 1) * 512],
                                 start=(j == 0), stop=(j == NJ - 1))
                return i
            P.op("pe", mm2, [self.b_actT, self.b_Wd], [b_pd])
            P.v("dve", "tensor_tensor", [b_xt, b_pd], [b_xt], out=xt[:, n * 512:(n + 1) * 512],
                in0=xt[:, n * 512:(n + 1) * 512], in1=pd[:], op=ALU.add)
        if last:
            st, b_st = self.st1[ti % 2]
            self.rstd(xt[:], [b_xt], D, st, b_st, 1)
            P.v("dve", "scalar_tensor_tensor", [b_xt, b_st, self.b_gN], [b_xt], out=xt[:], in0=xt[:],
                scalar=st[:, 1:2], in1=self.gN[:], op0=ALU.mult, op1=ALU.mult)
            P.dma("sp", self.y_out[ti * 128:(ti + 1) * 128, :], xt[:], [b_xt], self.b_y)
        elif li == len(self.layers) - 1:
            P.dma("sp", self.y_out[ti * 128:(ti + 1) * 128, :], xt[:], [b_xt], self.b_y)
        else:
            P.dma("sp", self.xres[ti * 128:(ti + 1) * 128, :], xt[:], [b_xt], self.b_xres)

    def mixer_alloc(self):
        sb = self.frame_alloc
        f4 = (128, 4, 128)
        self.tb, self.b_tb = sb("tb", (128, 432))
        self.zs, self.b_zs = sb("zs", (128, 512))
        self.mkv, self.b_mkv = sb("mkv", (128, 512))
        self.mos, self.b_mos = sb("mos", (128, 256))
        self.mix, self.b_mix = sb("mix", (128, D), BF16)
        self.mixT, self.b_mixT = sb("mixT", (128, KC, 128), BF16)
        self.halo, self.b_halo = sb("halo", (128, 12, 3))
        self.ctmp, self.b_ctmp = self.junk[:, 0:1536].rearrange("p (m t) -> p m t", m=12), self.b_junk
        self.Nm, self.b_Nm = self.rn[:, 0:4, :], self.b_rn
        self.Bm, self.b_Bm = self.rn[:, 4:8, :], self.b_rn
        self.dcL, self.b_dcL = self.G4, self.b_G4
        self.u, self.b_u = self.egcB, self.b_egcB
        self.osb, self.b_osb = self.dcT, self.b_dcT
        self.sq, self.b_sq = sb("sq", (128, 8, 128), BF16)
        self.qkT, self.b_qkT = sb("qkT", f4, BF16)
        self.Tt, self.b_Tt = sb("Tt", f4, BF16)
        self.wT, self.b_wT = sb("wT", f4, BF16)
        self.vnew, self.b_vnew = sb("vnew", f4, BF16)
        self.S32, self.b_S32 = sb("S32", f4)
        self.Sbf, self.b_Sbf = sb("Sbf", f4, BF16)
        self.gt, self.b_gt = sb("gt", (128, 64))
        self.edec, self.b_edec = sb("edec", (128, 4, 2))
        self.mqT, self.b_mqT = sb("mqT", (128, 2, 128), BF16)
        self.mqk_st, self.b_mqk_st = sb("mqk_st", (128, 512), BF16)
        self.g8, self.b_g8 = sb("g8", (128, 2, 16))
        self.mkT, self.b_mkT = sb("mkT", (128, 2, 128), BF16)
        self.pk, self.b_pk = sb("pk", (4, 5, 128))
        self.fm, self.b_fm = sb("fm", (4, 2, 128))
        self.car, self.b_car = sb("car", (4, 4))
        self.bsrc, self.b_bsrc = sb("bsrc", (4, 130))
        self.bd, self.b_bd = sb("bd", (4, 4, 130))
        self.tm, self.b_tm = sb("tm", (128, 5, 4))
        self.ex, self.b_ex = sb("ex", (128, 6, 4))
        self.ET, self.b_ET = sb("ET", f4)
        self.sT, self.b_sT = sb("sT", f4, BF16)
        self.vaug, self.b_vaug = sb("vaug", (128, 4, 65), BF16)
        self.sv, self.b_sv = sb("sv", (128, 4, 65))
        self.kwk0, self.b_kwk0 = sb("kwk0", (128, 4, 64), BF16)
        self.kwk1, self.b_kwk1 = sb("kwk1", (128, 4, 64), BF16)
        self.C32, self.b_C32 = sb("C32", (128, 2, 65))
        self.Cbf, self.b_Cbf = sb("Cbf", (128, 2, 65), BF16)
        self.rsb, self.b_rsb = sb("rsb", (128, 4, 65))
        self.edm, self.b_edm = sb("edm", (128, 4, 2))
        self.hout, self.b_hout = sb("hout", (128, 4, 64))
        self.cqn, self.b_cqn = sb("cqn", (128, 256), BF16)
        self.cqT, self.b_cqT = sb("cqT", (128, 2, 128), BF16)
        self.dqT, self.b_dqT = sb("dqT", (128, 2, 128), BF16)
        self.qlatT, self.b_qlatT = sb("qlatT", f4, BF16)
        self.qidxT, self.b_qidxT = sb("qidxT", (128, 3, 128), BF16)
        self.Dw, self.b_Dw = sb("Dw", (128, 8, 128), BF16)
        self.rr = [sb("rr%d" % i, (128, 512), BF16) for i in range(2)]
        self.PT = [sb("PT%d" % i, (128, 8, 128), BF16) for i in range(1)]
        self.olatT, self.b_olatT = sb("olatT", f4, BF16)
        self.bs, self.b_bs = sb("bs", (128, 16))
        self.wk, self.b_wk = sb("wk", (128, NBIS + 1))
        self.rs4, self.b_rs4 = sb("rs4", (128, 8))
        self.wconv, _ = sb("wconv", (128, 12, 4))
        self.pb, _ = sb("pb", (128, 24))
        self.gon, _ = sb("gon", (128, 128))
        self.mon, _ = sb("mon", (128, 64))
        self.gKVb, _ = sb("gKVb", (128, 128))
        self.gQ, _ = sb("gQ", (128, 2))
        self.p4, _ = sb("p4", (4, 4))
        self.Wuq, _ = sb("Wuq", (128, 2, 256), BF16)
        self.Wqi, _ = sb("Wqi", (128, 2, 256), BF16)
        self.wuv, _ = sb("wuv", (128, 4, 64), BF16)
        self.wuk_n, _ = sb("wuk_n", (128, 4, 64))
        self.wukT, _ = sb("wukT", (128, 2, 128), BF16)
        self.Wik4, _ = sb("Wik4", (128, KC, 128), BF16)
        self.b_lp = self.P.buf("layerprm")

    def mixer_setup(self, li):
        P, prm, l = self.P, self.prm, li
        self.km = os.environ.get("KMODE", "")
        if "nosetup" in self.km:
            return
        lp = self.b_lp
        sp = lambda out, in_, **kw: P.dma("sp", out, in_, [], lp, **kw)
        for j in range(4):
            sp(self.wconv[:, :, j], prm["gdn_conv"][l, j].rearrange("(m p) -> p m", p=128), allow_slow_non_contiguous=True)
        sp(self.pb[:, 0:4], prm["gdn_a_log"][l].partition_broadcast(128))
        sp(self.pb[:, 4:8], prm["gdn_dt_bias"][l].partition_broadcast(128))
        sp(self.gon[:], prm["gdn_out_norm"][l].partition_broadcast(128))
        sp(self.mon[:], prm["mlstm_out_norm"][l].partition_broadcast(128))
        sp(self.gKVb[:], prm["dsa_kv_norm"][l].partition_broadcast(128))
        sp(self.gQ[:], prm["dsa_q_norm"][l].rearrange("(k p) -> p k", p=128), allow_slow_non_contiguous=True)
        sp(self.p4[:, 0:1], prm["mlstm_i_bias"][l].rearrange("(h o) -> h o", o=1))
        sp(self.p4[:, 1:2], prm["mlstm_f_bias"][l].rearrange("(h o) -> h o", o=1))
        sp(self.wuk_n[:], prm["dsa_w_uk"][l].rearrange("h c d -> c h d"))
        P.dma("pool", self.Wuq[:], prm["dsa_w_uq"][l].rearrange("(k p) n -> p k n", p=128), [], lp)
        P.dma("pool", self.Wqi[:], prm["dsa_w_qidx"][l].rearrange("(k p) n -> p k n", p=128), [], lp)
        P.dma("pool", self.wuv[:], prm["dsa_w_uv"][l].rearrange("h c d -> c h d"), [], lp)
        P.v("act", "activation", [lp], [lp], out=self.pb[:, 8:12], in_=self.pb[:, 0:4], func=AF.Exp)
        P.v("dve", "tensor_scalar", [lp], [lp], out=self.pb[:, 8:12], in0=self.pb[:, 8:12], scalar1=-1.0,
            scalar2=None, op0=ALU.mult)
        P.v("dve", "tensor_scalar", [lp], [lp], out=self.p4[:, 2:3], in0=self.p4[:, 1:2], scalar1=-1.0,
            scalar2=None, op0=ALU.mult)
        pt, b_pt = self.bank()
        wn, identf = self.wuk_n, self.identf

        def tr(e):
            for m in range(2):
                i = e.transpose(out=pt[:, m * 128:(m + 1) * 128],
                                in_=wn[:, 2 * m:2 * m + 2, :], identity=identf[:])
            return i
        P.op("pe", tr, [lp, self.b_c], [b_pt])
        P.v("dve", "tensor_scalar", [b_pt], [lp], out=self.wukT[:], in0=pt[:, 0:256].rearrange("p (m c) -> p m c", m=2),
            scalar1=0.125, scalar2=None, op0=ALU.mult)
        for g in range(4):
            P.v("pool", "tensor_copy", [self.b_Wi], [lp], out=self.Wik4[:, :, g * 32:(g + 1) * 32],
                in_=self.Wi[:, :, IK:IK + 32])
        P.v("dve", "memset", [], [self.b_S32], ap=self.S32[:], constant=0.0)
        P.v("dve", "memset", [], [self.b_Sbf], ap=self.Sbf[:], constant=0.0)
        P.v("dve", "memset", [], [self.b_C32], ap=self.C32[:], constant=0.0)
        P.v("dve", "memset", [], [self.b_Cbf], ap=self.Cbf[:], constant=0.0)
        P.v("dve", "memset", [], [self.b_car], ap=self.car[:], constant=0.0)
        P.v("dve", "memset", [], [self.b_halo], ap=self.halo[:], constant=0.0)
        P.v("dve", "memset", [], [self.b_vnew], ap=self.vnew[:], constant=0.0)
        P.v("dve", "memset", [], [self.b_vaug], ap=self.vaug[:], constant=1.0)
        P.v("dve", "memset", [], [self.b_kidx], ap=self.kidx, constant=0.0)

    def mixer_tile(self, li, ti):
        P = self.P
        xt, b_xt = self.load_x(li, ti)
        if "nomix" in self.km:
            P.dma("sp", self.xres[ti * 128:(ti + 1) * 128, :], xt[:], [b_xt], self.b_xres)
            return
        self.norm_T(xt, b_xt, self.gA, 0)
        self.projections()
        if "noilv" in self.km:
            self.gdn()
            self.mlstm()
        else:
            self.interleave(self.gdn, self.mlstm, [0, 1, 2, 3, 4], [5, 6, 7])
        self.dsa(ti)
        pt, b_pt = self.bank()
        ptb = pt[:].bitcast(BF16)
        mix, identb, mixT, Wo = self.mix, self.identb, self.mixT, self.Wo

        def tr(e):
            for k in range(KC):
                i = e.transpose(out=ptb[:, k * 128:(k + 1) * 128], in_=mix[:, k * 128:(k + 1) * 128],
                                identity=identb[:])
            return i
        P.op("pe", tr, [self.b_mix, self.b_c], [b_pt])
        P.v("act", "copy", [b_pt], [self.b_mixT], out=mixT[:], in_=ptb.rearrange("p (k t) -> p k t", k=KC))
        for n in range(2):
            po, b_po = self.bank()

            def mm(e, n=n, po=po):
                for k in range(KC):
                    i = e.matmul(po[:], lhsT=mixT[:, k, :], rhs=Wo[:, k, n * 512:(n + 1) * 512],
                                 start=(k == 0), stop=(k == KC - 1))
                return i
            P.op("pe", mm, [self.b_mixT, self.b_Wo], [b_po])
            P.v("dve", "tensor_tensor", [b_xt, b_po], [b_xt], out=xt[:, n * 512:(n + 1) * 512],
                in0=xt[:, n * 512:(n + 1) * 512], in1=po[:], op=ALU.add)
        P.dma("sp", self.xres[ti * 128:(ti + 1) * 128, :], xt[:], [b_xt], self.b_xres)
        if self.dbg and li == 0:
            P.v("pool", "tensor_copy", [self.b_mix], [self.b_junk], out=self.junk[:, 0:D], in_=mix[:])
            P.dma("sp", self.dbg_out[ti * 128:(ti + 1) * 128, :], self.junk[:, 0:D], [self.b_junk], self.b_dbg)

    def proj_fm(self, col0, nm, dst_fn):
        P, Wi, hT = self.P, self.Wi, self.hT
        pt, b_pt = self.bank()

        def mm(e):
            for m in range(nm):
                for k in range(KC):
                    i = e.matmul(pt[:, m * 128:(m + 1) * 128], lhsT=Wi[:, k, col0 + m * 128:col0 + (m + 1) * 128],
                                 rhs=hT[:, k, 0:128], start=(k == 0), stop=(k == KC - 1))
            return i
        P.op("pe", mm, [self.b_hT, self.b_Wi], [b_pt])
        dst_fn(pt[:, 0:nm * 128].rearrange("p (m t) -> p m t", m=nm), b_pt)

    def proj_tm(self, col0, n, dst_fn):
        P, Wi, hT = self.P, self.Wi, self.hT
        pt, b_pt = self.bank()

        def mm(e):
            for k in range(KC):
                i = e.matmul(pt[:, 0:n], lhsT=hT[:, k, 0:128], rhs=Wi[:, k, col0:col0 + n],
                             start=(k == 0), stop=(k == KC - 1))
            return i
        P.op("pe", mm, [self.b_hT, self.b_Wi], [b_pt])
        dst_fn(pt, b_pt)

    def projections(self):
        P = self.P
        cb = self.cbuf
        km = self.km
        if "p1" not in km:
            for g, c0 in enumerate((GQ, GK, GV)):
                self.proj_fm(c0, 4, lambda ps, b, g=g: P.v("act", "copy", [b], [self.b_cbuf],
                                                           out=cb[:, 4 * g:4 * g + 4, 3:131], in_=ps))

        if "p2" not in km:
            stg, b_stg = self.mqk_st, self.b_mqk_st
            self.proj_tm(MQ, 512, lambda pt, b: P.v("act", "copy", [b], [b_stg], out=stg[:], in_=pt[:]))
            pq_, b_pq = self.bank()
            pqb = pq_[:].bitcast(BF16)
            identb = self.identb

            def trq(e):
                for m in range(4):
                    i = e.transpose(out=pqb[:, m * 128:(m + 1) * 128], in_=stg[:, m * 128:(m + 1) * 128], identity=identb[:])
                return i
            P.op("pe", trq, [b_stg, self.b_c], [b_pq])
            pq4 = pqb[:, 0:512].rearrange("p (m t) -> p m t", m=4)
            if "p2b" not in km:
                P.v("act", "copy", [b_pq], [self.b_mqT], out=self.mqT[:], in_=pq4[:, 0:2, :])
            if "p2a" not in km:
                P.v("dve", "tensor_scalar", [b_pq], [self.b_mkT], out=self.mkT[:], in0=pq4[:, 2:4, :], scalar1=0.125,
                    scalar2=None, op0=ALU.mult)
        if "p3" not in km:
            self.proj_tm(GZ, 512, lambda pt, b: P.v("act", "activation", [b], [self.b_zs], out=self.zs[:],
                                                    in_=pt[:], func=AF.Silu))
        if "p4" not in km:
            self.proj_tm(GA, 432, lambda pt, b: P.v("act", "copy", [b], [self.b_tb], out=self.tb[:], in_=pt[:, 0:432]))
        if "p5" not in km:
            self.proj_tm(MK, 512, lambda pt, b: P.v("dve", "tensor_copy", [b], [self.b_mkv], out=self.mkv[:], in_=pt[:]))
        if "p6" not in km:
            def mo_(pt, b):
                P.v("act", "activation", [b], [self.b_mos], out=self.mos[:], in_=pt[:, 0:256], func=AF.Sigmoid)
                P.v("dve", "tensor_copy", [b], [self.b_g8], out=self.g8[:, :, 0:4],
                    in_=pt[:, 256:264].rearrange("p (q h) -> p q h", q=2))
            self.proj_tm(MO, 264, mo_)

    def gdn(self):
        P = self.P
        c = [self.b_c]
        lp = self.b_lp
        gt, b_gt, tb = self.gt, self.b_gt, self.tb
        f4 = (128, 4, 128)
        P.v("dve", "tensor_tensor", [self.b_tb, lp], [b_gt], out=gt[:, 40:44], in0=tb[:, 0:4], in1=self.pb[:, 4:8], op=ALU.add)
        P.v("act", "activation", [b_gt], [b_gt], out=gt[:, 40:44], in_=gt[:, 40:44], func=AF.Exp)
        P.v("act", "activation", [b_gt], [b_gt], out=gt[:, 40:44], in_=gt[:, 40:44], func=AF.Ln, bias=1.0)
        P.v("dve", "tensor_tensor", [b_gt, lp], [b_gt], out=gt[:, 0:4], in0=gt[:, 40:44], in1=self.pb[:, 8:12], op=ALU.mult)
        P.v("act", "activation", [self.b_tb], [b_gt], out=gt[:, 4:8], in_=tb[:, 4:8], func=AF.Sigmoid)
        P.v("dve", "tensor_scalar", [b_gt], [b_gt], out=gt[:, 8:12], in0=gt[:, 4:8], scalar1=-1.0, scalar2=None, op0=ALU.mult)
        pg, b_pg = self.bank()
        utri, bones = self.utriC, self.bonesC

        def mm(e):
            e.matmul(pg[:, 0:4], lhsT=utri[:], rhs=gt[:, 0:4], start=True, stop=True)
            return e.matmul(pg[:, 4:8], lhsT=bones[:], rhs=gt[:, 0:4], start=True, stop=True)
        P.op("pe", mm, [b_gt] + c, [b_pg])
        P.v("dve", "tensor_copy", [b_pg], [b_gt], out=gt[:, 12:20], in_=pg[:, 0:8])
        P.v("act", "activation", [b_gt], [b_gt], out=gt[:, 20:24], in_=gt[:, 12:16], func=AF.Exp)
        P.v("dve", "tensor_tensor", [b_gt], [b_gt], out=gt[:, 40:44], in0=gt[:, 16:20], in1=gt[:, 12:16], op=ALU.subtract)
        P.v("act", "activation", [b_gt], [b_gt], out=gt[:, 24:28], in_=gt[:, 40:44], func=AF.Exp)
        P.v("dve", "tensor_tensor", [b_gt], [b_gt], out=gt[:, 28:32], in0=gt[:, 4:8], in1=gt[:, 20:24], op=ALU.mult)
        P.v("dve", "tensor_scalar", [b_gt] + c, [b_gt], out=gt[:, 32:36], in0=gt[:, 24:28], scalar1=self.rowm[:, 0:1], scalar2=None, op0=ALU.mult)
        P.v("dve", "tensor_scalar", [b_gt] + c, [b_gt], out=gt[:, 36:40], in0=gt[:, 24:28], scalar1=self.rowm[:, 1:2], scalar2=None, op0=ALU.mult)
        P.v("dve", "tensor_tensor", [b_gt] + c, [self.b_G4], out=self.G4[:], in0=bc(self.utriC[:].unsqueeze(1), f4),
            in1=bc(gt[:, 0:4].unsqueeze(2), f4), op=ALU.mult)
        pB, b_pB = self.bank()
        G4, onesf = self.G4, self.onesf
        P.op("pe", lambda e: e.matmul(pB[:], lhsT=onesf[:], rhs=G4[:].rearrange("p h i -> p (h i)"), start=True, stop=True),
             [self.b_G4] + c, [b_pB])
        pB4 = pB[:].rearrange("p (h i) -> p h i", h=4)
        P.v("act", "activation", [b_pB], [self.b_egcB], out=self.egcB[:], in_=pB4, func=AF.Exp)
        P.v("act", "activation", [b_pB], [self.b_edec], out=self.edec[:],
            in_=pB[:].rearrange("p (h c i) -> p h c i", h=4, c=2)[:, :, :, 63], func=AF.Exp)
        for h in range(4):
            P.v("dve", "scalar_tensor_tensor", [b_pB, b_gt] + c, [self.b_dcT], out=self.dcT[:, h, :], in0=pB4[:, h, :],
                scalar=gt[:, 12 + h:13 + h], in1=self.mnegT[:], op0=ALU.subtract, op1=ALU.add)
            P.v("dve", "scalar_tensor_tensor", [b_pB, b_gt] + c, [self.b_dcL], out=self.dcL[:, h, :], in0=pB4[:, h, :],
                scalar=gt[:, 12 + h:13 + h], in1=self.mposL[:], op0=ALU.subtract, op1=ALU.add)
        P.v("act", "activation", [self.b_dcT], [self.b_dcT], out=self.dcT[:], in_=self.dcT[:], func=AF.Exp)
        P.v("act", "activation", [self.b_dcL], [self.b_dcL], out=self.dcL[:], in_=self.dcL[:], func=AF.Exp, scale=-1.0)
        cb, ca, ct, wc = self.cbuf, self.cact, self.ctmp, self.wconv
        s12 = (128, 12, 128)
        P.v("pool", "tensor_copy", [self.b_halo], [self.b_cbuf], out=cb[:, :, 0:3], in_=self.halo[:])
        P.v("pool", "tensor_tensor", [self.b_cbuf, lp], [self.b_cact], out=ca[:], in0=cb[:, :, 0:128],
            in1=bc(wc[:, :, 0:1], s12), op=ALU.mult)
        for j in range(1, 4):
            P.v("pool", "tensor_tensor", [self.b_cbuf, lp], [self.b_ctmp], out=ct[:], in0=cb[:, :, j:j + 128],
                in1=bc(wc[:, :, j:j + 1], s12), op=ALU.mult)
            P.v("pool", "tensor_tensor", [self.b_cact, self.b_ctmp], [self.b_cact], out=ca[:], in0=ca[:], in1=ct[:], op=ALU.add)
        P.v("pool", "tensor_copy", [self.b_cbuf], [self.b_halo], out=self.halo[:], in_=cb[:, :, 128:131])
        P.v("act", "activation", [self.b_cact], [self.b_cact], out=ca[:], in_=ca[:], func=AF.Silu)
        P.v("pool", "tensor_tensor", [self.b_cact], [self.b_sq], out=self.sq[:], in0=ca[:, 0:8, :], in1=ca[:, 0:8, :], op=ALU.mult)
        sq, o128, ob = self.sq, self.ones128b, self.onesb
        for half, lh in ((0, o128), (1, ob)):
            pn, b_pn = self.bank()
            P.op("pe", lambda e, half=half, lh=lh, pn=pn: e.matmul(
                pn[:], lhsT=lh[:], rhs=sq[:, 4 * half:4 * half + 4, :].rearrange("p h i -> p (h i)"), start=True, stop=True),
                [self.b_sq] + c, [b_pn])
            rnv = self.rn[:, 4 * half:4 * half + 4, :].rearrange("p h i -> p (h i)")
            P.v("act", "activation", [b_pn] + c, [self.b_rn], out=rnv, in_=pn[:], func=AF.Ln,
                bias=self.epsc[:, 1 - half:2 - half])
            P.v("act", "activation", [self.b_rn], [self.b_rn], out=rnv, in_=rnv, func=AF.Exp, scale=-0.5)
        P.v("dve", "tensor_tensor", [self.b_cact, self.b_rn], [self.b_qT], out=self.qT[:], in0=ca[:, 0:4, :], in1=self.rn[:, 0:4, :], op=ALU.mult)
        P.v("pool", "tensor_tensor", [self.b_cact, self.b_rn], [self.b_kT], out=self.kT[:], in0=ca[:, 4:8, :], in1=self.rn[:, 4:8, :], op=ALU.mult)
        P.v("dve", "tensor_tensor", [self.b_qT, self.b_egcB], [self.b_qdT], out=self.qdT[:], in0=self.qT[:], in1=self.egcB[:], op=ALU.mult)
        pv, b_pv = self.bank()
        identf, identb, kT, qT = self.identf, self.identb, self.kT, self.qT

        def trv(e):
            for h in range(4):
                i = e.transpose(out=pv[:, h * 128:(h + 1) * 128], in_=ca[:, 8 + h, :], identity=identf[:])
            return i
        P.op("pe", trv, [self.b_cact] + c, [b_pv])
        P.v("dve", "tensor_tensor", [b_pv, b_gt], [self.b_vb], out=self.vb[:], in0=pv[:].rearrange("p (h d) -> p h d", h=4),
            in1=bc(gt[:, 4:8].unsqueeze(2), f4), op=ALU.mult)
        pk_, b_pk = self.bank()
        pkb = pk_[:].bitcast(BF16)

        def trk(e):
            for h in range(4):
                i = e.transpose(out=pkb[:, h * 128:(h + 1) * 128], in_=kT[:, h, :], identity=identb[:])
            return i
        P.op("pe", trk, [self.b_kT] + c, [b_pk])
        pk4 = pkb[:, 0:512].rearrange("p (h d) -> p h d", h=4)
        P.v("dve", "tensor_tensor", [b_pk, b_gt], [self.b_kbg], out=self.kbg[:], in0=pk4, in1=bc(gt[:, 28:32].unsqueeze(2), f4), op=ALU.mult)
        P.v("dve", "tensor_tensor", [b_pk, b_gt], [self.b_kd0], out=self.kd0[:], in0=pk4, in1=bc(gt[:, 32:36].unsqueeze(2), f4), op=ALU.mult)
        P.v("dve", "tensor_tensor", [b_pk, b_gt], [self.b_kd1], out=self.kd1[:], in0=pk4, in1=bc(gt[:, 36:40].unsqueeze(2), f4), op=ALU.mult)
        pkk, b_pkk = self.bank()
        pkq, b_pkq = self.bank()

        def mmk(e):
            for h in range(4):
                e.matmul(pkk[:, h * 128:(h + 1) * 128], lhsT=kT[:, h, :], rhs=kT[:, h, :], start=True, stop=True)
            for h in range(4):
                i = e.matmul(pkq[:, h * 128:(h + 1) * 128], lhsT=kT[:, h, :], rhs=qT[:, h, :], start=True, stop=True)
            return i
        P.op("pe", mmk, [self.b_kT, self.b_qT], [b_pkk, b_pkq])
        Nm, Bm, Qm = self.Nm, self.Bm, self.Qm
        for h in range(4):
            P.v("dve", "scalar_tensor_tensor", [b_pkk, b_gt, self.b_dcL], [self.b_Nm], out=Nm[:, h, :],
                in0=pkk[:, h * 128:(h + 1) * 128], scalar=gt[:, 8 + h:9 + h], in1=self.dcL[:, h, :], op0=ALU.mult, op1=ALU.mult)
        P.v("dve", "tensor_tensor", [b_pkq, self.b_dcT], [self.b_qkT], out=self.qkT[:], in0=pkq[:].rearrange("p (h i) -> p h i", h=4),
            in1=self.dcT[:], op=ALU.mult)
        pb0, b_pb0 = self.bank()

        def trn(e):
            for h in range(4):
                i = e.transpose(out=pb0[:, h * 128:(h + 1) * 128], in_=Nm[:, h, :], identity=identf[:])
            return i
        P.op("pe", trn, [self.b_Nm] + c, [b_pb0])
        pb04 = pb0[:].rearrange("p (h i) -> p h i", h=4)
        P.v("act", "copy", [b_pb0], [self.b_Bm], out=Bm[:], in_=pb04)
        P.v("dve", "tensor_tensor", [b_pb0] + c, [self.b_Qm], out=Qm[:], in0=pb04, in1=bc(identf[:].unsqueeze(1), f4), op=ALU.add)
        for lev in range(6):
            sqr = lev < 5
            upd = lev >= 1
            pN, b_pN = self.bank() if sqr else (None, None)
            pBb, b_pBb = self.bank() if sqr else (None, None)
            pQ, b_pQ = self.bank() if upd else (None, None)

            def mml(e, sqr=sqr, upd=upd, pN=pN, pBb=pBb, pQ=pQ):
                i = None
                for h in range(4):
                    sl = slice(h * 128, (h + 1) * 128)
                    if sqr:
                        e.matmul(pN[:, sl], lhsT=Bm[:, h, :], rhs=Nm[:, h, :], start=True, stop=True)
                        i = e.matmul(pBb[:, sl], lhsT=Nm[:, h, :], rhs=Bm[:, h, :], start=True, stop=True)
                    if upd:
                        i = e.matmul(pQ[:, sl], lhsT=Nm[:, h, :], rhs=Qm[:, h, :], start=True, stop=True)
                return i
            wr = [b for b in (b_pN, b_pBb, b_pQ) if b is not None]
            P.op("pe", mml, [self.b_Nm, self.b_Bm, self.b_Qm], wr)
            v4 = lambda t: t[:].rearrange("p (h i) -> p h i", h=4)
            if upd:
                if lev < 5:
                    P.v("dve", "tensor_tensor", [b_pQ, self.b_Qm], [self.b_Qm], out=Qm[:], in0=v4(pQ), in1=Qm[:], op=ALU.add)
                else:
                    P.v("dve", "tensor_tensor", [b_pQ, self.b_Qm], [self.b_Tt], out=self.Tt[:], in0=v4(pQ), in1=Qm[:], op=ALU.add)
            if sqr:
                P.v("act", "copy", [b_pN], [self.b_Nm], out=Nm[:], in_=v4(pN))
                P.v("dve", "tensor_copy", [b_pBb], [self.b_Bm], out=Bm[:], in_=v4(pBb))
        pu, b_pu = self.bank()
        pw, b_pw = self.bank()
        Tt, vb, kbg = self.Tt, self.vb, self.kbg

        def mmu(e):
            for h in range(4):
                e.matmul(pu[:, h * 128:(h + 1) * 128], lhsT=Tt[:, h, :], rhs=vb[:, h, :], start=True, stop=True)
            for h in range(4):
                i = e.matmul(pw[:, h * 128:(h + 1) * 128], lhsT=kbg[:, h, :], rhs=Tt[:, h, :], start=True, stop=True)
            return i
        P.op("pe", mmu, [self.b_Tt, self.b_vb, self.b_kbg], [b_pu, b_pw])
        P.v("act", "copy", [b_pu], [self.b_u], out=self.u[:], in_=pu[:].rearrange("p (h i) -> p h i", h=4))
        P.v("dve", "tensor_copy", [b_pw], [self.b_wT], out=self.wT[:], in_=pw[:].rearrange("p (h i) -> p h i", h=4))
        wT, Sbf, qdT, qkT, vnew, S32 = self.wT, self.Sbf, self.qdT, self.qkT, self.vnew, self.S32
        for cc in range(2):
            kd = self.kd0 if cc == 0 else self.kd1
            b_kd = self.b_kd0 if cc == 0 else self.b_kd1
            pV, b_pV = self.bank()

            def mmv(e, pV=pV):
                for h in range(4):
                    i = e.matmul(pV[:, h * 128:(h + 1) * 128], lhsT=wT[:, h, :], rhs=Sbf[:, h, :], start=True, stop=True)
                return i
            P.op("pe", mmv, [self.b_wT, self.b_Sbf], [b_pV])
            P.v("dve", "tensor_tensor", [self.b_u, b_pV], [self.b_vnew], out=vnew[:], in0=self.u[:],
                in1=pV[:].rearrange("p (h i) -> p h i", h=4), op=ALU.subtract)
            pO, b_pO = self.bank()
            pS, b_pS = self.bank()

            def mmo(e, pO=pO, pS=pS, kd=kd):
                for h in range(4):
                    sl = slice(h * 128, (h + 1) * 128)
                    e.matmul(pO[:, sl], lhsT=qdT[:, h, :], rhs=Sbf[:, h, :], start=True, stop=False)
                    e.matmul(pO[:, sl], lhsT=qkT[:, h, :], rhs=vnew[:, h, :], start=False, stop=True)
                for h in range(4):
                    i = e.matmul(pS[:, h * 128:(h + 1) * 128], lhsT=kd[:, h, :], rhs=vnew[:, h, :], start=True, stop=True)
                return i
            P.op("pe", mmo, [self.b_qdT, self.b_Sbf, self.b_qkT, self.b_vnew, b_kd], [b_pO, b_pS])
            rows = slice(cc * 64, (cc + 1) * 64)
            P.v("act", "copy", [b_pO], [self.b_osb], out=self.osb[rows], in_=pO[rows].rearrange("p (h i) -> p h i", h=4))
            P.v("dve", "tensor_tensor", [self.b_S32, self.b_edec], [self.b_S32], out=S32[:], in0=S32[:],
                in1=bc(self.edec[:, :, cc:cc + 1], f4), op=ALU.mult)
            P.v("dve", "tensor_tensor", [self.b_S32, b_pS], [self.b_S32], out=S32[:], in0=S32[:],
                in1=pS[:].rearrange("p (h i) -> p h i", h=4), op=ALU.add)
            P.v("act", "copy", [self.b_S32], [self.b_Sbf], out=Sbf[:], in_=S32[:])
        self.out_norm(self.osb, self.b_osb, 128, self.gon, self.zs, self.b_zs, 0)

    def out_norm(self, src, b_src, dv, gain, gate, b_gate, col0):
        P = self.P
        sh = (128, 4, dv)
        jv = self.junk[:, 0:4 * dv].rearrange("p (h d) -> p h d", h=4)
        st, b_st = self.st1[0]
        P.v("pool", "tensor_tensor", [b_src], [self.b_junk], out=jv, in0=src[:], in1=src[:], op=ALU.mult)
        P.v("dve", "tensor_reduce", [self.b_junk], [b_st], out=st[:, 0:4], in_=jv, axis=AX.X, op=ALU.add)
        P.v("act", "activation", [b_st, self.b_c], [b_st], out=st[:, 0:4], in_=st[:, 0:4], func=AF.Ln, scale=1.0 / dv,
            bias=self.epsc[:, 0:1])
        P.v("act", "activation", [b_st], [b_st], out=st[:, 0:4], in_=st[:, 0:4], func=AF.Exp, scale=-0.5)
        P.v("dve", "tensor_tensor", [b_src, b_st], [self.b_junk], out=jv, in0=src[:], in1=bc(st[:, 0:4].unsqueeze(2), sh), op=ALU.mult)
        P.v("pool", "tensor_tensor", [self.b_junk, self.b_lp], [self.b_junk], out=jv, in0=jv, in1=bc(gain[:].unsqueeze(1), sh), op=ALU.mult)
        P.v("dve", "tensor_tensor", [self.b_junk, b_gate], [self.b_mix], out=self.mix[:, col0:col0 + 4 * dv],
            in0=self.junk[:, 0:4 * dv], in1=gate[:, 0:4 * dv], op=ALU.mult)

    def mlstm(self):
        P = self.P
        c = [self.b_c]
        lp = self.b_lp
        f4 = (128, 4, 128)
        Wi, hT = self.Wi, self.hT
        pk, b_pk, fm, b_fm, car, b_car = self.pk, self.b_pk, self.fm, self.b_fm, self.car, self.b_car
        pg, b_pg = self.bank()

        g8, identf_ = self.g8, self.identf

        def mmg(e):
            for q in range(2):
                i = e.transpose(out=pg[0:4, q * 128:(q + 1) * 128], in_=g8[:, q, 0:4], identity=identf_[:])
            return i
        P.op("pe", mmg, [self.b_g8, self.b_c], [b_pg])
        P.v("act", "activation", [b_pg, lp], [b_fm], out=fm[:, 0, :], in_=pg[0:4, 128:256], func=AF.Exp, scale=-1.0,
            bias=self.p4[:, 2:3])
        P.v("act", "activation", [b_fm], [b_fm], out=fm[:, 0, :], in_=fm[:, 0, :], func=AF.Ln, bias=1.0)
        P.v("dve", "tensor_scalar", [b_pg, lp], [b_fm], out=fm[:, 1, :], in0=pg[0:4, 0:128], scalar1=self.p4[:, 0:1],
            scalar2=None, op0=ALU.add)
        P.v("dve", "tensor_tensor_scan", [b_fm, b_car] + c, [b_pk], out=pk[:, 2, :], data0=self.onesf[0:4, :],
            data1=fm[:, 0, :], initial=car[:, 0:1], op0=ALU.mult, op1=ALU.subtract)
        P.v("dve", "tensor_tensor", [b_fm, b_pk], [b_pk], out=pk[:, 0, :], in0=fm[:, 1, :], in1=pk[:, 2, :], op=ALU.subtract)
        P.v("dve", "tensor_tensor_scan", [b_pk, b_car], [b_pk], out=pk[:, 1, :], data0=pk[:, 0, :],
            data1=pk[:, 0, :], initial=car[:, 1:2], op0=ALU.max, op1=ALU.max)
        P.v("dve", "tensor_copy", [b_pk], [b_pk], out=pk[:, 3, :].rearrange("p (c i) -> p c i", c=2),
            in_=bc(pk[:, 1, :].rearrange("p (c i) -> p c i", c=2)[:, :, 63:64], (4, 2, 64)))
        P.v("dve", "tensor_copy", [b_car], [b_pk], out=pk[:, 4, 0:64], in_=bc(car[:, 1:2], (4, 64)))
        P.v("dve", "tensor_copy", [b_pk], [b_pk], out=pk[:, 4, 64:128], in_=bc(pk[:, 1, 63:64], (4, 64)))
        bsrc, b_bsrc, bd, b_bd = self.bsrc, self.b_bsrc, self.bd, self.b_bd
        P.v("dve", "tensor_scalar", [b_pk], [b_bsrc], out=bsrc[:, 0:128], in0=pk[:, 1, :], scalar1=-1.0, scalar2=None, op0=ALU.mult)
        P.v("dve", "tensor_tensor", [b_car, b_pk], [b_bsrc], out=bsrc[:, 128:129], in0=car[:, 1:2], in1=pk[:, 1, 63:64], op=ALU.subtract)
        P.v("dve", "tensor_tensor", [b_pk], [b_bsrc], out=bsrc[:, 129:130], in0=pk[:, 1, 63:64], in1=pk[:, 1, 127:128], op=ALU.subtract)
        P.v("dve", "tensor_tensor", [b_bsrc] + c, [b_bd], out=bd[:], in0=bc(bsrc[:].unsqueeze(1), (4, 4, 130)),
            in1=bc(self.identf[0:4, 0:4].unsqueeze(2), (4, 4, 130)), op=ALU.mult)
        P.v("dve", "tensor_copy", [b_pk], [b_car], out=car[:, 0:1], in_=pk[:, 2, 127:128])
        P.v("dve", "tensor_copy", [b_pk], [b_car], out=car[:, 1:2], in_=pk[:, 1, 127:128])
        ptm, b_ptm = self.bank()
        identf, onesf = self.identf, self.onesf

        def trt(e):
            for q in range(5):
                i = e.transpose(out=ptm[:, q * 4:(q + 1) * 4], in_=pk[:, q, :], identity=identf[0:4, 0:4])
            return i
        P.op("pe", trt, [b_pk] + c, [b_ptm])
        tm, b_tm, ex, b_ex = self.tm, self.b_tm, self.ex, self.b_ex
        P.v("dve", "tensor_copy", [b_ptm], [b_tm], out=tm[:], in_=ptm[:, 0:20].rearrange("p (q h) -> p q h", q=5))
        P.v("dve", "tensor_tensor", [b_tm], [b_ex], out=ex[:, 0, :], in0=tm[:, 0, :], in1=tm[:, 3, :], op=ALU.subtract)
        P.v("dve", "tensor_tensor", [b_tm], [b_ex], out=ex[:, 1, :], in0=tm[:, 4, :], in1=tm[:, 1, :], op=ALU.subtract)
        P.v("dve", "scalar_tensor_tensor", [b_tm], [b_ex], out=ex[:, 2, :], in0=tm[:, 2, :], scalar=-1.0, in1=tm[:, 1, :],
            op0=ALU.mult, op1=ALU.subtract)
        P.v("act", "activation", [b_ex], [b_ex], out=ex[:, 0:3, :], in_=ex[:, 0:3, :], func=AF.Exp)
        for cc in range(2):
            P.v("dve", "tensor_scalar", [b_ex] + c, [b_ex], out=ex[:, 3 + cc, :], in0=ex[:, 0, :], scalar1=self.rowm[:, cc:cc + 1],
                scalar2=0.125, op0=ALU.mult, op1=ALU.mult)
        pM, b_pM = self.bank()
        pD, b_pD = self.bank()

        def mmb(e):
            for h in range(4):
                e.matmul(pM[:, h * 128:(h + 1) * 128], lhsT=onesf[0:4, :], rhs=bd[:, h, 0:128], start=True, stop=True)
            return e.matmul(pD[:, 0:8], lhsT=onesf[0:4, :], rhs=bd[:, :, 128:130], start=True, stop=True)
        P.op("pe", mmb, [b_bd] + c, [b_pM, b_pD])
        P.v("act", "activation", [b_pD], [self.b_edm], out=self.edm[:], in_=pD[:, 0:8].rearrange("p (h c) -> p h c", h=4), func=AF.Exp)
        ET, b_ET = self.ET, self.b_ET
        for h in range(4):
            P.v("dve", "scalar_tensor_tensor", [b_pM, b_tm] + c, [b_ET], out=ET[:, h, :], in0=pM[:, h * 128:(h + 1) * 128],
                scalar=tm[:, 0, h:h + 1], in1=self.mnegT[:], op0=ALU.add, op1=ALU.add)
        P.v("act", "activation", [b_ET], [b_ET], out=ET[:], in_=ET[:], func=AF.Exp)
        mkT, mqT = self.mkT, self.mqT
        pqs = [self.bank(), self.bank()]

        def mmq(e):
            for h in range(4):
                r = slice((h % 2) * 64, (h % 2) * 64 + 64)
                i = e.matmul(pqs[h % 2][0][:, (h // 2) * 128:(h // 2 + 1) * 128], lhsT=mkT[r, h // 2, :], rhs=mqT[r, h // 2, :],
                             start=True, stop=True)
            return i
        P.op("pe", mmq, [self.b_mkT, self.b_mqT], [pqs[0][1], pqs[1][1]])
        for h in range(4):
            P.v("dve", "tensor_tensor", [pqs[h % 2][1], b_ET], [self.b_sT], out=self.sT[:, h, :],
                in0=pqs[h % 2][0][:, (h // 2) * 128:(h // 2 + 1) * 128], in1=ET[:, h, :], op=ALU.mult)
        mk4 = self.mkv[:, 0:256].rearrange("p (h d) -> p h d", h=4)
        mv4 = self.mkv[:, 256:512].rearrange("p (h d) -> p h d", h=4)
        s64 = (128, 4, 64)
        P.v("pool", "tensor_copy", [self.b_mkv], [self.b_vaug], out=self.vaug[:, :, 0:64], in_=mv4)
        P.v("dve", "tensor_tensor", [self.b_mkv, b_ex], [self.b_kwk0], out=self.kwk0[:], in0=mk4, in1=bc(ex[:, 3, :].unsqueeze(2), s64), op=ALU.mult)
        P.v("dve", "tensor_tensor", [self.b_mkv, b_ex], [self.b_kwk1], out=self.kwk1[:], in0=mk4, in1=bc(ex[:, 4, :].unsqueeze(2), s64), op=ALU.mult)
        psv, b_psv = self.bank()
        sT, vaug = self.sT, self.vaug

        def mms(e):
            for h in range(4):
                i = e.matmul(psv[:, h * 65:(h + 1) * 65], lhsT=sT[:, h, :], rhs=vaug[:, h, :], start=True, stop=True)
            return i
        P.op("pe", mms, [self.b_sT, self.b_vaug], [b_psv])
        P.v("act", "copy", [b_psv], [self.b_sv], out=self.sv[:], in_=psv[:, 0:260].rearrange("p (h d) -> p h d", h=4))
        C32, Cbf, rsb = self.C32, self.Cbf, self.rsb
        for cc in range(2):
            kwk = self.kwk0 if cc == 0 else self.kwk1
            b_kwk = self.b_kwk0 if cc == 0 else self.b_kwk1
            pAs = [self.bank(), self.bank()]

            def mma(e, pAs=pAs):
                for h in range(4):
                    r = slice((h % 2) * 64, (h % 2) * 64 + 64)
                    i = e.matmul(pAs[h % 2][0][:, (h // 2) * 65:(h // 2 + 1) * 65], lhsT=mqT[r, h // 2, :], rhs=Cbf[r, h // 2, :],
                                 start=True, stop=True)
                return i
            P.op("pe", mma, [self.b_mqT, self.b_Cbf], [pAs[0][1], pAs[1][1]])
            rows = slice(cc * 64, (cc + 1) * 64)
            for h in range(4):
                P.v("dve", "scalar_tensor_tensor", [pAs[h % 2][1], b_ex, self.b_sv], [self.b_rsb], out=rsb[rows, h, :],
                    in0=pAs[h % 2][0][rows, (h // 2) * 65:(h // 2 + 1) * 65], scalar=ex[rows, 1, h:h + 1], in1=self.sv[rows, h, :],
                    op0=ALU.mult, op1=ALU.add)
            pC, b_pC = self.bank()

            def mmc(e, pC=pC, kwk=kwk):
                for m in range(2):
                    i = e.matmul(pC[:, m * 130:(m + 1) * 130], lhsT=kwk[:, 2 * m:2 * m + 2, :],
                                 rhs=vaug[:, 2 * m:2 * m + 2, :], start=True, stop=True)
                return i
            P.op("pe", mmc, [b_kwk, self.b_vaug], [b_pC])
            for m in range(2):
                for hh in range(2):
                    r = slice(hh * 64, hh * 64 + 64)
                    h = 2 * m + hh
                    P.v("dve", "scalar_tensor_tensor", [self.b_C32, self.b_edm, b_pC], [self.b_C32], out=C32[r, m, :], in0=C32[r, m, :],
                        scalar=self.edm[r, h, cc:cc + 1], in1=pC[r, m * 130 + hh * 65:m * 130 + hh * 65 + 65], op0=ALU.mult, op1=ALU.add)
            P.v("act", "copy", [self.b_C32], [self.b_Cbf], out=Cbf[:], in_=C32[:])
        st, b_st = self.st1[1]
        P.v("dve", "tensor_scalar", [self.b_rsb], [b_ex], out=ex[:, 5, :], in0=rsb[:, :, 64], scalar1=-1.0, scalar2=None, op0=ALU.mult)
        P.v("dve", "tensor_tensor", [self.b_rsb, b_ex], [b_ex], out=ex[:, 5, :], in0=rsb[:, :, 64], in1=ex[:, 5, :], op=ALU.max)
        P.v("dve", "tensor_tensor", [b_ex], [b_st], out=st[:, 0:4], in0=ex[:, 5, :], in1=ex[:, 2, :], op=ALU.max)
        P.v("dve", "reciprocal", [b_st], [b_st], out=st[:, 0:4], in_=st[:, 0:4])
        P.v("dve", "tensor_tensor", [self.b_rsb, b_st], [self.b_hout], out=self.hout[:], in0=rsb[:, :, 0:64],
            in1=bc(st[:, 0:4].unsqueeze(2), s64), op=ALU.mult)
        self.out_norm(self.hout, self.b_hout, 64, self.mon, self.mos, self.b_mos, 768)

    def dsa(self, ti):
        P = self.P
        c = [self.b_c]
        lp = self.b_lp
        tb, b_tb = self.tb, self.b_tb
        qt = ti
        S = 128 * (qt + 1)
        T0 = qt * 128
        identb, identf = self.identb, self.identf
        st, b_st = self.st1[0]
        self.rstd(tb[:, 8:264], [b_tb], 256, st, b_st, 2)
        P.v("act", "activation", [b_tb, b_st], [self.b_cqn], out=self.cqn[:], in_=tb[:, 8:264], func=AF.Copy, scale=st[:, 2:3])
        p1, b_p1 = self.bank()
        p1b = p1[:].bitcast(BF16)
        cqn = self.cqn

        def tr1(e):
            for k in range(2):
                i = e.transpose(out=p1b[:, k * 128:(k + 1) * 128], in_=cqn[:, k * 128:(k + 1) * 128], identity=identb[:])
            return i
        P.op("pe", tr1, [self.b_cqn] + c, [b_p1])
        P.v("dve", "tensor_tensor", [b_p1, lp], [self.b_cqT], out=self.cqT[:], in0=p1b[:, 0:256].rearrange("p (k t) -> p k t", k=2),
            in1=bc(self.gQ[:].unsqueeze(2), (128, 2, 128)), op=ALU.mult)
        self.rstd(tb[:, 264:392], [b_tb], 128, st, b_st, 3)
        P.v("dve", "scalar_tensor_tensor", [b_tb, b_st, lp], [self.b_ckvK], out=self.ckvK[:, qt, :], in0=tb[:, 264:392],
            scalar=st[:, 3:4], in1=self.gKVb[:], op0=ALU.mult, op1=ALU.mult)
        p2, b_p2 = self.bank()
        p2b = p2[:].bitcast(BF16)
        ckvK = self.ckvK
        P.op("pe", lambda e: e.transpose(out=p2b[:, 0:128], in_=ckvK[:, qt, :], identity=identb[:]), [self.b_ckvK] + c, [b_p2])
        P.v("act", "copy", [b_p2], [self.b_ckvT], out=self.ckvT[:, T0:T0 + 128], in_=p2b[:, 0:128])
        p3, b_p3 = self.bank()
        Wik4, hT = self.Wik4, self.hT

        def mmk(e):
            for k in range(KC):
                i = e.matmul(p3[:, 0:128], lhsT=Wik4[:, k, :], rhs=hT[:, k, 0:128], start=(k == 0), stop=(k == KC - 1))
            return i
        P.op("pe", mmk, [self.b_hT, lp], [b_p3])
        P.v("act", "copy", [b_p3], [self.b_kidx], out=self.kidx[:, T0:T0 + 128], in_=p3[:, 0:128])
        p4_, b_p4 = self.bank()
        Wuq, Wqi, cqT = self.Wuq, self.Wqi, self.cqT

        def mmq(e):
            for m in range(2):
                for k in range(2):
                    i = e.matmul(p4_[:, m * 128:(m + 1) * 128], lhsT=Wuq[:, k, m * 128:(m + 1) * 128],
                                 rhs=cqT[:, k, :], start=(k == 0), stop=(k == 1))
            return i
        P.op("pe", mmq, [self.b_cqT, lp], [b_p4])
        P.v("act", "copy", [b_p4], [self.b_dqT], out=self.dqT[:], in_=p4_[:, 0:256].rearrange("p (m t) -> p m t", m=2))
        p6, b_p6 = self.bank()

        def mmqi(e):
            for m in range(3):
                w = 96 if m < 2 else 64
                for k in range(2):
                    i = e.matmul(p6[0:w, m * 128:(m + 1) * 128], lhsT=Wqi[:, k, m * 96:m * 96 + w],
                                 rhs=cqT[:, k, :], start=(k == 0), stop=(k == 1))
            return i
        P.op("pe", mmqi, [self.b_cqT, lp], [b_p6])
        P.v("dve", "tensor_copy", [b_p6], [self.b_qidxT], out=self.qidxT[0:96, 0:2, :], in_=p6[0:96, 0:256].rearrange("p (m t) -> p m t", m=2))
        P.v("dve", "tensor_copy", [b_p6], [self.b_qidxT], out=self.qidxT[0:64, 2, :], in_=p6[0:64, 256:384])
        wukT, dqT = self.wukT, self.dqT
        p5s = [self.bank(), self.bank()]

        def mml(e):
            for h in range(4):
                r = slice((h % 2) * 64, (h % 2) * 64 + 64)
                i = e.matmul(p5s[h % 2][0][:, (h // 2) * 128:(h // 2 + 1) * 128], lhsT=wukT[r, h // 2, :], rhs=dqT[r, h // 2, :],
                             start=True, stop=True)
            return i
        P.op("pe", mml, [self.b_dqT, lp], [p5s[0][1], p5s[1][1]])
        for h in range(4):
            P.v("act", "copy", [p5s[h % 2][1]], [self.b_qlatT], out=self.qlatT[:, h, :],
                in_=p5s[h % 2][0][:, (h // 2) * 128:(h // 2 + 1) * 128])
        P.v("dve", "tensor_scalar", [b_tb], [b_st], out=self.rs4[:, 0:8], in0=tb[:, 424:432], scalar1=0.0625, scalar2=None, op0=ALU.mult)
        P.v("dve", "tensor_tensor", [b_st] + c, [self.b_Dw], out=self.Dw[:], in0=bc(identf[:].unsqueeze(1), (128, 8, 128)),
            in1=bc(self.rs4[:, 0:8].unsqueeze(2), (128, 8, 128)), op=ALU.mult)
        score, b_score = self.score, self.b_score
        qidxT, kidx, Dw = self.qidxT, self.kidx, self.Dw
        nblk = (S + 511) // 512
        for kb in range(nblk):
            k0 = kb * 512
            n = min(512, S - k0)
            pacc, b_pacc = self.bank()
            phs = {}

            def idx_mm(h):
                ph, b_ph = self.bank((pacc,))
                r = slice((h % 3) * 32, (h % 3) * 32 + 32)
                P.op("pe", lambda e, ph=ph, r=r, h=h, k0=k0, n=n: e.matmul(
                    ph[:, 0:n], lhsT=qidxT[r, h // 3, :], rhs=kidx[r, k0:k0 + n], start=True, stop=True),
                    [self.b_qidxT, self.b_kidx], [b_ph])
                phs[h] = (ph, b_ph)
            idx_mm(0)
            for h in range(8):
                if h < 7:
                    idx_mm(h + 1)
                ph, b_ph = phs[h]
                rr, b_rr = self.rr[h % 2]
                P.v("act", "activation", [b_ph], [b_rr], out=rr[:, 0:n], in_=ph[:, 0:n], func=AF.Relu)
                P.op("pe", lambda e, pacc=pacc, rr=rr, h=h, n=n: e.matmul(
                    pacc[:, 0:n], lhsT=Dw[:, h, :], rhs=rr[:, 0:n], start=(h == 0), stop=(h == 7)),
                    [self.b_Dw, b_rr], [b_pacc])
            P.v("act", "copy", [b_pacc], [b_score], out=score[:, k0:k0 + n], in_=pacc[:, 0:n])
        bs, b_bs = self.bs, self.b_bs
        maskb, b_maskb = self.maskb, self.b_maskb
        if S > self.ksel:
            P.v("dve", "tensor_reduce", [b_score], [b_bs], out=bs[:, 0:1], in_=score[:, 0:S], axis=AX.X, op=ALU.min)
            P.v("dve", "tensor_reduce", [b_score], [b_bs], out=bs[:, 1:2], in_=score[:, 0:S], axis=AX.X, op=ALU.max)
            P.v("dve", "tensor_tensor", [b_score] + c, [b_score], out=score[:, T0:S], in0=score[:, T0:S], in1=self.caus[:], op=ALU.add)
            P.v("dve", "tensor_tensor", [b_bs], [b_bs], out=bs[:, 2:3], in0=bs[:, 1:2], in1=bs[:, 0:1], op=ALU.subtract)
            P.v("dve", "tensor_scalar", [b_bs] + c, [self.b_wk], out=self.wk[:], in0=self.pows[:], scalar1=bs[:, 2:3], scalar2=None, op0=ALU.mult)
            jb = self.Pm
            P.v("dve", "tensor_tensor", [b_bs, self.b_wk], [b_bs], out=bs[:, 3:4], in0=bs[:, 0:1], in1=self.wk[:, 0:1], op=ALU.add)
            for k in range(NBIS):
                P.v("dve", "tensor_scalar", [b_score, b_bs], [self.b_Pm, b_bs], out=jb[:, 0:S], in0=score[:, 0:S], scalar1=bs[:, 3:4],
                    scalar2=None, op0=ALU.is_ge, op1=ALU.add, accum_out=bs[:, 4:5])
                P.v("dve", "tensor_scalar", [b_bs], [b_bs], out=bs[:, 5:6], in0=bs[:, 4:5], scalar1=float(self.ksel) - 0.5, scalar2=0.5,
                    op0=ALU.is_ge, op1=ALU.subtract)
                P.v("dve", "scalar_tensor_tensor", [b_bs, self.b_wk], [b_bs], out=bs[:, 3:4], in0=bs[:, 5:6], scalar=self.wk[:, k:k + 1],
                    in1=bs[:, 3:4], op0=ALU.mult, op1=ALU.add)
            P.v("dve", "tensor_tensor", [b_bs, self.b_wk], [b_bs], out=bs[:, 0:1], in0=bs[:, 3:4], in1=self.wk[:, NBIS:NBIS + 1], op=ALU.subtract)
            P.v("dve", "tensor_scalar", [b_score, b_bs], [b_maskb], out=maskb[:, 0:S], in0=score[:, 0:S], scalar1=bs[:, 0:1], scalar2=NEG,
                op0=ALU.is_lt, op1=ALU.mult)
        else:
            if T0 > 0:
                P.v("pool", "memset", [], [b_maskb], ap=maskb[:, 0:T0], constant=0.0)
            P.v("pool", "tensor_copy", c, [b_maskb], out=maskb[:, T0:S], in_=self.causb[:])
        qlatT, ckvT, Pm, b_Pm = self.qlatT, self.ckvT, self.Pm, self.b_Pm
        PT, b_PT = self.PT[0]
        pol, b_pol = self.bank()
        nb = S // 128
        rs4, b_rs4 = self.rs4, self.b_rs4
        for h in range(4):
            for kb in range(nblk):
                k0 = kb * 512
                n = min(512, S - k0)
                pl, b_pl = self.bank((pol,))
                def mmlg(e, pl=pl, h=h, k0=k0, n=n):
                    e.matmul(pl[:, 0:n], lhsT=qlatT[:, h, :], rhs=ckvT[:, k0:k0 + n], start=True, stop=False)
                    return e.matmul(pl[:, 0:n], lhsT=identb[:], rhs=maskb[:, k0:k0 + n], start=False, stop=True)
                P.op("pe", mmlg, [self.b_qlatT, self.b_ckvT, b_maskb] + c, [b_pl])
                P.v("act", "copy", [b_pl], [b_score], out=score[:, k0:k0 + n], in_=pl[:, 0:n])
            P.v("dve", "tensor_reduce", [b_score], [b_bs], out=bs[:, 8:9], in_=score[:, 0:S], axis=AX.X, op=ALU.max, negate=True)
            P.v("act", "activation", [b_score, b_bs], [b_Pm, b_rs4], out=Pm[:, 0:S], in_=score[:, 0:S], func=AF.Exp, bias=bs[:, 8:9],
                accum_out=rs4[:, 8 + h - 8 + 0:8 + h - 8 + 1] if False else self.bs[:, 10 + h:11 + h])
            for g0 in range(0, nb, 8):
                g1 = min(nb, g0 + 8)
                ptp, b_ptp = self.bank((pol,))
                ptb = ptp[:].bitcast(BF16)

                def trp(e, g0=g0, g1=g1, ptb=ptb):
                    for b_ in range(g0, g1):
                        i = e.transpose(out=ptb[:, (b_ - g0) * 128:(b_ - g0 + 1) * 128], in_=Pm[:, b_ * 128:(b_ + 1) * 128], identity=identb[:])
                    return i
                P.op("pe", trp, [b_Pm] + c, [b_ptp])
                ng = g1 - g0
                P.v("act", "copy", [b_ptp], [b_PT], out=PT[:, 0:ng, :], in_=ptb[:, 0:ng * 128].rearrange("p (b i) -> p b i", b=ng))

                def mmpv(e, g0=g0, g1=g1, h=h):
                    for b_ in range(g0, g1):
                        i = e.matmul(pol[:, h * 128:(h + 1) * 128], lhsT=ckvK[:, b_, :], rhs=PT[:, b_ - g0, :],
                                     start=(b_ == 0), stop=(b_ == nb - 1))
                    return i
                P.op("pe", mmpv, [self.b_ckvK, b_PT], [b_pol])
        P.v("act", "copy", [b_pol], [self.b_olatT], out=self.olatT[:], in_=pol[:].rearrange("p (h i) -> p h i", h=4))
        py, b_py = self.bank()
        olatT, wuv = self.olatT, self.wuv

        def mmy(e):
            for h in range(4):
                i = e.matmul(py[:, h * 64:(h + 1) * 64], lhsT=olatT[:, h, :], rhs=wuv[:, h, :], start=True, stop=True)
            return i
        P.op("pe", mmy, [self.b_olatT, lp], [b_py])
        P.v("dve", "reciprocal", [b_bs], [b_bs], out=bs[:, 10:14], in_=bs[:, 10:14])
        P.v("dve", "tensor_tensor", [b_py, b_bs], [self.b_mix], out=self.mix[:, 512:768].rearrange("p (h d) -> p h d", h=4),
            in0=py[:, 0:256].rearrange("p (h d) -> p h d", h=4), in1=bc(bs[:, 10:14].unsqueeze(2), (128, 4, 64)), op=ALU.mult)


PARAM_NAMES = ["attn_norm", "w_in", "gdn_conv", "gdn_a_log", "gdn_dt_bias", "gdn_out_norm", "dsa_q_norm",
               "dsa_kv_norm", "dsa_w_uq", "dsa_w_qidx", "dsa_w_uk", "dsa_w_uv", "mlstm_i_bias",
               "mlstm_f_bias", "mlstm_out_norm", "w_out", "ffn_norm", "w_gate", "w_up", "w_down"]


FUSED = True
_CACHE = {}


def _builder(T, L, final, ksel):
    key = (T, L, final, ksel)
    if key not in _CACHE:
        _CACHE[key] = Builder(T, list(range(L)), final, ksel)
    return _CACHE[key]


def kernel(**inputs):
    x = np.ascontiguousarray(inputs["x"], dtype=np.float32)
    Bn, T, _ = x.shape
    depth = inputs["w_in"].shape[0]
    ksel = min(256, T // 4)
    prm = {k: np.ascontiguousarray(inputs[k], dtype=np.float32) for k in PARAM_NAMES}
    fin = np.ascontiguousarray(inputs["final_norm"], dtype=np.float32)
    cores = list(range(Bn))
    if FUSED:
        b = _builder(T, depth, True, ksel)
        in_maps = [dict(prm, final_norm=fin, x=x[i]) for i in range(Bn)]
        res = run_bass_kernel_spmd(b.nc, in_maps, core_ids=cores)
        return np.stack([np.asarray(r["y"]) for r in res.results], axis=0).astype(np.float32)
    cur = [x[i] for i in range(Bn)]
    for l in range(depth):
        last = l == depth - 1
        b = _builder(T, 1, last, ksel)
        lw = {k: np.ascontiguousarray(v[l:l + 1]) for k, v in prm.items()}
        in_maps = [dict(lw, final_norm=fin, x=np.ascontiguousarray(cur[i])) for i in range(Bn)]
        res = run_bass_kernel_spmd(b.nc, in_maps, core_ids=cores)
        cur = [np.asarray(r["y"]) for r in res.results]
    return np.stack(cur, axis=0).astype(np.float32)
```

```python
from contextlib import ExitStack
import os
import threading
import numpy as np
import concourse.bass as bass
import concourse.mybir as mybir
from concourse.bass_utils import run_bass_kernel_spmd

F32 = mybir.dt.float32
BF16 = mybir.dt.bfloat16
ALU = mybir.AluOpType
AF = mybir.ActivationFunctionType
AX = mybir.AxisListType

D = 1024
KC = 8
FF = 2816
NJ = 22
IN_DIM = 3512
GQ, GK, GV, GZ, GA, GB_, CQ, CKV, IK, IW, MQ, MK, MV, MO, MI, MF = (
    0, 512, 1024, 1536, 2048, 2052, 2056, 2312, 2440, 2472, 2480, 2736, 2992, 3248, 3504, 3508)
EPS = 1e-6
NEG = -30000.0
NBIS = 12


class Buf:
    __slots__ = ("name", "w", "r", "sem", "cnt", "al", "psum")

    def __init__(self, name):
        self.name = name
        self.w = None
        self.r = {}
        self.sem = None
        self.cnt = 0
        self.al = []
        self.psum = False


class Prog:
    ENGS = ("pe", "act", "dve", "pool", "sp")

    def __init__(self, nc, stack):
        self.nc = nc
        self.stack = stack
        self.stream = {e: [] for e in self.ENGS}
        self.cnt = {e: 0 for e in self.ENGS}
        self.waited = {e: {} for e in self.ENGS}
        self.semobj = {e: stack.enter_context(nc.semaphore("s_" + e)) for e in self.ENGS}
        self.nbuf = 0
        self.dsems = []
        self.ilv = None

    def buf(self, name):
        return Buf(name)

    def _need(self, eng, tok):
        if tok is None:
            return
        key, val = tok
        if key == eng:
            if eng == "pe":
                return
            if val < self.cnt[eng] - 1:
                return
        if self.waited[eng].get(key, 0) >= val:
            return
        self.waited[eng][key] = val
        self.stream[eng].append(("w", key, val))

    def _deps(self, eng, reads, writes):
        for b in reads:
            self._need(eng, b.w)
            if b.psum:
                for t in b.r.items():
                    if t[0] != eng:
                        self._need(eng, t)
        for b in writes:
            for x in [b] + b.al:
                self._need(eng, x.w)
                for t in x.r.items():
                    self._need(eng, t)

    def _mark(self, tok, reads, writes):
        for b in reads:
            b.r[tok[0]] = tok[1]
        for b in writes:
            b.w = tok
            b.r = {}

    def op(self, eng, fn, reads=(), writes=()):
        if self.ilv is not None:
            self.ilv()
        self._deps(eng, reads, writes)
        self.cnt[eng] += 1
        tok = (eng, self.cnt[eng])
        self.stream[eng].append(("o", fn, eng, 1))
        self._mark(tok, reads, writes)

    def v(self, eng, meth, reads, writes, **kw):
        self.op(eng, lambda e: getattr(e, meth)(**kw), reads, writes)

    def dma(self, q, out_ap, in_ap, reads, wbuf, **kw):
        self._deps(q, reads, (wbuf,))
        if wbuf.sem is None:
            wbuf.sem = {}
            wbuf.cnt = {}
        if q not in wbuf.sem:
            self.nbuf += 1
            key = ("d", self.nbuf)
            wbuf.sem[q] = key
            wbuf.cnt[q] = 0
            self.dsems.append((wbuf, q))
            self.semobj[key] = self.stack.enter_context(self.nc.semaphore("d%d" % self.nbuf))
        wbuf.cnt[q] += 16
        key = wbuf.sem[q]
        tok = (key, wbuf.cnt[q])
        self.stream[q].append(("o", lambda e: e.dma_start(out=out_ap, in_=in_ap, **kw), key, 16))
        self._mark(tok, reads, (wbuf,))

    def barrier(self):
        toks = [(e, self.cnt[e]) for e in self.ENGS if self.cnt[e] > 0]
        toks += [(b.sem[q], b.cnt[q]) for (b, q) in self.dsems]
        for e in self.ENGS:
            for t in toks:
                if t[0] != e:
                    self._need(e, t)

    def final_wait(self, eng, bufs):
        for b in bufs:
            self._need(eng, b.w)

    def emit(self):
        nc = self.nc
        with nc.Block() as block:
            def mk(ename):
                def body(e):
                    for it in self.stream[ename]:
                        if it[0] == "w":
                            e.wait_ge(self.semobj[it[1]], it[2])
                        else:
                            it[1](e).then_inc(self.semobj[it[2]], it[3])
                return body
            block.tensor(mk("pe"))
            block.scalar(mk("act"))
            block.vector(mk("dve"))
            block.gpsimd(mk("pool"))
            block.sync(mk("sp"))


def bc(ap, shape):
    return ap.to_broadcast(list(shape))


class Builder:
    def __init__(self, T, layers, final, ksel, dbg=None, first=True):
        self.T, self.layers, self.final, self.ksel, self.dbg = T, layers, final, ksel, dbg
        self.NT = T // 128
        nc = self.nc = bass.Bass("TRN2", target_bir_lowering=False)
        L = len(layers)
        di = lambda n, s: nc.dram_tensor(n, list(s), F32, kind="ExternalInput").ap()
        self.x_in = di("x", (T, D))
        self.prm = dict(
            attn_norm=di("attn_norm", (L, D)), w_in=di("w_in", (L, D, IN_DIM)),
            gdn_conv=di("gdn_conv", (L, 4, 1536)), gdn_a_log=di("gdn_a_log", (L, 4)),
            gdn_dt_bias=di("gdn_dt_bias", (L, 4)), gdn_out_norm=di("gdn_out_norm", (L, 128)),
            dsa_q_norm=di("dsa_q_norm", (L, 256)), dsa_kv_norm=di("dsa_kv_norm", (L, 128)),
            dsa_w_uq=di("dsa_w_uq", (L, 256, 256)), dsa_w_qidx=di("dsa_w_qidx", (L, 256, 256)),
            dsa_w_uk=di("dsa_w_uk", (L, 4, 128, 64)), dsa_w_uv=di("dsa_w_uv", (L, 4, 128, 64)),
            mlstm_i_bias=di("mlstm_i_bias", (L, 4)), mlstm_f_bias=di("mlstm_f_bias", (L, 4)),
            mlstm_out_norm=di("mlstm_out_norm", (L, 64)), w_out=di("w_out", (L, D, D)),
            ffn_norm=di("ffn_norm", (L, D)), w_gate=di("w_gate", (L, D, FF)),
            w_up=di("w_up", (L, D, FF)), w_down=di("w_down", (L, FF, D)),
            final_norm=di("final_norm", (D,)))
        self.y_out = nc.dram_tensor("y", [T, D], F32, kind="ExternalOutput").ap()
        self.xres = nc.dram_tensor("xres", [T, D], F32).ap()
        if dbg:
            self.dbg_out = nc.dram_tensor("dbg", [T, D], F32, kind="ExternalOutput").ap()
        with ExitStack() as st:
            self.st = st
            self.P = Prog(nc, st)
            self.pre_alloc()
            self.alloc()
            self.consts()
            for li in range(L):
                self.layer(li)
            self.P.final_wait("sp", [self.b_y] + ([self.b_dbg] if dbg else []))
            self.P.final_wait("pool", [self.b_y])
            self.P.emit()

    def sb(self, name, shape, dt=F32):
        t = self.nc.alloc_sbuf_tensor(name, list(shape), dt)
        b = self.P.buf(name)
        return t, b

    def frame_alloc(self, name, shape, dt=F32, part=128):
        n = 1
        for d in shape[1:]:
            n *= d
        units = n * (2 if dt == F32 else 1)
        units += units % 2
        self.fo = (self.fo + 31) // 32 * 32
        assert self.fo + units <= self.FN, (name, self.fo, units, self.FN)
        ap = self.F[:, self.fo:self.fo + units]
        self.fo += units
        self.fmax = max(self.fmax, self.fo)
        if dt == F32:
            ap = ap.bitcast(F32)
        ap = ap[:, 0:n]
        if len(shape) == 3:
            ap = ap.rearrange("p (a b) -> p a b", a=shape[1])
        if shape[0] != 128:
            ap = ap[0:shape[0]]
        return ap, self.P.buf(name)

    def pre_alloc(self):
        self.xt = [self.sb("xt%d" % i, (128, D)) for i in range(2)]
        self.xti = 0
        self.junk, self.b_junk = self.sb("junk", (128, 1536))
        self.st1 = [self.sb("st1_%d" % i, (128, 4)) for i in range(2)]
        self.xn, self.b_xn = self.sb("xn", (128, D), BF16)
        self.hT, self.b_hT = self.sb("hT", (128, KC, 128), BF16)
        self.gA, _ = self.sb("gA", (128, KC))
        self.gF, _ = self.sb("gF", (128, KC))

    def alloc(self):
        P, nc, T = self.P, self.nc, self.T
        self.FN = 94000
        self.F = nc.alloc_sbuf_tensor("F", [128, self.FN], BF16)
        self.fo = 0
        self.fmax = 0
        fa = self.frame_alloc
        self.Wg, self.b_Wg = fa("Wg", (128, KC, FF), BF16)
        self.Wu, self.b_Wu = fa("Wu", (128, KC, FF), BF16)
        self.Wd, self.b_Wd = fa("Wd", (128, NJ, D), BF16)
        self.actT, self.b_actT = fa("actT", (128, NJ, 128), BF16)
        self.sg, self.b_sg = fa("sg", (128, 128))
        self.gN, self.b_gN = fa("gN", (128, D))
        self.fo = 0
        self.Wi, self.b_Wi = fa("Wi", (128, KC, IN_DIM), BF16)
        self.Wo, self.b_Wo = fa("Wo", (128, KC, D), BF16)
        self.kidx, self.b_kidx = fa("kidx", (128, T), BF16)
        self.ckvT, self.b_ckvT = fa("ckvT", (128, T), BF16)
        self.ckvK, self.b_ckvK = fa("ckvK", (128, T // 128, 128), BF16)
        arena0 = self.fo
        self.score, self.b_score = fa("score", (128, T))
        self.Pm, self.b_Pm = fa("Pm", (128, T), BF16)
        self.maskb, self.b_maskb = fa("maskb", (128, T), BF16)
        arena1 = max(self.fo, arena0 + 16384)
        big = [self.b_score, self.b_Pm, self.b_maskb]
        self.fo = arena0
        f4 = (128, 4, 128)
        ov = []

        def fo_(name, shape, dt=F32):
            ap, b = fa(name, shape, dt)
            ov.append(b)
            return ap, b
        self.cbuf, self.b_cbuf = fo_("cbuf", (128, 12, 131))
        self.cact, self.b_cact = fo_("cact", (128, 12, 128))
        self.rn, self.b_rn = fo_("rn", (128, 8, 128))
        self.Qm, self.b_Qm = fo_("Qm", f4)
        self.G4, self.b_G4 = fo_("G4", f4)
        self.dcT, self.b_dcT = fo_("dcT", f4)
        self.egcB, self.b_egcB = fo_("egcB", f4)
        self.qT, self.b_qT = fo_("qT", f4, BF16)
        self.kT, self.b_kT = fo_("kT", f4, BF16)
        self.qdT, self.b_qdT = fo_("qdT", f4, BF16)
        self.kbg, self.b_kbg = fo_("kbg", f4, BF16)
        self.kd0, self.b_kd0 = fo_("kd0", f4, BF16)
        self.kd1, self.b_kd1 = fo_("kd1", f4, BF16)
        self.vb, self.b_vb = fo_("vb", f4, BF16)
        assert self.fo <= arena1, (self.fo, arena1)
        for b_ in ov:
            b_.al = list(big)
        for b_ in big:
            b_.al = list(ov)
        self.fo = arena1
        self.mixer_alloc()
        self.ps = []
        for i in range(8):
            t = self.st.enter_context(nc.psum_tensor("ps%d" % i, [128, 512], F32))
            pb_ = P.buf("ps%d" % i)
            pb_.psum = True
            self.ps.append((t, pb_))
        self.psi = 0
        self._tl = threading.local()
        self.b_y = P.buf("y")
        self.b_dbg = P.buf("dbg")
        self.b_xres = P.buf("xres")
        self.b_prm = P.buf("prm")

    def bank(self, excl=()):
        grp = getattr(self._tl, "grp", None)
        if grp is not None:
            lst, st = grp
            t, b = self.ps[lst[st[0] % len(lst)]]
            st[0] += 1
            return t, b
        while True:
            t, b = self.ps[self.psi % 8]
            self.psi += 1
            if not any(t is x for x in excl):
                return t, b

    def interleave(self, fa, fb, banks_a, banks_b):
        P = self.P
        sa, sb_ = threading.Semaphore(0), threading.Semaphore(0)
        done = {"a": False, "b": False}
        err = []

        def run(me, other, f, banks, s_me, s_other):
            self._tl.grp = (banks, [0])
            self._tl.me = me
            s_me.acquire()
            try:
                f()
            except BaseException as ex:
                err.append(ex)
            done[me] = True
            s_other.release()

        def hook():
            me = getattr(self._tl, "me", None)
            if me is None:
                return
            other = "b" if me == "a" else "a"
            if not done[other]:
                (sb_ if me == "a" else sa).release()
                (sa if me == "a" else sb_).acquire()
        ta = threading.Thread(target=run, args=("a", "b", fa, banks_a, sa, sb_))
        tb_ = threading.Thread(target=run, args=("b", "a", fb, banks_b, sb_, sa))
        P.ilv = hook
        ta.start()
        tb_.start()
        sa.release()
        ta.join()
        tb_.join()
        P.ilv = None
        if err:
            raise err[0]

    def consts(self):
        P = self.P
        sel = lambda out, pat, cmp, fill, base, cm, bufs: P.v(
            "pool", "affine_select", bufs, bufs, out=out, in_=out, pattern=pat, compare_op=cmp,
            fill=fill, base=base, channel_multiplier=cm)

        def chunkify(t, b, fill):
            sel(t[:, 0:64], [[0, 64]], ALU.is_ge, fill, 63, -1, [b])
            sel(t[:, 64:128], [[0, 64]], ALU.is_ge, fill, -64, 1, [b])
        self.identf, self.b_c = self.sb("identf", (128, 128))
        bcst = [self.b_c]
        P.v("pool", "memset", [], bcst, ap=self.identf[:], constant=1.0)
        sel(self.identf[:], [[-1, 128]], ALU.is_equal, 0.0, 0, 1, bcst)
        self.identb, _ = self.sb("identb", (128, 128), BF16)
        P.v("pool", "tensor_copy", bcst, bcst, out=self.identb[:], in_=self.identf[:])
        self.onesf, _ = self.sb("onesf", (128, 128))
        P.v("pool", "memset", [], bcst, ap=self.onesf[:], constant=1.0)
        self.onesb, _ = self.sb("onesb", (128, 128), BF16)
        P.v("pool", "memset", [], bcst, ap=self.onesb[:], constant=1.0)
        self.ones128b, _ = self.sb("ones128b", (128, 128), BF16)
        P.v("pool", "memset", [], bcst, ap=self.ones128b[:], constant=128.0)
        self.utriC, _ = self.sb("utriC", (128, 128))
        P.v("pool", "memset", [], bcst, ap=self.utriC[:], constant=1.0)
        sel(self.utriC[:], [[1, 128]], ALU.is_ge, 0.0, 0, -1, bcst)
        chunkify(self.utriC, self.b_c, 0.0)
        self.bonesC, _ = self.sb("bonesC", (128, 128))
        P.v("pool", "memset", [], bcst, ap=self.bonesC[:], constant=1.0)
        chunkify(self.bonesC, self.b_c, 0.0)
        self.mnegT, _ = self.sb("mnegT", (128, 128))
        P.v("pool", "memset", [], bcst, ap=self.mnegT[:], constant=0.0)
        sel(self.mnegT[:], [[1, 128]], ALU.is_ge, NEG, 0, -1, bcst)
        chunkify(self.mnegT, self.b_c, NEG)
        self.mposL, _ = self.sb("mposL", (128, 128))
        P.v("pool", "memset", [], bcst, ap=self.mposL[:], constant=0.0)
        sel(self.mposL[:], [[-1, 128]], ALU.is_ge, -NEG, -1, 1, bcst)
        chunkify(self.mposL, self.b_c, -NEG)
        self.caus, _ = self.sb("caus", (128, 128))
        P.v("pool", "memset", [], bcst, ap=self.caus[:], constant=0.0)
        sel(self.caus[:], [[-1, 128]], ALU.is_ge, -1e30, 0, 1, bcst)
        self.causb, _ = self.sb("causb", (128, 128), BF16)
        P.v("pool", "memset", [], bcst, ap=self.causb[:], constant=0.0)
        sel(self.causb[:], [[-1, 128]], ALU.is_ge, NEG, 0, 1, bcst)
        self.rowm, _ = self.sb("rowm", (128, 2))
        P.v("pool", "memset", [], bcst, ap=self.rowm[:], constant=1.0)
        sel(self.rowm[:, 0:1], [[0, 1]], ALU.is_ge, 0.0, 63, -1, bcst)
        sel(self.rowm[:, 1:2], [[0, 1]], ALU.is_ge, 0.0, -64, 1, bcst)
        self.epsc, _ = self.sb("epsc", (128, 2))
        P.v("pool", "memset", [], bcst, ap=self.epsc[:, 0:1], constant=EPS)
        P.v("pool", "memset", [], bcst, ap=self.epsc[:, 1:2], constant=128.0 * EPS)
        self.pows, _ = self.sb("pows", (128, NBIS + 1))
        for k in range(NBIS + 1):
            P.v("pool", "memset", [], bcst, ap=self.pows[:, k:k + 1], constant=float(2.0 ** -(k + 1)))

    def rstd(self, src_ap, src_bufs, n, st, b_st, col, eps_col=0, post_scale=None):
        P = self.P
        P.v("act", "activation", src_bufs, [self.b_junk, b_st], out=self.junk[:, 0:n], in_=src_ap,
            func=AF.Square, accum_out=st[:, col:col + 1])
        P.v("act", "activation", [b_st, self.b_c], [b_st], out=st[:, col:col + 1], in_=st[:, col:col + 1],
            func=AF.Ln, scale=1.0 / n, bias=self.epsc[:, eps_col:eps_col + 1])
        P.v("act", "activation", [b_st], [b_st], out=st[:, col:col + 1], in_=st[:, col:col + 1],
            func=AF.Exp, scale=-0.5)

    def norm_T(self, xt, b_xt, gain, ncols_off, width=128):
        P = self.P
        st, b_st = self.st1[self.psi % 2]
        self.rstd(xt[:], [b_xt], D, st, b_st, 0)
        P.v("act", "activation", [b_xt, b_st], [self.b_xn], out=self.xn[:], in_=xt[:], func=AF.Copy,
            scale=st[:, 0:1])
        pt, b_pt = self.bank()
        ptb = pt[:].bitcast(BF16)
        xn, identb = self.xn, self.identb

        def tr(e):
            for k in range(KC):
                i = e.transpose(out=ptb[:, k * 128:(k + 1) * 128], in_=xn[:, k * 128:(k + 1) * 128],
                                identity=identb[:])
            return i
        P.op("pe", tr, [self.b_xn, self.b_c], [b_pt])
        P.v("dve", "tensor_tensor", [b_pt, self.b_prm], [self.b_hT],
            out=self.hT[:, :, ncols_off:ncols_off + 128],
            in0=ptb.rearrange("p (k t) -> p k t", k=KC),
            in1=bc(gain[:].unsqueeze(2), (128, KC, 128)), op=ALU.mult)

    def load_x(self, li, ti):
        P = self.P
        xt, b_xt = self.xt[self.xti % 2]
        self.xti += 1
        if li == 0:
            P.dma("sp", xt[:], self.x_in[ti * 128:(ti + 1) * 128, :], [], b_xt)
        else:
            P.dma("sp", xt[:], self.xres[ti * 128:(ti + 1) * 128, :], [self.b_xres], b_xt)
        return xt, b_xt

    def layer(self, li):
        P, prm = self.P, self.prm
        l = li
        P.barrier()
        P.dma("pool", self.Wi, prm["w_in"][l].rearrange("(k p) n -> p k n", p=128), [], self.b_Wi)
        P.dma("pool", self.Wo, prm["w_out"][l].rearrange("(k p) n -> p k n", p=128), [], self.b_Wo)
        P.dma("sp", self.gA[:], prm["attn_norm"][l].rearrange("(k p) -> p k", p=128), [], self.b_prm,
              allow_slow_non_contiguous=True)
        P.dma("sp", self.gF[:], prm["ffn_norm"][l].rearrange("(k p) -> p k", p=128), [], self.b_prm,
              allow_slow_non_contiguous=True)
        self.mixer_setup(li)
        for ti in range(self.NT):
            self.mixer_tile(li, ti)
        P.barrier()
        P.dma("pool", self.Wg, prm["w_gate"][l].rearrange("(k p) n -> p k n", p=128), [], self.b_Wg)
        P.dma("pool", self.Wu, prm["w_up"][l].rearrange("(k p) n -> p k n", p=128), [], self.b_Wu)
        P.dma("pool", self.Wd, prm["w_down"][l].rearrange("(k p) n -> p k n", p=128), [], self.b_Wd)
        last = self.final and li == len(self.layers) - 1
        if last:
            P.dma("sp", self.gN[:], prm["final_norm"].partition_broadcast(128), [], self.b_gN)
        for ti in range(self.NT):
            self.ffn_tile(li, ti, last)

    def ffn_tile(self, li, ti, last):
        P = self.P
        xt, b_xt = self.xt[self.xti % 2]
        self.xti += 1
        P.dma("sp", xt[:], self.xres[ti * 128:(ti + 1) * 128, :], [self.b_xres], b_xt)
        self.norm_T(xt, b_xt, self.gF, 0)
        hT, Wg, Wu, Wd, actT = self.hT, self.Wg, self.Wu, self.Wd, self.actT
        for j in range(NJ):
            pg, b_pg = self.bank()

            def mm(e, j=j, pg=pg):
                for (W, c0) in ((Wg, 0), (Wu, 128)):
                    for k in range(KC):
                        i = e.matmul(pg[:, c0:c0 + 128], lhsT=W[:, k, j * 128:(j + 1) * 128], rhs=hT[:, k, :],
                                     start=(k == 0), stop=(k == KC - 1))
                return i
            P.op("pe", mm, [self.b_hT, self.b_Wg, self.b_Wu], [b_pg])
            P.v("act", "activation", [b_pg], [self.b_sg], out=self.sg[:], in_=pg[:, 0:128], func=AF.Silu)
            P.v("dve", "tensor_tensor", [self.b_sg, b_pg], [self.b_actT], out=actT[:, j, :], in0=self.sg[:],
                in1=pg[:, 128:256], op=ALU.mult)
        for n in range(2):
            pd, b_pd = self.bank()

            def mm2(e, n=n, pd=pd):
                for j in range(NJ):
                    i = e.matmul(pd[:], lhsT=actT[:, j, :], rhs=Wd[:, j, n * 512:(n + 1) * 512],
                                 start=(j == 0), stop=(j == NJ - 1))
                return i
            P.op("pe", mm2, [self.b_actT, self.b_Wd], [b_pd])
            P.v("dve", "tensor_tensor", [b_xt, b_pd], [b_xt], out=xt[:, n * 512:(n + 1) * 512],
                in0=xt[:, n * 512:(n + 1) * 512], in1=pd[:], op=ALU.add)
        if last:
            st, b_st = self.st1[ti % 2]
            self.rstd(xt[:], [b_xt], D, st, b_st, 1)
            P.v("dve", "scalar_tensor_tensor", [b_xt, b_st, self.b_gN], [b_xt], out=xt[:], in0=xt[:],
                scalar=st[:, 1:2], in1=self.gN[:], op0=ALU.mult, op1=ALU.mult)
            P.dma("sp", self.y_out[ti * 128:(ti + 1) * 128, :], xt[:], [b_xt], self.b_y)
        elif li == len(self.layers) - 1:
            P.dma("sp", self.y_out[ti * 128:(ti + 1) * 128, :], xt[:], [b_xt], self.b_y)
        else:
            P.dma("sp", self.xres[ti * 128:(ti + 1) * 128, :], xt[:], [b_xt], self.b_xres)

    def mixer_alloc(self):
        sb = self.frame_alloc
        f4 = (128, 4, 128)
        self.tb, self.b_tb = sb("tb", (128, 432))
        self.zs, self.b_zs = sb("zs", (128, 512))
        self.mkv, self.b_mkv = sb("mkv", (128, 512))
        self.mos, self.b_mos = sb("mos", (128, 256))
        self.mix, self.b_mix = sb("mix", (128, D), BF16)
        self.mixT, self.b_mixT = sb("mixT", (128, KC, 128), BF16)
        self.halo, self.b_halo = sb("halo", (128, 12, 3))
        self.ctmp, self.b_ctmp = self.junk[:, 0:1536].rearrange("p (m t) -> p m t", m=12), self.b_junk
        self.Nm, self.b_Nm = self.rn[:, 0:4, :], self.b_rn
        self.Bm, self.b_Bm = self.rn[:, 4:8, :], self.b_rn
        self.dcL, self.b_dcL = self.G4, self.b_G4
        self.u, self.b_u = self.egcB, self.b_egcB
        self.osb, self.b_osb = self.dcT, self.b_dcT
        self.sq, self.b_sq = sb("sq", (128, 8, 128), BF16)
        self.qkT, self.b_qkT = sb("qkT", f4, BF16)
        self.Tt, self.b_Tt = sb("Tt", f4, BF16)
        self.wT, self.b_wT = sb("wT", f4, BF16)
        self.vnew, self.b_vnew = sb("vnew", f4, BF16)
        self.S32, self.b_S32 = sb("S32", f4)
        self.Sbf, self.b_Sbf = sb("Sbf", f4, BF16)
        self.gt, self.b_gt = sb("gt", (128, 64))
        self.edec, self.b_edec = sb("edec", (128, 4, 2))
        self.mqT, self.b_mqT = sb("mqT", (128, 2, 128), BF16)
        self.mqk_st, self.b_mqk_st = sb("mqk_st", (128, 512), BF16)
        self.g8, self.b_g8 = sb("g8", (128, 2, 16))
        self.mkT, self.b_mkT = sb("mkT", (128, 2, 128), BF16)
        self.pk, self.b_pk = sb("pk", (4, 5, 128))
        self.fm, self.b_fm = sb("fm", (4, 2, 128))
        self.car, self.b_car = sb("car", (4, 4))
        self.bsrc, self.b_bsrc = sb("bsrc", (4, 130))
        self.bd, self.b_bd = sb("bd", (4, 4, 130))
        self.tm, self.b_tm = sb("tm", (128, 5, 4))
        self.ex, self.b_ex = sb("ex", (128, 6, 4))
        self.ET, self.b_ET = sb("ET", f4)
        self.sT, self.b_sT = sb("sT", f4, BF16)
        self.vaug, self.b_vaug = sb("vaug", (128, 4, 65), BF16)
        self.sv, self.b_sv = sb("sv", (128, 4, 65))
        self.kwk0, self.b_kwk0 = sb("kwk0", (128, 4, 64), BF16)
        self.kwk1, self.b_kwk1 = sb("kwk1", (128, 4, 64), BF16)
        self.C32, self.b_C32 = sb("C32", (128, 2, 65))
        self.Cbf, self.b_Cbf = sb("Cbf", (128, 2, 65), BF16)
        self.rsb, self.b_rsb = sb("rsb", (128, 4, 65))
        self.edm, self.b_edm = sb("edm", (128, 4, 2))
        self.hout, self.b_hout = sb("hout", (128, 4, 64))
        self.cqn, self.b_cqn = sb("cqn", (128, 256), BF16)
        self.cqT, self.b_cqT = sb("cqT", (128, 2, 128), BF16)
        self.dqT, self.b_dqT = sb("dqT", (128, 2, 128), BF16)
        self.qlatT, self.b_qlatT = sb("qlatT", f4, BF16)
        self.qidxT, self.b_qidxT = sb("qidxT", (128, 3, 128), BF16)
        self.Dw, self.b_Dw = sb("Dw", (128, 8, 128), BF16)
        self.rr = [sb("rr%d" % i, (128, 512), BF16) for i in range(2)]
        self.PT = [sb("PT%d" % i, (128, 8, 128), BF16) for i in range(1)]
        self.olatT, self.b_olatT = sb("olatT", f4, BF16)
        self.bs, self.b_bs = sb("bs", (128, 16))
        self.wk, self.b_wk = sb("wk", (128, NBIS + 1))
        self.rs4, self.b_rs4 = sb("rs4", (128, 8))
        self.wconv, _ = sb("wconv", (128, 12, 4))
        self.pb, _ = sb("pb", (128, 24))
        self.gon, _ = sb("gon", (128, 128))
        self.mon, _ = sb("mon", (128, 64))
        self.gKVb, _ = sb("gKVb", (128, 128))
        self.gQ, _ = sb("gQ", (128, 2))
        self.p4, _ = sb("p4", (4, 4))
        self.Wuq, _ = sb("Wuq", (128, 2, 256), BF16)
        self.Wqi, _ = sb("Wqi", (128, 2, 256), BF16)
        self.wuv, _ = sb("wuv", (128, 4, 64), BF16)
        self.wuk_n, _ = sb("wuk_n", (128, 4, 64))
        self.wukT, _ = sb("wukT", (128, 2, 128), BF16)
        self.Wik4, _ = sb("Wik4", (128, KC, 128), BF16)
        self.b_lp = self.P.buf("layerprm")

    def mixer_setup(self, li):
        P, prm, l = self.P, self.prm, li
        self.km = os.environ.get("KMODE", "")
        if "nosetup" in self.km:
            return
        lp = self.b_lp
        sp = lambda out, in_, **kw: P.dma("sp", out, in_, [], lp, **kw)
        for j in range(4):
            sp(self.wconv[:, :, j], prm["gdn_conv"][l, j].rearrange("(m p) -> p m", p=128), allow_slow_non_contiguous=True)
        sp(self.pb[:, 0:4], prm["gdn_a_log"][l].partition_broadcast(128))
        sp(self.pb[:, 4:8], prm["gdn_dt_bias"][l].partition_broadcast(128))
        sp(self.gon[:], prm["gdn_out_norm"][l].partition_broadcast(128))
        sp(self.mon[:], prm["mlstm_out_norm"][l].partition_broadcast(128))
        sp(self.gKVb[:], prm["dsa_kv_norm"][l].partition_broadcast(128))
        sp(self.gQ[:], prm["dsa_q_norm"][l].rearrange("(k p) -> p k", p=128), allow_slow_non_contiguous=True)
        sp(self.p4[:, 0:1], prm["mlstm_i_bias"][l].rearrange("(h o) -> h o", o=1))
        sp(self.p4[:, 1:2], prm["mlstm_f_bias"][l].rearrange("(h o) -> h o", o=1))
        sp(self.wuk_n[:], prm["dsa_w_uk"][l].rearrange("h c d -> c h d"))
        P.dma("pool", self.Wuq[:], prm["dsa_w_uq"][l].rearrange("(k p) n -> p k n", p=128), [], lp)
        P.dma("pool", self.Wqi[:], prm["dsa_w_qidx"][l].rearrange("(k p) n -> p k n", p=128), [], lp)
        P.dma("pool", self.wuv[:], prm["dsa_w_uv"][l].rearrange("h c d -> c h d"), [], lp)
        P.v("act", "activation", [lp], [lp], out=self.pb[:, 8:12], in_=self.pb[:, 0:4], func=AF.Exp)
        P.v("dve", "tensor_scalar", [lp], [lp], out=self.pb[:, 8:12], in0=self.pb[:, 8:12], scalar1=-1.0,
            scalar2=None, op0=ALU.mult)
        P.v("dve", "tensor_scalar", [lp], [lp], out=self.p4[:, 2:3], in0=self.p4[:, 1:2], scalar1=-1.0,
            scalar2=None, op0=ALU.mult)
        pt, b_pt = self.bank()
        wn, identf = self.wuk_n, self.identf

        def tr(e):
            for m in range(2):
                i = e.transpose(out=pt[:, m * 128:(m + 1) * 128],
                                in_=wn[:, 2 * m:2 * m + 2, :], identity=identf[:])
            return i
        P.op("pe", tr, [lp, self.b_c], [b_pt])
        P.v("dve", "tensor_scalar", [b_pt], [lp], out=self.wukT[:], in0=pt[:, 0:256].rearrange("p (m c) -> p m c", m=2),
            scalar1=0.125, scalar2=None, op0=ALU.mult)
        for g in range(4):
            P.v("pool", "tensor_copy", [self.b_Wi], [lp], out=self.Wik4[:, :, g * 32:(g + 1) * 32],
                in_=self.Wi[:, :, IK:IK + 32])
        P.v("dve", "memset", [], [self.b_S32], ap=self.S32[:], constant=0.0)
        P.v("dve", "memset", [], [self.b_Sbf], ap=self.Sbf[:], constant=0.0)
        P.v("dve", "memset", [], [self.b_C32], ap=self.C32[:], constant=0.0)
        P.v("dve", "memset", [], [self.b_Cbf], ap=self.Cbf[:], constant=0.0)
        P.v("dve", "memset", [], [self.b_car], ap=self.car[:], constant=0.0)
        P.v("dve", "memset", [], [self.b_halo], ap=self.halo[:], constant=0.0)
        P.v("dve", "memset", [], [self.b_vnew], ap=self.vnew[:], constant=0.0)
        P.v("dve", "memset", [], [self.b_vaug], ap=self.vaug[:], constant=1.0)
        P.v("dve", "memset", [], [self.b_kidx], ap=self.kidx, constant=0.0)

    def mixer_tile(self, li, ti):
        P = self.P
        xt, b_xt = self.load_x(li, ti)
        if "nomix" in self.km:
            P.dma("sp", self.xres[ti * 128:(ti + 1) * 128, :], xt[:], [b_xt], self.b_xres)
            return
        self.norm_T(xt, b_xt, self.gA, 0)
        self.projections()
        if "noilv" in self.km:
            self.gdn()
            self.mlstm()
        else:
            self.interleave(self.gdn, self.mlstm, [0, 1, 2, 3, 4], [5, 6, 7])
        self.dsa(ti)
        pt, b_pt = self.bank()
        ptb = pt[:].bitcast(BF16)
        mix, identb, mixT, Wo = self.mix, self.identb, self.mixT, self.Wo

        def tr(e):
            for k in range(KC):
                i = e.transpose(out=ptb[:, k * 128:(k + 1) * 128], in_=mix[:, k * 128:(k + 1) * 128],
                                identity=identb[:])
            return i
        P.op("pe", tr, [self.b_mix, self.b_c], [b_pt])
        P.v("act", "copy", [b_pt], [self.b_mixT], out=mixT[:], in_=ptb.rearrange("p (k t) -> p k t", k=KC))
        for n in range(2):
            po, b_po = self.bank()

            def mm(e, n=n, po=po):
                for k in range(KC):
                    i = e.matmul(po[:], lhsT=mixT[:, k, :], rhs=Wo[:, k, n * 512:(n + 1) * 512],
                                 start=(k == 0), stop=(k == KC - 1))
                return i
            P.op("pe", mm, [self.b_mixT, self.b_Wo], [b_po])
            P.v("dve", "tensor_tensor", [b_xt, b_po], [b_xt], out=xt[:, n * 512:(n + 1) * 512],
                in0=xt[:, n * 512:(n + 1) * 512], in1=po[:], op=ALU.add)
        P.dma("sp", self.xres[ti * 128:(ti + 1) * 128, :], xt[:], [b_xt], self.b_xres)
        if self.dbg and li == 0:
            P.v("pool", "tensor_copy", [self.b_mix], [self.b_junk], out=self.junk[:, 0:D], in_=mix[:])
            P.dma("sp", self.dbg_out[ti * 128:(ti + 1) * 128, :], self.junk[:, 0:D], [self.b_junk], self.b_dbg)

    def proj_fm(self, col0, nm, dst_fn):
        P, Wi, hT = self.P, self.Wi, self.hT
        pt, b_pt = self.bank()

        def mm(e):
            for m in range(nm):
                for k in range(KC):
                    i = e.matmul(pt[:, m * 128:(m + 1) * 128], lhsT=Wi[:, k, col0 + m * 128:col0 + (m + 1) * 128],
                                 rhs=hT[:, k, 0:128], start=(k == 0), stop=(k == KC - 1))
            return i
        P.op("pe", mm, [self.b_hT, self.b_Wi], [b_pt])
        dst_fn(pt[:, 0:nm * 128].rearrange("p (m t) -> p m t", m=nm), b_pt)

    def proj_tm(self, col0, n, dst_fn):
        P, Wi, hT = self.P, self.Wi, self.hT
        pt, b_pt = self.bank()

        def mm(e):
            for k in range(KC):
                i = e.matmul(pt[:, 0:n], lhsT=hT[:, k, 0:128], rhs=Wi[:, k, col0:col0 + n],
                             start=(k == 0), stop=(k == KC - 1))
            return i
        P.op("pe", mm, [self.b_hT, self.b_Wi], [b_pt])
        dst_fn(pt, b_pt)

    def projections(self):
        P = self.P
        cb = self.cbuf
        km = self.km
        if "p1" not in km:
            for g, c0 in enumerate((GQ, GK, GV)):
                self.proj_fm(c0, 4, lambda ps, b, g=g: P.v("act", "copy", [b], [self.b_cbuf],
                                                           out=cb[:, 4 * g:4 * g + 4, 3:131], in_=ps))

        if "p2" not in km:
            stg, b_stg = self.mqk_st, self.b_mqk_st
            self.proj_tm(MQ, 512, lambda pt, b: P.v("act", "copy", [b], [b_stg], out=stg[:], in_=pt[:]))
            pq_, b_pq = self.bank()
            pqb = pq_[:].bitcast(BF16)
            identb = self.identb

            def trq(e):
                for m in range(4):
                    i = e.transpose(out=pqb[:, m * 128:(m + 1) * 128], in_=stg[:, m * 128:(m + 1) * 128], identity=identb[:])
                return i
            P.op("pe", trq, [b_stg, self.b_c], [b_pq])
            pq4 = pqb[:, 0:512].rearrange("p (m t) -> p m t", m=4)
            if "p2b" not in km:
                P.v("act", "copy", [b_pq], [self.b_mqT], out=self.mqT[:], in_=pq4[:, 0:2, :])
            if "p2a" not in km:
                P.v("dve", "tensor_scalar", [b_pq], [self.b_mkT], out=self.mkT[:], in0=pq4[:, 2:4, :], scalar1=0.125,
                    scalar2=None, op0=ALU.mult)
        if "p3" not in km:
            self.proj_tm(GZ, 512, lambda pt, b: P.v("act", "activation", [b], [self.b_zs], out=self.zs[:],
                                                    in_=pt[:], func=AF.Silu))
        if "p4" not in km:
            self.proj_tm(GA, 432, lambda pt, b: P.v("act", "copy", [b], [self.b_tb], out=self.tb[:], in_=pt[:, 0:432]))
        if "p5" not in km:
            self.proj_tm(MK, 512, lambda pt, b: P.v("dve", "tensor_copy", [b], [self.b_mkv], out=self.mkv[:], in_=pt[:]))
        if "p6" not in km:
            def mo_(pt, b):
                P.v("act", "activation", [b], [self.b_mos], out=self.mos[:], in_=pt[:, 0:256], func=AF.Sigmoid)
                P.v("dve", "tensor_copy", [b], [self.b_g8], out=self.g8[:, :, 0:4],
                    in_=pt[:, 256:264].rearrange("p (q h) -> p q h", q=2))
            self.proj_tm(MO, 264, mo_)

    def gdn(self):
        P = self.P
        c = [self.b_c]
        lp = self.b_lp
        gt, b_gt, tb = self.gt, self.b_gt, self.tb
        f4 = (128, 4, 128)
        P.v("dve", "tensor_tensor", [self.b_tb, lp], [b_gt], out=gt[:, 40:44], in0=tb[:, 0:4], in1=self.pb[:, 4:8], op=ALU.add)
        P.v("act", "activation", [b_gt], [b_gt], out=gt[:, 40:44], in_=gt[:, 40:44], func=AF.Exp)
        P.v("act", "activation", [b_gt], [b_gt], out=gt[:, 40:44], in_=gt[:, 40:44], func=AF.Ln, bias=1.0)
        P.v("dve", "tensor_tensor", [b_gt, lp], [b_gt], out=gt[:, 0:4], in0=gt[:, 40:44], in1=self.pb[:, 8:12], op=ALU.mult)
        P.v("act", "activation", [self.b_tb], [b_gt], out=gt[:, 4:8], in_=tb[:, 4:8], func=AF.Sigmoid)
        P.v("dve", "tensor_scalar", [b_gt], [b_gt], out=gt[:, 8:12], in0=gt[:, 4:8], scalar1=-1.0, scalar2=None, op0=ALU.mult)
        pg, b_pg = self.bank()
        utri, bones = self.utriC, self.bonesC

        def mm(e):
            e.matmul(pg[:, 0:4], lhsT=utri[:], rhs=gt[:, 0:4], start=True, stop=True)
            return e.matmul(pg[:, 4:8], lhsT=bones[:], rhs=gt[:, 0:4], start=True, stop=True)
        P.op("pe", mm, [b_gt] + c, [b_pg])
        P.v("dve", "tensor_copy", [b_pg], [b_gt], out=gt[:, 12:20], in_=pg[:, 0:8])
        P.v("act", "activation", [b_gt], [b_gt], out=gt[:, 20:24], in_=gt[:, 12:16], func=AF.Exp)
        P.v("dve", "tensor_tensor", [b_gt], [b_gt], out=gt[:, 40:44], in0=gt[:, 16:20], in1=gt[:, 12:16], op=ALU.subtract)
        P.v("act", "activation", [b_gt], [b_gt], out=gt[:, 24:28], in_=gt[:, 40:44], func=AF.Exp)
        P.v("dve", "tensor_tensor", [b_gt], [b_gt], out=gt[:, 28:32], in0=gt[:, 4:8], in1=gt[:, 20:24], op=ALU.mult)
        P.v("dve", "tensor_scalar", [b_gt] + c, [b_gt], out=gt[:, 32:36], in0=gt[:, 24:28], scalar1=self.rowm[:, 0:1], scalar2=None, op0=ALU.mult)
        P.v("dve", "tensor_scalar", [b_gt] + c, [b_gt], out=gt[:, 36:40], in0=gt[:, 24:28], scalar1=self.rowm[:, 1:2], scalar2=None, op0=ALU.mult)
        P.v("dve", "tensor_tensor", [b_gt] + c, [self.b_G4], out=self.G4[:], in0=bc(self.utriC[:].unsqueeze(1), f4),
            in1=bc(gt[:, 0:4].unsqueeze(2), f4), op=ALU.mult)
        pB, b_pB = self.bank()
        G4, onesf = self.G4, self.onesf
        P.op("pe", lambda e: e.matmul(pB[:], lhsT=onesf[:], rhs=G4[:].rearrange("p h i -> p (h i)"), start=True, stop=True),
             [self.b_G4] + c, [b_pB])
        pB4 = pB[:].rearrange("p (h i) -> p h i", h=4)
        P.v("act", "activation", [b_pB], [self.b_egcB], out=self.egcB[:], in_=pB4, func=AF.Exp)
        P.v("act", "activation", [b_pB], [self.b_edec], out=self.edec[:],
            in_=pB[:].rearrange("p (h c i) -> p h c i", h=4, c=2)[:, :, :, 63], func=AF.Exp)
        for h in range(4):
            P.v("dve", "scalar_tensor_tensor", [b_pB, b_gt] + c, [self.b_dcT], out=self.dcT[:, h, :], in0=pB4[:, h, :],
                scalar=gt[:, 12 + h:13 + h], in1=self.mnegT[:], op0=ALU.subtract, op1=ALU.add)
            P.v("dve", "scalar_tensor_tensor", [b_pB, b_gt] + c, [self.b_dcL], out=self.dcL[:, h, :], in0=pB4[:, h, :],
                scalar=gt[:, 12 + h:13 + h], in1=self.mposL[:], op0=ALU.subtract, op1=ALU.add)
        P.v("act", "activation", [self.b_dcT], [self.b_dcT], out=self.dcT[:], in_=self.dcT[:], func=AF.Exp)
        P.v("act", "activation", [self.b_dcL], [self.b_dcL], out=self.dcL[:], in_=self.dcL[:], func=AF.Exp, scale=-1.0)
        cb, ca, ct, wc = self.cbuf, self.cact, self.ctmp, self.wconv
        s12 = (128, 12, 128)
        P.v("pool", "tensor_copy", [self.b_halo], [self.b_cbuf], out=cb[:, :, 0:3], in_=self.halo[:])
        P.v("pool", "tensor_tensor", [self.b_cbuf, lp], [self.b_cact], out=ca[:], in0=cb[:, :, 0:128],
            in1=bc(wc[:, :, 0:1], s12), op=ALU.mult)
        for j in range(1, 4):
            P.v("pool", "tensor_tensor", [self.b_cbuf, lp], [self.b_ctmp], out=ct[:], in0=cb[:, :, j:j + 128],
                in1=bc(wc[:, :, j:j + 1], s12), op=ALU.mult)
            P.v("pool", "tensor_tensor", [self.b_cact, self.b_ctmp], [self.b_cact], out=ca[:], in0=ca[:], in1=ct[:], op=ALU.add)
        P.v("pool", "tensor_copy", [self.b_cbuf], [self.b_halo], out=self.halo[:], in_=cb[:, :, 128:131])
        P.v("act", "activation", [self.b_cact], [self.b_cact], out=ca[:], in_=ca[:], func=AF.Silu)
        P.v("pool", "tensor_tensor", [self.b_cact], [self.b_sq], out=self.sq[:], in0=ca[:, 0:8, :], in1=ca[:, 0:8, :], op=ALU.mult)
        sq, o128, ob = self.sq, self.ones128b, self.onesb
        for half, lh in ((0, o128), (1, ob)):
            pn, b_pn = self.bank()
            P.op("pe", lambda e, half=half, lh=lh, pn=pn: e.matmul(
                pn[:], lhsT=lh[:], rhs=sq[:, 4 * half:4 * half + 4, :].rearrange("p h i -> p (h i)"), start=True, stop=True),
                [self.b_sq] + c, [b_pn])
            rnv = self.rn[:, 4 * half:4 * half + 4, :].rearrange("p h i -> p (h i)")
            P.v("act", "activation", [b_pn] + c, [self.b_rn], out=rnv, in_=pn[:], func=AF.Ln,
                bias=self.epsc[:, 1 - half:2 - half])
            P.v("act", "activation", [self.b_rn], [self.b_rn], out=rnv, in_=rnv, func=AF.Exp, scale=-0.5)
        P.v("dve", "tensor_tensor", [self.b_cact, self.b_rn], [self.b_qT], out=self.qT[:], in0=ca[:, 0:4, :], in1=self.rn[:, 0:4, :], op=ALU.mult)
        P.v("pool", "tensor_tensor", [self.b_cact, self.b_rn], [self.b_kT], out=self.kT[:], in0=ca[:, 4:8, :], in1=self.rn[:, 4:8, :], op=ALU.mult)
        P.v("dve", "tensor_tensor", [self.b_qT, self.b_egcB], [self.b_qdT], out=self.qdT[:], in0=self.qT[:], in1=self.egcB[:], op=ALU.mult)
        pv, b_pv = self.bank()
        identf, identb, kT, qT = self.identf, self.identb, self.kT, self.qT

        def trv(e):
            for h in range(4):
                i = e.transpose(out=pv[:, h * 128:(h + 1) * 128], in_=ca[:, 8 + h, :], identity=identf[:])
            return i
        P.op("pe", trv, [self.b_cact] + c, [b_pv])
        P.v("dve", "tensor_tensor", [b_pv, b_gt], [self.b_vb], out=self.vb[:], in0=pv[:].rearrange("p (h d) -> p h d", h=4),
            in1=bc(gt[:, 4:8].unsqueeze(2), f4), op=ALU.mult)
        pk_, b_pk = self.bank()
        pkb = pk_[:].bitcast(BF16)

        def trk(e):
            for h in range(4):
                i = e.transpose(out=pkb[:, h * 128:(h + 1) * 128], in_=kT[:, h, :], identity=identb[:])
            return i
        P.op("pe", trk, [self.b_kT] + c, [b_pk])
        pk4 = pkb[:, 0:512].rearrange("p (h d) -> p h d", h=4)
        P.v("dve", "tensor_tensor", [b_pk, b_gt], [self.b_kbg], out=self.kbg[:], in0=pk4, in1=bc(gt[:, 28:32].unsqueeze(2), f4), op=ALU.mult)
        P.v("dve", "tensor_tensor", [b_pk, b_gt], [self.b_kd0], out=self.kd0[:], in0=pk4, in1=bc(gt[:, 32:36].unsqueeze(2), f4), op=ALU.mult)
        P.v("dve", "tensor_tensor", [b_pk, b_gt], [self.b_kd1], out=self.kd1[:], in0=pk4, in1=bc(gt[:, 36:40].unsqueeze(2), f4), op=ALU.mult)
        pkk, b_pkk = self.bank()
        pkq, b_pkq = self.bank()

        def mmk(e):
            for h in range(4):
                e.matmul(pkk[:, h * 128:(h + 1) * 128], lhsT=kT[:, h, :], rhs=kT[:, h, :], start=True, stop=True)
            for h in range(4):
                i = e.matmul(pkq[:, h * 128:(h + 1) * 128], lhsT=kT[:, h, :], rhs=qT[:, h, :], start=True, stop=True)
            return i
        P.op("pe", mmk, [self.b_kT, self.b_qT], [b_pkk, b_pkq])
        Nm, Bm, Qm = self.Nm, self.Bm, self.Qm
        for h in range(4):
            P.v("dve", "scalar_tensor_tensor", [b_pkk, b_gt, self.b_dcL], [self.b_Nm], out=Nm[:, h, :],
                in0=pkk[:, h * 128:(h + 1) * 128], scalar=gt[:, 8 + h:9 + h], in1=self.dcL[:, h, :], op0=ALU.mult, op1=ALU.mult)
        P.v("dve", "tensor_tensor", [b_pkq, self.b_dcT], [self.b_qkT], out=self.qkT[:], in0=pkq[:].rearrange("p (h i) -> p h i", h=4),
            in1=self.dcT[:], op=ALU.mult)
        pb0, b_pb0 = self.bank()

        def trn(e):
            for h in range(4):
                i = e.transpose(out=pb0[:, h * 128:(h + 1) * 128], in_=Nm[:, h, :], identity=identf[:])
            return i
        P.op("pe", trn, [self.b_Nm] + c, [b_pb0])
        pb04 = pb0[:].rearrange("p (h i) -> p h i", h=4)
        P.v("act", "copy", [b_pb0], [self.b_Bm], out=Bm[:], in_=pb04)
        P.v("dve", "tensor_tensor", [b_pb0] + c, [self.b_Qm], out=Qm[:], in0=pb04, in1=bc(identf[:].unsqueeze(1), f4), op=ALU.add)
        for lev in range(6):
            sqr = lev < 5
            upd = lev >= 1
            pN, b_pN = self.bank() if sqr else (None, None)
            pBb, b_pBb = self.bank() if sqr else (None, None)
            pQ, b_pQ = self.bank() if upd else (None, None)

            def mml(e, sqr=sqr, upd=upd, pN=pN, pBb=pBb, pQ=pQ):
                i = None
                for h in range(4):
                    sl = slice(h * 128, (h + 1) * 128)
                    if sqr:
                        e.matmul(pN[:, sl], lhsT=Bm[:, h, :], rhs=Nm[:, h, :], start=True, stop=True)
                        i = e.matmul(pBb[:, sl], lhsT=Nm[:, h, :], rhs=Bm[:, h, :], start=True, stop=True)
                    if upd:
                        i = e.matmul(pQ[:, sl], lhsT=Nm[:, h, :], rhs=Qm[:, h, :], start=True, stop=True)
                return i
            wr = [b for b in (b_pN, b_pBb, b_pQ) if b is not None]
            P.op("pe", mml, [self.b_Nm, self.b_Bm, self.b_Qm], wr)
            v4 = lambda t: t[:].rearrange("p (h i) -> p h i", h=4)
            if upd:
                if lev < 5:
                    P.v("dve", "tensor_tensor", [b_pQ, self.b_Qm], [self.b_Qm], out=Qm[:], in0=v4(pQ), in1=Qm[:], op=ALU.add)
                else:
                    P.v("dve", "tensor_tensor", [b_pQ, self.b_Qm], [self.b_Tt], out=self.Tt[:], in0=v4(pQ), in1=Qm[:], op=ALU.add)
            if sqr:
                P.v("act", "copy", [b_pN], [self.b_Nm], out=Nm[:], in_=v4(pN))
                P.v("dve", "tensor_copy", [b_pBb], [self.b_Bm], out=Bm[:], in_=v4(pBb))
        pu, b_pu = self.bank()
        pw, b_pw = self.bank()
        Tt, vb, kbg = self.Tt, self.vb, self.kbg

        def mmu(e):
            for h in range(4):
                e.matmul(pu[:, h * 128:(h + 1) * 128], lhsT=Tt[:, h, :], rhs=vb[:, h, :], start=True, stop=True)
            for h in range(4):
                i = e.matmul(pw[:, h * 128:(h + 1) * 128], lhsT=kbg[:, h, :], rhs=Tt[:, h, :], start=True, stop=True)
            return i
        P.op("pe", mmu, [self.b_Tt, self.b_vb, self.b_kbg], [b_pu, b_pw])
        P.v("act", "copy", [b_pu], [self.b_u], out=self.u[:], in_=pu[:].rearrange("p (h i) -> p h i", h=4))
        P.v("dve", "tensor_copy", [b_pw], [self.b_wT], out=self.wT[:], in_=pw[:].rearrange("p (h i) -> p h i", h=4))
        wT, Sbf, qdT, qkT, vnew, S32 = self.wT, self.Sbf, self.qdT, self.qkT, self.vnew, self.S32
        for cc in range(2):
            kd = self.kd0 if cc == 0 else self.kd1
            b_kd = self.b_kd0 if cc == 0 else self.b_kd1
            pV, b_pV = self.bank()

            def mmv(e, pV=pV):
                for h in range(4):
                    i = e.matmul(pV[:, h * 128:(h + 1) * 128], lhsT=wT[:, h, :], rhs=Sbf[:, h, :], start=True, stop=True)
                return i
            P.op("pe", mmv, [self.b_wT, self.b_Sbf], [b_pV])
            P.v("dve", "tensor_tensor", [self.b_u, b_pV], [self.b_vnew], out=vnew[:], in0=self.u[:],
                in1=pV[:].rearrange("p (h i) -> p h i", h=4), op=ALU.subtract)
            pO, b_pO = self.bank()
            pS, b_pS = self.bank()

            def mmo(e, pO=pO, pS=pS, kd=kd):
                for h in range(4):
                    sl = slice(h * 128, (h + 1) * 128)
                    e.matmul(pO[:, sl], lhsT=qdT[:, h, :], rhs=Sbf[:, h, :], start=True, stop=False)
                    e.matmul(pO[:, sl], lhsT=qkT[:, h, :], rhs=vnew[:, h, :], start=False, stop=True)
                for h in range(4):
                    i = e.matmul(pS[:, h * 128:(h + 1) * 128], lhsT=kd[:, h, :], rhs=vnew[:, h, :], start=True, stop=True)
                return i
            P.op("pe", mmo, [self.b_qdT, self.b_Sbf, self.b_qkT, self.b_vnew, b_kd], [b_pO, b_pS])
            rows = slice(cc * 64, (cc + 1) * 64)
            P.v("act", "copy", [b_pO], [self.b_osb], out=self.osb[rows], in_=pO[rows].rearrange("p (h i) -> p h i", h=4))
            P.v("dve", "tensor_tensor", [self.b_S32, self.b_edec], [self.b_S32], out=S32[:], in0=S32[:],
                in1=bc(self.edec[:, :, cc:cc + 1], f4), op=ALU.mult)
            P.v("dve", "tensor_tensor", [self.b_S32, b_pS], [self.b_S32], out=S32[:], in0=S32[:],
                in1=pS[:].rearrange("p (h i) -> p h i", h=4), op=ALU.add)
            P.v("act", "copy", [self.b_S32], [self.b_Sbf], out=Sbf[:], in_=S32[:])
        self.out_norm(self.osb, self.b_osb, 128, self.gon, self.zs, self.b_zs, 0)

    def out_norm(self, src, b_src, dv, gain, gate, b_gate, col0):
        P = self.P
        sh = (128, 4, dv)
        jv = self.junk[:, 0:4 * dv].rearrange("p (h d) -> p h d", h=4)
        st, b_st = self.st1[0]
        P.v("pool", "tensor_tensor", [b_src], [self.b_junk], out=jv, in0=src[:], in1=src[:], op=ALU.mult)
        P.v("dve", "tensor_reduce", [self.b_junk], [b_st], out=st[:, 0:4], in_=jv, axis=AX.X, op=ALU.add)
        P.v("act", "activation", [b_st, self.b_c], [b_st], out=st[:, 0:4], in_=st[:, 0:4], func=AF.Ln, scale=1.0 / dv,
            bias=self.epsc[:, 0:1])
        P.v("act", "activation", [b_st], [b_st], out=st[:, 0:4], in_=st[:, 0:4], func=AF.Exp, scale=-0.5)
        P.v("dve", "tensor_tensor", [b_src, b_st], [self.b_junk], out=jv, in0=src[:], in1=bc(st[:, 0:4].unsqueeze(2), sh), op=ALU.mult)
        P.v("pool", "tensor_tensor", [self.b_junk, self.b_lp], [self.b_junk], out=jv, in0=jv, in1=bc(gain[:].unsqueeze(1), sh), op=ALU.mult)
        P.v("dve", "tensor_tensor", [self.b_junk, b_gate], [self.b_mix], out=self.mix[:, col0:col0 + 4 * dv],
            in0=self.junk[:, 0:4 * dv], in1=gate[:, 0:4 * dv], op=ALU.mult)

    def mlstm(self):
        P = self.P
        c = [self.b_c]
        lp = self.b_lp
        f4 = (128, 4, 128)
        Wi, hT = self.Wi, self.hT
        pk, b_pk, fm, b_fm, car, b_car = self.pk, self.b_pk, self.fm, self.b_fm, self.car, self.b_car
        pg, b_pg = self.bank()

        g8, identf_ = self.g8, self.identf

        def mmg(e):
            for q in range(2):
                i = e.transpose(out=pg[0:4, q * 128:(q + 1) * 128], in_=g8[:, q, 0:4], identity=identf_[:])
            return i
        P.op("pe", mmg, [self.b_g8, self.b_c], [b_pg])
        P.v("act", "activation", [b_pg, lp], [b_fm], out=fm[:, 0, :], in_=pg[0:4, 128:256], func=AF.Exp, scale=-1.0,
            bias=self.p4[:, 2:3])
        P.v("act", "activation", [b_fm], [b_fm], out=fm[:, 0, :], in_=fm[:, 0, :], func=AF.Ln, bias=1.0)
        P.v("dve", "tensor_scalar", [b_pg, lp], [b_fm], out=fm[:, 1, :], in0=pg[0:4, 0:128], scalar1=self.p4[:, 0:1],
            scalar2=None, op0=ALU.add)
        P.v("dve", "tensor_tensor_scan", [b_fm, b_car] + c, [b_pk], out=pk[:, 2, :], data0=self.onesf[0:4, :],
            data1=fm[:, 0, :], initial=car[:, 0:1], op0=ALU.mult, op1=ALU.subtract)
        P.v("dve", "tensor_tensor", [b_fm, b_pk], [b_pk], out=pk[:, 0, :], in0=fm[:, 1, :], in1=pk[:, 2, :], op=ALU.subtract)
        P.v("dve", "tensor_tensor_scan", [b_pk, b_car], [b_pk], out=pk[:, 1, :], data0=pk[:, 0, :],
            data1=pk[:, 0, :], initial=car[:, 1:2], op0=ALU.max, op1=ALU.max)
        P.v("dve", "tensor_copy", [b_pk], [b_pk], out=pk[:, 3, :].rearrange("p (c i) -> p c i", c=2),
            in_=bc(pk[:, 1, :].rearrange("p (c i) -> p c i", c=2)[:, :, 63:64], (4, 2, 64)))
        P.v("dve", "tensor_copy", [b_car], [b_pk], out=pk[:, 4, 0:64], in_=bc(car[:, 1:2], (4, 64)))
        P.v("dve", "tensor_copy", [b_pk], [b_pk], out=pk[:, 4, 64:128], in_=bc(pk[:, 1, 63:64], (4, 64)))
        bsrc, b_bsrc, bd, b_bd = self.bsrc, self.b_bsrc, self.bd, self.b_bd
        P.v("dve", "tensor_scalar", [b_pk], [b_bsrc], out=bsrc[:, 0:128], in0=pk[:, 1, :], scalar1=-1.0, scalar2=None, op0=ALU.mult)
        P.v("dve", "tensor_tensor", [b_car, b_pk], [b_bsrc], out=bsrc[:, 128:129], in0=car[:, 1:2], in1=pk[:, 1, 63:64], op=ALU.subtract)
        P.v("dve", "tensor_tensor", [b_pk], [b_bsrc], out=bsrc[:, 129:130], in0=pk[:, 1, 63:64], in1=pk[:, 1, 127:128], op=ALU.subtract)
        P.v("dve", "tensor_tensor", [b_bsrc] + c, [b_bd], out=bd[:], in0=bc(bsrc[:].unsqueeze(1), (4, 4, 130)),
            in1=bc(self.identf[0:4, 0:4].unsqueeze(2), (4, 4, 130)), op=ALU.mult)
        P.v("dve", "tensor_copy", [b_pk], [b_car], out=car[:, 0:1], in_=pk[:, 2, 127:128])
        P.v("dve", "tensor_copy", [b_pk], [b_car], out=car[:, 1:2], in_=pk[:, 1, 127:128])
        ptm, b_ptm = self.bank()
        identf, onesf = self.identf, self.onesf

        def trt(e):
            for q in range(5):
                i = e.transpose(out=ptm[:, q * 4:(q + 1) * 4], in_=pk[:, q, :], identity=identf[0:4, 0:4])
            return i
        P.op("pe", trt, [b_pk] + c, [b_ptm])
        tm, b_tm, ex, b_ex = self.tm, self.b_tm, self.ex, self.b_ex
        P.v("dve", "tensor_copy", [b_ptm], [b_tm], out=tm[:], in_=ptm[:, 0:20].rearrange("p (q h) -> p q h", q=5))
        P.v("dve", "tensor_tensor", [b_tm], [b_ex], out=ex[:, 0, :], in0=tm[:, 0, :], in1=tm[:, 3, :], op=ALU.subtract)
        P.v("dve", "tensor_tensor", [b_tm], [b_ex], out=ex[:, 1, :], in0=tm[:, 4, :], in1=tm[:, 1, :], op=ALU.subtract)
        P.v("dve", "scalar_tensor_tensor", [b_tm], [b_ex], out=ex[:, 2, :], in0=tm[:, 2, :], scalar=-1.0, in1=tm[:, 1, :],
            op0=ALU.mult, op1=ALU.subtract)
        P.v("act", "activation", [b_ex], [b_ex], out=ex[:, 0:3, :], in_=ex[:, 0:3, :], func=AF.Exp)
        for cc in range(2):
            P.v("dve", "tensor_scalar", [b_ex] + c, [b_ex], out=ex[:, 3 + cc, :], in0=ex[:, 0, :], scalar1=self.rowm[:, cc:cc + 1],
                scalar2=0.125, op0=ALU.mult, op1=ALU.mult)
        pM, b_pM = self.bank()
        pD, b_pD = self.bank()

        def mmb(e):
            for h in range(4):
                e.matmul(pM[:, h * 128:(h + 1) * 128], lhsT=onesf[0:4, :], rhs=bd[:, h, 0:128], start=True, stop=True)
            return e.matmul(pD[:, 0:8], lhsT=onesf[0:4, :], rhs=bd[:, :, 128:130], start=True, stop=True)
        P.op("pe", mmb, [b_bd] + c, [b_pM, b_pD])
        P.v("act", "activation", [b_pD], [self.b_edm], out=self.edm[:], in_=pD[:, 0:8].rearrange("p (h c) -> p h c", h=4), func=AF.Exp)
        ET, b_ET = self.ET, self.b_ET
        for h in range(4):
            P.v("dve", "scalar_tensor_tensor", [b_pM, b_tm] + c, [b_ET], out=ET[:, h, :], in0=pM[:, h * 128:(h + 1) * 128],
                scalar=tm[:, 0, h:h + 1], in1=self.mnegT[:], op0=ALU.add, op1=ALU.add)
        P.v("act", "activation", [b_ET], [b_ET], out=ET[:], in_=ET[:], func=AF.Exp)
        mkT, mqT = self.mkT, self.mqT
        pqs = [self.bank(), self.bank()]

        def mmq(e):
            for h in range(4):
                r = slice((h % 2) * 64, (h % 2) * 64 + 64)
                i = e.matmul(pqs[h % 2][0][:, (h // 2) * 128:(h // 2 + 1) * 128], lhsT=mkT[r, h // 2, :], rhs=mqT[r, h // 2, :],
                             start=True, stop=True)
            return i
        P.op("pe", mmq, [self.b_mkT, self.b_mqT], [pqs[0][1], pqs[1][1]])
        for h in range(4):
            P.v("dve", "tensor_tensor", [pqs[h % 2][1], b_ET], [self.b_sT], out=self.sT[:, h, :],
                in0=pqs[h % 2][0][:, (h // 2) * 128:(h // 2 + 1) * 128], in1=ET[:, h, :], op=ALU.mult)
        mk4 = self.mkv[:, 0:256].rearrange("p (h d) -> p h d", h=4)
        mv4 = self.mkv[:, 256:512].rearrange("p (h d) -> p h d", h=4)
        s64 = (128, 4, 64)
        P.v("pool", "tensor_copy", [self.b_mkv], [self.b_vaug], out=self.vaug[:, :, 0:64], in_=mv4)
        P.v("dve", "tensor_tensor", [self.b_mkv, b_ex], [self.b_kwk0], out=self.kwk0[:], in0=mk4, in1=bc(ex[:, 3, :].unsqueeze(2), s64), op=ALU.mult)
        P.v("dve", "tensor_tensor", [self.b_mkv, b_ex], [self.b_kwk1], out=self.kwk1[:], in0=mk4, in1=bc(ex[:, 4, :].unsqueeze(2), s64), op=ALU.mult)
        psv, b_psv = self.bank()
        sT, vaug = self.sT, self.vaug

        def mms(e):
            for h in range(4):
                i = e.matmul(psv[:, h * 65:(h + 1) * 65], lhsT=sT[:, h, :], rhs=vaug[:, h, :], start=True, stop=True)
            return i
        P.op("pe", mms, [self.b_sT, self.b_vaug], [b_psv])
        P.v("act", "copy", [b_psv], [self.b_sv], out=self.sv[:], in_=psv[:, 0:260].rearrange("p (h d) -> p h d", h=4))
        C32, Cbf, rsb = self.C32, self.Cbf, self.rsb
        for cc in range(2):
            kwk = self.kwk0 if cc == 0 else self.kwk1
            b_kwk = self.b_kwk0 if cc == 0 else self.b_kwk1
            pAs = [self.bank(), self.bank()]

            def mma(e, pAs=pAs):
                for h in range(4):
                    r = slice((h % 2) * 64, (h % 2) * 64 + 64)
                    i = e.matmul(pAs[h % 2][0][:, (h // 2) * 65:(h // 2 + 1) * 65], lhsT=mqT[r, h // 2, :], rhs=Cbf[r, h // 2, :],
                                 start=True, stop=True)
                return i
            P.op("pe", mma, [self.b_mqT, self.b_Cbf], [pAs[0][1], pAs[1][1]])
            rows = slice(cc * 64, (cc + 1) * 64)
            for h in range(4):
                P.v("dve", "scalar_tensor_tensor", [pAs[h % 2][1], b_ex, self.b_sv], [self.b_rsb], out=rsb[rows, h, :],
                    in0=pAs[h % 2][0][rows, (h // 2) * 65:(h // 2 + 1) * 65], scalar=ex[rows, 1, h:h + 1], in1=self.sv[rows, h, :],
                    op0=ALU.mult, op1=ALU.add)
            pC, b_pC = self.bank()

            def mmc(e, pC=pC, kwk=kwk):
                for m in range(2):
                    i = e.matmul(pC[:, m * 130:(m + 1) * 130], lhsT=kwk[:, 2 * m:2 * m + 2, :],
                                 rhs=vaug[:, 2 * m:2 * m + 2, :], start=True, stop=True)
                return i
            P.op("pe", mmc, [b_kwk, self.b_vaug], [b_pC])
            for m in range(2):
                for hh in range(2):
                    r = slice(hh * 64, hh * 64 + 64)
                    h = 2 * m + hh
                    P.v("dve", "scalar_tensor_tensor", [self.b_C32, self.b_edm, b_pC], [self.b_C32], out=C32[r, m, :], in0=C32[r, m, :],
                        scalar=self.edm[r, h, cc:cc + 1], in1=pC[r, m * 130 + hh * 65:m * 130 + hh * 65 + 65], op0=ALU.mult, op1=ALU.add)
            P.v("act", "copy", [self.b_C32], [self.b_Cbf], out=Cbf[:], in_=C32[:])
        st, b_st = self.st1[1]
        P.v("dve", "tensor_scalar", [self.b_rsb], [b_ex], out=ex[:, 5, :], in0=rsb[:, :, 64], scalar1=-1.0, scalar2=None, op0=ALU.mult)
        P.v("dve", "tensor_tensor", [self.b_rsb, b_ex], [b_ex], out=ex[:, 5, :], in0=rsb[:, :, 64], in1=ex[:, 5, :], op=ALU.max)
        P.v("dve", "tensor_tensor", [b_ex], [b_st], out=st[:, 0:4], in0=ex[:, 5, :], in1=ex[:, 2, :], op=ALU.max)
        P.v("dve", "reciprocal", [b_st], [b_st], out=st[:, 0:4], in_=st[:, 0:4])
        P.v("dve", "tensor_tensor", [self.b_rsb, b_st], [self.b_hout], out=self.hout[:], in0=rsb[:, :, 0:64],
            in1=bc(st[:, 0:4].unsqueeze(2), s64), op=ALU.mult)
        self.out_norm(self.hout, self.b_hout, 64, self.mon, self.mos, self.b_mos, 768)

    def dsa(self, ti):
        P = self.P
        c = [self.b_c]
        lp = self.b_lp
        tb, b_tb = self.tb, self.b_tb
        qt = ti
        S = 128 * (qt + 1)
        T0 = qt * 128
        identb, identf = self.identb, self.identf
        st, b_st = self.st1[0]
        self.rstd(tb[:, 8:264], [b_tb], 256, st, b_st, 2)
        P.v("act", "activation", [b_tb, b_st], [self.b_cqn], out=self.cqn[:], in_=tb[:, 8:264], func=AF.Copy, scale=st[:, 2:3])
        p1, b_p1 = self.bank()
        p1b = p1[:].bitcast(BF16)
        cqn = self.cqn

        def tr1(e):
            for k in range(2):
                i = e.transpose(out=p1b[:, k * 128:(k + 1) * 128], in_=cqn[:, k * 128:(k + 1) * 128], identity=identb[:])
            return i
        P.op("pe", tr1, [self.b_cqn] + c, [b_p1])
        P.v("dve", "tensor_tensor", [b_p1, lp], [self.b_cqT], out=self.cqT[:], in0=p1b[:, 0:256].rearrange("p (k t) -> p k t", k=2),
            in1=bc(self.gQ[:].unsqueeze(2), (128, 2, 128)), op=ALU.mult)
        self.rstd(tb[:, 264:392], [b_tb], 128, st, b_st, 3)
        P.v("dve", "scalar_tensor_tensor", [b_tb, b_st, lp], [self.b_ckvK], out=self.ckvK[:, qt, :], in0=tb[:, 264:392],
            scalar=st[:, 3:4], in1=self.gKVb[:], op0=ALU.mult, op1=ALU.mult)
        p2, b_p2 = self.bank()
        p2b = p2[:].bitcast(BF16)
        ckvK = self.ckvK
        P.op("pe", lambda e: e.transpose(out=p2b[:, 0:128], in_=ckvK[:, qt, :], identity=identb[:]), [self.b_ckvK] + c, [b_p2])
        P.v("act", "copy", [b_p2], [self.b_ckvT], out=self.ckvT[:, T0:T0 + 128], in_=p2b[:, 0:128])
        p3, b_p3 = self.bank()
        Wik4, hT = self.Wik4, self.hT

        def mmk(e):
            for k in range(KC):
                i = e.matmul(p3[:, 0:128], lhsT=Wik4[:, k, :], rhs=hT[:, k, 0:128], start=(k == 0), stop=(k == KC - 1))
            return i
        P.op("pe", mmk, [self.b_hT, lp], [b_p3])
        P.v("act", "copy", [b_p3], [self.b_kidx], out=self.kidx[:, T0:T0 + 128], in_=p3[:, 0:128])
        p4_, b_p4 = self.bank()
        Wuq, Wqi, cqT = self.Wuq, self.Wqi, self.cqT

        def mmq(e):
            for m in range(2):
                for k in range(2):
                    i = e.matmul(p4_[:, m * 128:(m + 1) * 128], lhsT=Wuq[:, k, m * 128:(m + 1) * 128],
                                 rhs=cqT[:, k, :], start=(k == 0), stop=(k == 1))
            return i
        P.op("pe", mmq, [self.b_cqT, lp], [b_p4])
        P.v("act", "copy", [b_p4], [self.b_dqT], out=self.dqT[:], in_=p4_[:, 0:256].rearrange("p (m t) -> p m t", m=2))
        p6, b_p6 = self.bank()

        def mmqi(e):
            for m in range(3):
                w = 96 if m < 2 else 64
                for k in range(2):
                    i = e.matmul(p6[0:w, m * 128:(m + 1) * 128], lhsT=Wqi[:, k, m * 96:m * 96 + w],
                                 rhs=cqT[:, k, :], start=(k == 0), stop=(k == 1))
            return i
        P.op("pe", mmqi, [self.b_cqT, lp], [b_p6])
        P.v("dve", "tensor_copy", [b_p6], [self.b_qidxT], out=self.qidxT[0:96, 0:2, :], in_=p6[0:96, 0:256].rearrange("p (m t) -> p m t", m=2))
        P.v("dve", "tensor_copy", [b_p6], [self.b_qidxT], out=self.qidxT[0:64, 2, :], in_=p6[0:64, 256:384])
        wukT, dqT = self.wukT, self.dqT
        p5s = [self.bank(), self.bank()]

        def mml(e):
            for h in range(4):
                r = slice((h % 2) * 64, (h % 2) * 64 + 64)
                i = e.matmul(p5s[h % 2][0][:, (h // 2) * 128:(h // 2 + 1) * 128], lhsT=wukT[r, h // 2, :], rhs=dqT[r, h // 2, :],
                             start=True, stop=True)
            return i
        P.op("pe", mml, [self.b_dqT, lp], [p5s[0][1], p5s[1][1]])
        for h in range(4):
            P.v("act", "copy", [p5s[h % 2][1]], [self.b_qlatT], out=self.qlatT[:, h, :],
                in_=p5s[h % 2][0][:, (h // 2) * 128:(h // 2 + 1) * 128])
        P.v("dve", "tensor_scalar", [b_tb], [b_st], out=self.rs4[:, 0:8], in0=tb[:, 424:432], scalar1=0.0625, scalar2=None, op0=ALU.mult)
        P.v("dve", "tensor_tensor", [b_st] + c, [self.b_Dw], out=self.Dw[:], in0=bc(identf[:].unsqueeze(1), (128, 8, 128)),
            in1=bc(self.rs4[:, 0:8].unsqueeze(2), (128, 8, 128)), op=ALU.mult)
        score, b_score = self.score, self.b_score
        qidxT, kidx, Dw = self.qidxT, self.kidx, self.Dw
        nblk = (S + 511) // 512
        for kb in range(nblk):
            k0 = kb * 512
            n = min(512, S - k0)
            pacc, b_pacc = self.bank()
            phs = {}

            def idx_mm(h):
                ph, b_ph = self.bank((pacc,))
                r = slice((h % 3) * 32, (h % 3) * 32 + 32)
                P.op("pe", lambda e, ph=ph, r=r, h=h, k0=k0, n=n: e.matmul(
                    ph[:, 0:n], lhsT=qidxT[r, h // 3, :], rhs=kidx[r, k0:k0 + n], start=True, stop=True),
                    [self.b_qidxT, self.b_kidx], [b_ph])
                phs[h] = (ph, b_ph)
            idx_mm(0)
            for h in range(8):
                if h < 7:
                    idx_mm(h + 1)
                ph, b_ph = phs[h]
                rr, b_rr = self.rr[h % 2]
                if h % 2 == 0:
                    P.v("act", "activation", [b_ph], [b_rr], out=rr[:, 0:n], in_=ph[:, 0:n], func=AF.Relu)
                else:
                    P.v("dve", "tensor_scalar", [b_ph], [b_rr], out=rr[:, 0:n], in0=ph[:, 0:n], scalar1=0.0, scalar2=None, op0=ALU.max)
                P.op("pe", lambda e, pacc=pacc, rr=rr, h=h, n=n: e.matmul(
                    pacc[:, 0:n], lhsT=Dw[:, h, :], rhs=rr[:, 0:n], start=(h == 0), stop=(h == 7)),
                    [self.b_Dw, b_rr], [b_pacc])
            P.v("act", "copy", [b_pacc], [b_score], out=score[:, k0:k0 + n], in_=pacc[:, 0:n])
        bs, b_bs = self.bs, self.b_bs
        maskb, b_maskb = self.maskb, self.b_maskb
        if S > self.ksel:
            P.v("dve", "tensor_reduce", [b_score], [b_bs], out=bs[:, 0:1], in_=score[:, 0:S], axis=AX.X, op=ALU.min)
            P.v("dve", "tensor_reduce", [b_score], [b_bs], out=bs[:, 1:2], in_=score[:, 0:S], axis=AX.X, op=ALU.max)
            P.v("dve", "tensor_tensor", [b_score] + c, [b_score], out=score[:, T0:S], in0=score[:, T0:S], in1=self.caus[:], op=ALU.add)
            P.v("dve", "tensor_tensor", [b_bs], [b_bs], out=bs[:, 2:3], in0=bs[:, 1:2], in1=bs[:, 0:1], op=ALU.subtract)
            P.v("dve", "tensor_scalar", [b_bs] + c, [self.b_wk], out=self.wk[:], in0=self.pows[:], scalar1=bs[:, 2:3], scalar2=None, op0=ALU.mult)
            jb = self.Pm
            P.v("dve", "tensor_tensor", [b_bs, self.b_wk], [b_bs], out=bs[:, 3:4], in0=bs[:, 0:1], in1=self.wk[:, 0:1], op=ALU.add)
            for k in range(NBIS):
                P.v("dve", "tensor_scalar", [b_score, b_bs], [self.b_Pm, b_bs], out=jb[:, 0:S], in0=score[:, 0:S], scalar1=bs[:, 3:4],
                    scalar2=None, op0=ALU.is_ge, op1=ALU.add, accum_out=bs[:, 4:5])
                P.v("dve", "tensor_scalar", [b_bs], [b_bs], out=bs[:, 5:6], in0=bs[:, 4:5], scalar1=float(self.ksel) - 0.5, scalar2=0.5,
                    op0=ALU.is_ge, op1=ALU.subtract)
                P.v("dve", "scalar_tensor_tensor", [b_bs, self.b_wk], [b_bs], out=bs[:, 3:4], in0=bs[:, 5:6], scalar=self.wk[:, k:k + 1],
                    in1=bs[:, 3:4], op0=ALU.mult, op1=ALU.add)
            P.v("dve", "tensor_tensor", [b_bs, self.b_wk], [b_bs], out=bs[:, 0:1], in0=bs[:, 3:4], in1=self.wk[:, NBIS:NBIS + 1], op=ALU.subtract)
            P.v("dve", "tensor_scalar", [b_score, b_bs], [b_maskb], out=maskb[:, 0:S], in0=score[:, 0:S], scalar1=bs[:, 0:1], scalar2=NEG,
                op0=ALU.is_lt, op1=ALU.mult)
        else:
            if T0 > 0:
                P.v("pool", "memset", [], [b_maskb], ap=maskb[:, 0:T0], constant=0.0)
            P.v("pool", "tensor_copy", c, [b_maskb], out=maskb[:, T0:S], in_=self.causb[:])
        qlatT, ckvT, Pm, b_Pm = self.qlatT, self.ckvT, self.Pm, self.b_Pm
        PT, b_PT = self.PT[0]
        pol, b_pol = self.bank()
        nb = S // 128
        rs4, b_rs4 = self.rs4, self.b_rs4
        for h in range(4):
            for kb in range(nblk):
                k0 = kb * 512
                n = min(512, S - k0)
                pl, b_pl = self.bank((pol,))
                def mmlg(e, pl=pl, h=h, k0=k0, n=n):
                    e.matmul(pl[:, 0:n], lhsT=qlatT[:, h, :], rhs=ckvT[:, k0:k0 + n], start=True, stop=False)
                    return e.matmul(pl[:, 0:n], lhsT=identb[:], rhs=maskb[:, k0:k0 + n], start=False, stop=True)
                P.op("pe", mmlg, [self.b_qlatT, self.b_ckvT, b_maskb] + c, [b_pl])
                P.v("act", "copy", [b_pl], [b_score], out=score[:, k0:k0 + n], in_=pl[:, 0:n])
            P.v("dve", "tensor_reduce", [b_score], [b_bs], out=bs[:, 8:9], in_=score[:, 0:S], axis=AX.X, op=ALU.max, negate=True)
            P.v("act", "activation", [b_score, b_bs], [b_Pm, b_rs4], out=Pm[:, 0:S], in_=score[:, 0:S], func=AF.Exp, bias=bs[:, 8:9],
                accum_out=rs4[:, 8 + h - 8 + 0:8 + h - 8 + 1] if False else self.bs[:, 10 + h:11 + h])
            groups = [(g0, min(nb, g0 + 8)) for g0 in range(0, nb, 8)]
            tps = {}

            def issue_tr(gi):
                g0, g1 = groups[gi]
                ptp, b_ptp = self.bank((pol,))
                ptb = ptp[:].bitcast(BF16)

                def trp(e, g0=g0, g1=g1, ptb=ptb):
                    for b_ in range(g0, g1):
                        i = e.transpose(out=ptb[:, (b_ - g0) * 128:(b_ - g0 + 1) * 128], in_=Pm[:, b_ * 128:(b_ + 1) * 128], identity=identb[:])
                    return i
                P.op("pe", trp, [b_Pm] + c, [b_ptp])
                tps[gi] = (ptb, b_ptp)
            issue_tr(0)
            for gi, (g0, g1) in enumerate(groups):
                if gi + 1 < len(groups):
                    issue_tr(gi + 1)
                ptb, b_ptp = tps[gi]
                ng = g1 - g0
                P.v("act", "copy", [b_ptp], [b_PT], out=PT[:, 0:ng, :], in_=ptb[:, 0:ng * 128].rearrange("p (b i) -> p b i", b=ng))

                def mmpv(e, g0=g0, g1=g1, h=h):
                    for b_ in range(g0, g1):
                        i = e.matmul(pol[:, h * 128:(h + 1) * 128], lhsT=ckvK[:, b_, :], rhs=PT[:, b_ - g0, :],
                                     start=(b_ == 0), stop=(b_ == nb - 1))
                    return i
                P.op("pe", mmpv, [self.b_ckvK, b_PT], [b_pol])
        P.v("act", "copy", [b_pol], [self.b_olatT], out=self.olatT[:], in_=pol[:].rearrange("p (h i) -> p h i", h=4))
        py, b_py = self.bank()
        olatT, wuv = self.olatT, self.wuv

        def mmy(e):
            for h in range(4):
                i = e.matmul(py[:, h * 64:(h + 1) * 64], lhsT=olatT[:, h, :], rhs=wuv[:, h, :], start=True, stop=True)
            return i
        P.op("pe", mmy, [self.b_olatT, lp], [b_py])
        P.v("dve", "reciprocal", [b_bs], [b_bs], out=bs[:, 10:14], in_=bs[:, 10:14])
        P.v("dve", "tensor_tensor", [b_py, b_bs], [self.b_mix], out=self.mix[:, 512:768].rearrange("p (h d) -> p h d", h=4),
            in0=py[:, 0:256].rearrange("p (h d) -> p h d", h=4), in1=bc(bs[:, 10:14].unsqueeze(2), (128, 4, 64)), op=ALU.mult)


PARAM_NAMES = ["attn_norm", "w_in", "gdn_conv", "gdn_a_log", "gdn_dt_bias", "gdn_out_norm", "dsa_q_norm",
               "dsa_kv_norm", "dsa_w_uq", "dsa_w_qidx", "dsa_w_uk", "dsa_w_uv", "mlstm_i_bias",
               "mlstm_f_bias", "mlstm_out_norm", "w_out", "ffn_norm", "w_gate", "w_up", "w_down"]


FUSED = True
_CACHE = {}


def _builder(T, L, final, ksel):
    key = (T, L, final, ksel)
    if key not in _CACHE:
        _CACHE[key] = Builder(T, list(range(L)), final, ksel)
    return _CACHE[key]


def kernel(**inputs):
    x = np.ascontiguousarray(inputs["x"], dtype=np.float32)
    Bn, T, _ = x.shape
    depth = inputs["w_in"].shape[0]
    ksel = min(256, T // 4)
    prm = {k: np.ascontiguousarray(inputs[k], dtype=np.float32) for k in PARAM_NAMES}
    fin = np.ascontiguousarray(inputs["final_norm"], dtype=np.float32)
    cores = list(range(Bn))
    if FUSED:
        b = _builder(T, depth, True, ksel)
        in_maps = [dict(prm, final_norm=fin, x=x[i]) for i in range(Bn)]
        res = run_bass_kernel_spmd(b.nc, in_maps, core_ids=cores)
        return np.stack([np.asarray(r["y"]) for r in res.results], axis=0).astype(np.float32)
    cur = [x[i] for i in range(Bn)]
    for l in range(depth):
        last = l == depth - 1
        b = _builder(T, 1, last, ksel)
        lw = {k: np.ascontiguousarray(v[l:l + 1]) for k, v in prm.items()}
        in_maps = [dict(lw, final_norm=fin, x=np.ascontiguousarray(cur[i])) for i in range(Bn)]
        res = run_bass_kernel_spmd(b.nc, in_maps, core_ids=cores)
        cur = [np.asarray(r["y"]) for r in res.results]
    return np.stack(cur, axis=0).astype(np.float32)
```

```python
from contextlib import ExitStack
import os
import threading
import numpy as np
import concourse.bass as bass
import concourse.mybir as mybir
from concourse.bass_utils import run_bass_kernel_spmd

F32 = mybir.dt.float32
BF16 = mybir.dt.bfloat16
ALU = mybir.AluOpType
AF = mybir.ActivationFunctionType
AX = mybir.AxisListType

D = 1024
KC = 8
FF = 2816
NJ = 22
IN_DIM = 3512
GQ, GK, GV, GZ, GA, GB_, CQ, CKV, IK, IW, MQ, MK, MV, MO, MI, MF = (
    0, 512, 1024, 1536, 2048, 2052, 2056, 2312, 2440, 2472, 2480, 2736, 2992, 3248, 3504, 3508)
EPS = 1e-6
NEG = -30000.0
NBIS = 12


class Buf:
    __slots__ = ("name", "w", "r", "sem", "cnt", "al", "psum")

    def __init__(self, name):
        self.name = name
        self.w = None
        self.r = {}
        self.sem = None
        self.cnt = 0
        self.al = []
        self.psum = False


class Prog:
    ENGS = ("pe", "act", "dve", "pool", "sp")

    def __init__(self, nc, stack):
        self.nc = nc
        self.stack = stack
        self.stream = {e: [] for e in self.ENGS}
        self.cnt = {e: 0 for e in self.ENGS}
        self.waited = {e: {} for e in self.ENGS}
        self.semobj = {e: stack.enter_context(nc.semaphore("s_" + e)) for e in self.ENGS}
        self.nbuf = 0
        self.dsems = []
        self.ilv = None

    def buf(self, name):
        return Buf(name)

    def _need(self, eng, tok):
        if tok is None:
            return
        key, val = tok
        if key == eng:
            if eng == "pe":
                return
            if val < self.cnt[eng] - 1:
                return
        if self.waited[eng].get(key, 0) >= val:
            return
        self.waited[eng][key] = val
        self.stream[eng].append(("w", key, val))

    def _deps(self, eng, reads, writes):
        for b in reads:
            self._need(eng, b.w)
            if b.psum:
                for t in b.r.items():
                    if t[0] != eng:
                        self._need(eng, t)
        for b in writes:
            for x in [b] + b.al:
                self._need(eng, x.w)
                for t in x.r.items():
                    self._need(eng, t)

    def _mark(self, tok, reads, writes):
        for b in reads:
            b.r[tok[0]] = tok[1]
        for b in writes:
            b.w = tok
            b.r = {}

    def op(self, eng, fn, reads=(), writes=()):
        if self.ilv is not None:
            self.ilv()
        self._deps(eng, reads, writes)
        self.cnt[eng] += 1
        tok = (eng, self.cnt[eng])
        self.stream[eng].append(("o", fn, eng, 1))
        self._mark(tok, reads, writes)

    def v(self, eng, meth, reads, writes, **kw):
        self.op(eng, lambda e: getattr(e, meth)(**kw), reads, writes)

    def dma(self, q, out_ap, in_ap, reads, wbuf, **kw):
        self._deps(q, reads, (wbuf,))
        if wbuf.sem is None:
            wbuf.sem = {}
            wbuf.cnt = {}
        if q not in wbuf.sem:
            self.nbuf += 1
            key = ("d", self.nbuf)
            wbuf.sem[q] = key
            wbuf.cnt[q] = 0
            self.dsems.append((wbuf, q))
            self.semobj[key] = self.stack.enter_context(self.nc.semaphore("d%d" % self.nbuf))
        wbuf.cnt[q] += 16
        key = wbuf.sem[q]
        tok = (key, wbuf.cnt[q])
        self.stream[q].append(("o", lambda e: e.dma_start(out=out_ap, in_=in_ap, **kw), key, 16))
        self._mark(tok, reads, (wbuf,))

    def barrier(self):
        toks = [(e, self.cnt[e]) for e in self.ENGS if self.cnt[e] > 0]
        toks += [(b.sem[q], b.cnt[q]) for (b, q) in self.dsems]
        for e in self.ENGS:
            for t in toks:
                if t[0] != e:
                    self._need(e, t)

    def final_wait(self, eng, bufs):
        for b in bufs:
            self._need(eng, b.w)

    def emit(self):
        nc = self.nc
        with nc.Block() as block:
            def mk(ename):
                def body(e):
                    for it in self.stream[ename]:
                        if it[0] == "w":
                            e.wait_ge(self.semobj[it[1]], it[2])
                        else:
                            it[1](e).then_inc(self.semobj[it[2]], it[3])
                return body
            block.tensor(mk("pe"))
            block.scalar(mk("act"))
            block.vector(mk("dve"))
            block.gpsimd(mk("pool"))
            block.sync(mk("sp"))


def bc(ap, shape):
    return ap.to_broadcast(list(shape))


class Builder:
    def __init__(self, T, layers, final, ksel, dbg=None, first=True):
        self.T, self.layers, self.final, self.ksel, self.dbg = T, layers, final, ksel, dbg
        self.NT = T // 128
        nc = self.nc = bass.Bass("TRN2", target_bir_lowering=False)
        L = len(layers)
        di = lambda n, s: nc.dram_tensor(n, list(s), F32, kind="ExternalInput").ap()
        self.x_in = di("x", (T, D))
        self.prm = dict(
            attn_norm=di("attn_norm", (L, D)), w_in=di("w_in", (L, D, IN_DIM)),
            gdn_conv=di("gdn_conv", (L, 4, 1536)), gdn_a_log=di("gdn_a_log", (L, 4)),
            gdn_dt_bias=di("gdn_dt_bias", (L, 4)), gdn_out_norm=di("gdn_out_norm", (L, 128)),
            dsa_q_norm=di("dsa_q_norm", (L, 256)), dsa_kv_norm=di("dsa_kv_norm", (L, 128)),
            dsa_w_uq=di("dsa_w_uq", (L, 256, 256)), dsa_w_qidx=di("dsa_w_qidx", (L, 256, 256)),
            dsa_w_uk=di("dsa_w_uk", (L, 4, 128, 64)), dsa_w_uv=di("dsa_w_uv", (L, 4, 128, 64)),
            mlstm_i_bias=di("mlstm_i_bias", (L, 4)), mlstm_f_bias=di("mlstm_f_bias", (L, 4)),
            mlstm_out_norm=di("mlstm_out_norm", (L, 64)), w_out=di("w_out", (L, D, D)),
            ffn_norm=di("ffn_norm", (L, D)), w_gate=di("w_gate", (L, D, FF)),
            w_up=di("w_up", (L, D, FF)), w_down=di("w_down", (L, FF, D)),
            final_norm=di("final_norm", (D,)))
        self.y_out = nc.dram_tensor("y", [T, D], F32, kind="ExternalOutput").ap()
        self.xres = nc.dram_tensor("xres", [T, D], F32).ap()
        if dbg:
            self.dbg_out = nc.dram_tensor("dbg", [T, D], F32, kind="ExternalOutput").ap()
        with ExitStack() as st:
            self.st = st
            self.P = Prog(nc, st)
            self.pre_alloc()
            self.alloc()
            self.consts()
            for li in range(L):
                self.layer(li)
            self.P.final_wait("sp", [self.b_y] + ([self.b_dbg] if dbg else []))
            self.P.final_wait("pool", [self.b_y])
            self.P.emit()

    def sb(self, name, shape, dt=F32):
        t = self.nc.alloc_sbuf_tensor(name, list(shape), dt)
        b = self.P.buf(name)
        return t, b

    def frame_alloc(self, name, shape, dt=F32, part=128):
        n = 1
        for d in shape[1:]:
            n *= d
        units = n * (2 if dt == F32 else 1)
        units += units % 2
        self.fo = (self.fo + 31) // 32 * 32
        assert self.fo + units <= self.FN, (name, self.fo, units, self.FN)
        ap = self.F[:, self.fo:self.fo + units]
        self.fo += units
        self.fmax = max(self.fmax, self.fo)
        if dt == F32:
            ap = ap.bitcast(F32)
        ap = ap[:, 0:n]
        if len(shape) == 3:
            ap = ap.rearrange("p (a b) -> p a b", a=shape[1])
        if shape[0] != 128:
            ap = ap[0:shape[0]]
        return ap, self.P.buf(name)

    def pre_alloc(self):
        self.xt = [self.sb("xt%d" % i, (128, D)) for i in range(2)]
        self.xti = 0
        self.junk, self.b_junk = self.sb("junk", (128, 1536))
        self.st1 = [self.sb("st1_%d" % i, (128, 4)) for i in range(2)]
        self.xn, self.b_xn = self.sb("xn", (128, D), BF16)
        self.hT, self.b_hT = self.sb("hT", (128, KC, 128), BF16)
        self.gA, _ = self.sb("gA", (128, KC))
        self.gF, _ = self.sb("gF", (128, KC))

    def alloc(self):
        P, nc, T = self.P, self.nc, self.T
        self.FN = 94000
        self.F = nc.alloc_sbuf_tensor("F", [128, self.FN], BF16)
        self.fo = 0
        self.fmax = 0
        fa = self.frame_alloc
        self.Wg, self.b_Wg = fa("Wg", (128, KC, FF), BF16)
        self.Wu, self.b_Wu = fa("Wu", (128, KC, FF), BF16)
        self.Wd, self.b_Wd = fa("Wd", (128, NJ, D), BF16)
        self.actT, self.b_actT = fa("actT", (128, NJ, 128), BF16)
        self.sg, self.b_sg = fa("sg", (128, 128))
        self.gN, self.b_gN = fa("gN", (128, D))
        self.fo = 0
        self.Wi, self.b_Wi = fa("Wi", (128, KC, IN_DIM), BF16)
        self.Wo, self.b_Wo = fa("Wo", (128, KC, D), BF16)
        self.kidx, self.b_kidx = fa("kidx", (128, T), BF16)
        self.ckvT, self.b_ckvT = fa("ckvT", (128, T), BF16)
        self.ckvK, self.b_ckvK = fa("ckvK", (128, T // 128, 128), BF16)
        arena0 = self.fo
        self.score, self.b_score = fa("score", (128, T))
        self.Pm, self.b_Pm = fa("Pm", (128, T), BF16)
        self.maskb, self.b_maskb = fa("maskb", (128, T), BF16)
        arena1 = max(self.fo, arena0 + 16384)
        big = [self.b_score, self.b_Pm, self.b_maskb]
        self.fo = arena0
        f4 = (128, 4, 128)
        ov = []

        def fo_(name, shape, dt=F32):
            ap, b = fa(name, shape, dt)
            ov.append(b)
            return ap, b
        self.cbuf, self.b_cbuf = fo_("cbuf", (128, 12, 131))
        self.cact, self.b_cact = fo_("cact", (128, 12, 128))
        self.rn, self.b_rn = fo_("rn", (128, 8, 128))
        self.Qm, self.b_Qm = fo_("Qm", f4)
        self.G4, self.b_G4 = fo_("G4", f4)
        self.dcT, self.b_dcT = fo_("dcT", f4)
        self.egcB, self.b_egcB = fo_("egcB", f4)
        self.qT, self.b_qT = fo_("qT", f4, BF16)
        self.kT, self.b_kT = fo_("kT", f4, BF16)
        self.qdT, self.b_qdT = fo_("qdT", f4, BF16)
        self.kbg, self.b_kbg = fo_("kbg", f4, BF16)
        self.kd0, self.b_kd0 = fo_("kd0", f4, BF16)
        self.kd1, self.b_kd1 = fo_("kd1", f4, BF16)
        self.vb, self.b_vb = fo_("vb", f4, BF16)
        assert self.fo <= arena1, (self.fo, arena1)
        for b_ in ov:
            b_.al = list(big)
        for b_ in big:
            b_.al = list(ov)
        self.fo = arena1
        self.mixer_alloc()
        self.ps = []
        for i in range(8):
            t = self.st.enter_context(nc.psum_tensor("ps%d" % i, [128, 512], F32))
            pb_ = P.buf("ps%d" % i)
            pb_.psum = True
            self.ps.append((t, pb_))
        self.psi = 0
        self._tl = threading.local()
        self.b_y = P.buf("y")
        self.b_dbg = P.buf("dbg")
        self.b_xres = P.buf("xres")
        self.b_prm = P.buf("prm")

    def bank(self, excl=()):
        grp = getattr(self._tl, "grp", None)
        if grp is not None:
            lst, st = grp
            t, b = self.ps[lst[st[0] % len(lst)]]
            st[0] += 1
            return t, b
        while True:
            t, b = self.ps[self.psi % 8]
            self.psi += 1
            if not any(t is x for x in excl):
                return t, b

    def interleave(self, fa, fb, banks_a, banks_b):
        P = self.P
        sa, sb_ = threading.Semaphore(0), threading.Semaphore(0)
        done = {"a": False, "b": False}
        err = []

        def run(me, other, f, banks, s_me, s_other):
            self._tl.grp = (banks, [0])
            self._tl.me = me
            s_me.acquire()
            try:
                f()
            except BaseException as ex:
                err.append(ex)
            done[me] = True
            s_other.release()

        def hook():
            me = getattr(self._tl, "me", None)
            if me is None:
                return
            other = "b" if me == "a" else "a"
            if not done[other]:
                (sb_ if me == "a" else sa).release()
                (sa if me == "a" else sb_).acquire()
        ta = threading.Thread(target=run, args=("a", "b", fa, banks_a, sa, sb_))
        tb_ = threading.Thread(target=run, args=("b", "a", fb, banks_b, sb_, sa))
        P.ilv = hook
        ta.start()
        tb_.start()
        sa.release()
        ta.join()
        tb_.join()
        P.ilv = None
        if err:
            raise err[0]

    def consts(self):
        P = self.P
        sel = lambda out, pat, cmp, fill, base, cm, bufs: P.v(
            "pool", "affine_select", bufs, bufs, out=out, in_=out, pattern=pat, compare_op=cmp,
            fill=fill, base=base, channel_multiplier=cm)

        def chunkify(t, b, fill):
            sel(t[:, 0:64], [[0, 64]], ALU.is_ge, fill, 63, -1, [b])
            sel(t[:, 64:128], [[0, 64]], ALU.is_ge, fill, -64, 1, [b])
        self.identf, self.b_c = self.sb("identf", (128, 128))
        bcst = [self.b_c]
        P.v("pool", "memset", [], bcst, ap=self.identf[:], constant=1.0)
        sel(self.identf[:], [[-1, 128]], ALU.is_equal, 0.0, 0, 1, bcst)
        self.identb, _ = self.sb("identb", (128, 128), BF16)
        P.v("pool", "tensor_copy", bcst, bcst, out=self.identb[:], in_=self.identf[:])
        self.onesf, _ = self.sb("onesf", (128, 128))
        P.v("pool", "memset", [], bcst, ap=self.onesf[:], constant=1.0)
        self.onesb, _ = self.sb("onesb", (128, 128), BF16)
        P.v("pool", "memset", [], bcst, ap=self.onesb[:], constant=1.0)
        self.ones128b, _ = self.sb("ones128b", (128, 128), BF16)
        P.v("pool", "memset", [], bcst, ap=self.ones128b[:], constant=128.0)
        self.utriC, _ = self.sb("utriC", (128, 128))
        P.v("pool", "memset", [], bcst, ap=self.utriC[:], constant=1.0)
        sel(self.utriC[:], [[1, 128]], ALU.is_ge, 0.0, 0, -1, bcst)
        chunkify(self.utriC, self.b_c, 0.0)
        self.bonesC, _ = self.sb("bonesC", (128, 128))
        P.v("pool", "memset", [], bcst, ap=self.bonesC[:], constant=1.0)
        chunkify(self.bonesC, self.b_c, 0.0)
        self.mnegT, _ = self.sb("mnegT", (128, 128))
        P.v("pool", "memset", [], bcst, ap=self.mnegT[:], constant=0.0)
        sel(self.mnegT[:], [[1, 128]], ALU.is_ge, NEG, 0, -1, bcst)
        chunkify(self.mnegT, self.b_c, NEG)
        self.mposL, _ = self.sb("mposL", (128, 128))
        P.v("pool", "memset", [], bcst, ap=self.mposL[:], constant=0.0)
        sel(self.mposL[:], [[-1, 128]], ALU.is_ge, -NEG, -1, 1, bcst)
        chunkify(self.mposL, self.b_c, -NEG)
        self.caus, _ = self.sb("caus", (128, 128))
        P.v("pool", "memset", [], bcst, ap=self.caus[:], constant=0.0)
        sel(self.caus[:], [[-1, 128]], ALU.is_ge, -1e30, 0, 1, bcst)
        self.causb, _ = self.sb("causb", (128, 128), BF16)
        P.v("pool", "memset", [], bcst, ap=self.causb[:], constant=0.0)
        sel(self.causb[:], [[-1, 128]], ALU.is_ge, NEG, 0, 1, bcst)
        self.rowm, _ = self.sb("rowm", (128, 2))
        P.v("pool", "memset", [], bcst, ap=self.rowm[:], constant=1.0)
        sel(self.rowm[:, 0:1], [[0, 1]], ALU.is_ge, 0.0, 63, -1, bcst)
        sel(self.rowm[:, 1:2], [[0, 1]], ALU.is_ge, 0.0, -64, 1, bcst)
        self.epsc, _ = self.sb("epsc", (128, 2))
        P.v("pool", "memset", [], bcst, ap=self.epsc[:, 0:1], constant=EPS)
        P.v("pool", "memset", [], bcst, ap=self.epsc[:, 1:2], constant=128.0 * EPS)
        self.pows, _ = self.sb("pows", (128, NBIS + 1))
        for k in range(NBIS + 1):
            P.v("pool", "memset", [], bcst, ap=self.pows[:, k:k + 1], constant=float(2.0 ** -(k + 1)))

    def rstd(self, src_ap, src_bufs, n, st, b_st, col, eps_col=0, post_scale=None):
        P = self.P
        P.v("act", "activation", src_bufs, [self.b_junk, b_st], out=self.junk[:, 0:n], in_=src_ap,
            func=AF.Square, accum_out=st[:, col:col + 1])
        P.v("act", "activation", [b_st, self.b_c], [b_st], out=st[:, col:col + 1], in_=st[:, col:col + 1],
            func=AF.Ln, scale=1.0 / n, bias=self.epsc[:, eps_col:eps_col + 1])
        P.v("act", "activation", [b_st], [b_st], out=st[:, col:col + 1], in_=st[:, col:col + 1],
            func=AF.Exp, scale=-0.5)

    def norm_T(self, xt, b_xt, gain, ncols_off, width=128):
        P = self.P
        st, b_st = self.st1[self.psi % 2]
        self.rstd(xt[:], [b_xt], D, st, b_st, 0)
        P.v("act", "activation", [b_xt, b_st], [self.b_xn], out=self.xn[:], in_=xt[:], func=AF.Copy,
            scale=st[:, 0:1])
        pt, b_pt = self.bank()
        ptb = pt[:].bitcast(BF16)
        xn, identb = self.xn, self.identb

        def tr(e):
            for k in range(KC):
                i = e.transpose(out=ptb[:, k * 128:(k + 1) * 128], in_=xn[:, k * 128:(k + 1) * 128],
                                identity=identb[:])
            return i
        P.op("pe", tr, [self.b_xn, self.b_c], [b_pt])
        P.v("dve", "tensor_tensor", [b_pt, self.b_prm], [self.b_hT],
            out=self.hT[:, :, ncols_off:ncols_off + 128],
            in0=ptb.rearrange("p (k t) -> p k t", k=KC),
            in1=bc(gain[:].unsqueeze(2), (128, KC, 128)), op=ALU.mult)

    def load_x(self, li, ti):
        P = self.P
        xt, b_xt = self.xt[self.xti % 2]
        self.xti += 1
        if li == 0:
            P.dma("sp", xt[:], self.x_in[ti * 128:(ti + 1) * 128, :], [], b_xt)
        else:
            P.dma("sp", xt[:], self.xres[ti * 128:(ti + 1) * 128, :], [self.b_xres], b_xt)
        return xt, b_xt

    def layer(self, li):
        P, prm = self.P, self.prm
        l = li
        P.barrier()
        P.dma("pool", self.Wi, prm["w_in"][l].rearrange("(k p) n -> p k n", p=128), [], self.b_Wi)
        P.dma("pool", self.Wo, prm["w_out"][l].rearrange("(k p) n -> p k n", p=128), [], self.b_Wo)
        P.dma("sp", self.gA[:], prm["attn_norm"][l].rearrange("(k p) -> p k", p=128), [], self.b_prm,
              allow_slow_non_contiguous=True)
        P.dma("sp", self.gF[:], prm["ffn_norm"][l].rearrange("(k p) -> p k", p=128), [], self.b_prm,
              allow_slow_non_contiguous=True)
        self.mixer_setup(li)
        for ti in range(self.NT):
            self.mixer_tile(li, ti)
        P.barrier()
        P.dma("pool", self.Wg, prm["w_gate"][l].rearrange("(k p) n -> p k n", p=128), [], self.b_Wg)
        P.dma("pool", self.Wu, prm["w_up"][l].rearrange("(k p) n -> p k n", p=128), [], self.b_Wu)
        P.dma("pool", self.Wd, prm["w_down"][l].rearrange("(k p) n -> p k n", p=128), [], self.b_Wd)
        last = self.final and li == len(self.layers) - 1
        if last:
            P.dma("sp", self.gN[:], prm["final_norm"].partition_broadcast(128), [], self.b_gN)
        for ti in range(self.NT):
            self.ffn_tile(li, ti, last)

    def ffn_tile(self, li, ti, last):
        P = self.P
        xt, b_xt = self.xt[self.xti % 2]
        self.xti += 1
        P.dma("sp", xt[:], self.xres[ti * 128:(ti + 1) * 128, :], [self.b_xres], b_xt)
        self.norm_T(xt, b_xt, self.gF, 0)
        hT, Wg, Wu, Wd, actT = self.hT, self.Wg, self.Wu, self.Wd, self.actT
        for j in range(NJ):
            pg, b_pg = self.bank()

            def mm(e, j=j, pg=pg):
                for (W, c0) in ((Wg, 0), (Wu, 128)):
                    for k in range(KC):
                        i = e.matmul(pg[:, c0:c0 + 128], lhsT=W[:, k, j * 128:(j + 1) * 128], rhs=hT[:, k, :],
                                     start=(k == 0), stop=(k == KC - 1))
                return i
            P.op("pe", mm, [self.b_hT, self.b_Wg, self.b_Wu], [b_pg])
            P.v("act", "activation", [b_pg], [self.b_sg], out=self.sg[:], in_=pg[:, 0:128], func=AF.Silu)
            P.v("dve", "tensor_tensor", [self.b_sg, b_pg], [self.b_actT], out=actT[:, j, :], in0=self.sg[:],
                in1=pg[:, 128:256], op=ALU.mult)
        for n in range(2):
            pd, b_pd = self.bank()

            def mm2(e, n=n, pd=pd):
                for j in range(NJ):
                    i = e.matmul(pd[:], lhsT=actT[:, j, :], rhs=Wd[:, j, n * 512:(n + 1) * 512],
                                 start=(j == 0), stop=(j == NJ - 1))
                return i
            P.op("pe", mm2, [self.b_actT, self.b_Wd], [b_pd])
            P.v("dve", "tensor_tensor", [b_xt, b_pd], [b_xt], out=xt[:, n * 512:(n + 1) * 512],
                in0=xt[:, n * 512:(n + 1) * 512], in1=pd[:], op=ALU.add)
        if last:
            st, b_st = self.st1[ti % 2]
            self.rstd(xt[:], [b_xt], D, st, b_st, 1)
            P.v("dve", "scalar_tensor_tensor", [b_xt, b_st, self.b_gN], [b_xt], out=xt[:], in0=xt[:],
                scalar=st[:, 1:2], in1=self.gN[:], op0=ALU.mult, op1=ALU.mult)
            P.dma("sp", self.y_out[ti * 128:(ti + 1) * 128, :], xt[:], [b_xt], self.b_y)
        elif li == len(self.layers) - 1:
            P.dma("sp", self.y_out[ti * 128:(ti + 1) * 128, :], xt[:], [b_xt], self.b_y)
        else:
            P.dma("sp", self.xres[ti * 128:(ti + 1) * 128, :], xt[:], [b_xt], self.b_xres)

    def mixer_alloc(self):
        sb = self.frame_alloc
        f4 = (128, 4, 128)
        self.tb, self.b_tb = sb("tb", (128, 432))
        self.zs, self.b_zs = sb("zs", (128, 512))
        self.mkv, self.b_mkv = sb("mkv", (128, 512))
        self.mos, self.b_mos = sb("mos", (128, 256))
        self.mix, self.b_mix = sb("mix", (128, D), BF16)
        self.mixT, self.b_mixT = sb("mixT", (128, KC, 128), BF16)
        self.halo, self.b_halo = sb("halo", (128, 12, 3))
        self.ctmp, self.b_ctmp = self.junk[:, 0:1536].rearrange("p (m t) -> p m t", m=12), self.b_junk
        self.Nm, self.b_Nm = self.rn[:, 0:4, :], self.b_rn
        self.Bm, self.b_Bm = self.rn[:, 4:8, :], self.b_rn
        self.dcL, self.b_dcL = self.G4, self.b_G4
        self.u, self.b_u = self.egcB, self.b_egcB
        self.osb, self.b_osb = self.dcT, self.b_dcT
        self.sq, self.b_sq = sb("sq", (128, 8, 128), BF16)
        self.qkT, self.b_qkT = sb("qkT", f4, BF16)
        self.Tt, self.b_Tt = sb("Tt", f4, BF16)
        self.wT, self.b_wT = sb("wT", f4, BF16)
        self.vnew, self.b_vnew = sb("vnew", f4, BF16)
        self.S32, self.b_S32 = sb("S32", f4)
        self.Sbf, self.b_Sbf = sb("Sbf", f4, BF16)
        self.gt, self.b_gt = sb("gt", (128, 64))
        self.edec, self.b_edec = sb("edec", (128, 4, 2))
        self.mqT, self.b_mqT = sb("mqT", (128, 2, 128), BF16)
        self.mqk_st, self.b_mqk_st = sb("mqk_st", (128, 512), BF16)
        self.g8, self.b_g8 = sb("g8", (128, 2, 16))
        self.mkT, self.b_mkT = sb("mkT", (128, 2, 128), BF16)
        self.pk, self.b_pk = sb("pk", (4, 5, 128))
        self.fm, self.b_fm = sb("fm", (4, 2, 128))
        self.car, self.b_car = sb("car", (4, 4))
        self.bsrc, self.b_bsrc = sb("bsrc", (4, 130))
        self.bd, self.b_bd = sb("bd", (4, 4, 130))
        self.tm, self.b_tm = sb("tm", (128, 5, 4))
        self.ex, self.b_ex = sb("ex", (128, 6, 4))
        self.ET, self.b_ET = sb("ET", f4)
        self.sT, self.b_sT = sb("sT", f4, BF16)
        self.vaug, self.b_vaug = sb("vaug", (128, 4, 65), BF16)
        self.sv, self.b_sv = sb("sv", (128, 4, 65))
        self.kwk0, self.b_kwk0 = sb("kwk0", (128, 4, 64), BF16)
        self.kwk1, self.b_kwk1 = sb("kwk1", (128, 4, 64), BF16)
        self.C32, self.b_C32 = sb("C32", (128, 2, 65))
        self.Cbf, self.b_Cbf = sb("Cbf", (128, 2, 65), BF16)
        self.rsb, self.b_rsb = sb("rsb", (128, 4, 65))
        self.edm, self.b_edm = sb("edm", (128, 4, 2))
        self.hout, self.b_hout = sb("hout", (128, 4, 64))
        self.cqn, self.b_cqn = sb("cqn", (128, 256), BF16)
        self.cqT, self.b_cqT = sb("cqT", (128, 2, 128), BF16)
        self.dqT, self.b_dqT = sb("dqT", (128, 2, 128), BF16)
        self.qlatT, self.b_qlatT = sb("qlatT", f4, BF16)
        self.qidxT, self.b_qidxT = sb("qidxT", (128, 3, 128), BF16)
        self.Dw, self.b_Dw = sb("Dw", (128, 8, 128), BF16)
        self.rr = [sb("rr%d" % i, (128, 512), BF16) for i in range(2)]
        self.PT = [sb("PT%d" % i, (128, 8, 128), BF16) for i in range(1)]
        self.olatT, self.b_olatT = sb("olatT", f4, BF16)
        self.bs, self.b_bs = sb("bs", (128, 16))
        self.wk, self.b_wk = sb("wk", (128, NBIS + 1))
        self.rs4, self.b_rs4 = sb("rs4", (128, 8))
        self.wconv, _ = sb("wconv", (128, 12, 4))
        self.pb, _ = sb("pb", (128, 24))
        self.gon, _ = sb("gon", (128, 128))
        self.mon, _ = sb("mon", (128, 64))
        self.gKVb, _ = sb("gKVb", (128, 128))
        self.gQ, _ = sb("gQ", (128, 2))
        self.p4, _ = sb("p4", (4, 4))
        self.Wuq, _ = sb("Wuq", (128, 2, 256), BF16)
        self.Wqi, _ = sb("Wqi", (128, 2, 256), BF16)
        self.wuv, _ = sb("wuv", (128, 4, 64), BF16)
        self.wuk_n, _ = sb("wuk_n", (128, 4, 64))
        self.wukT, _ = sb("wukT", (128, 2, 128), BF16)
        self.Wik4, _ = sb("Wik4", (128, KC, 128), BF16)
        self.b_lp = self.P.buf("layerprm")

    def mixer_setup(self, li):
        P, prm, l = self.P, self.prm, li
        self.km = os.environ.get("KMODE", "")
        if "nosetup" in self.km:
            return
        lp = self.b_lp
        sp = lambda out, in_, **kw: P.dma("sp", out, in_, [], lp, **kw)
        for j in range(4):
            sp(self.wconv[:, :, j], prm["gdn_conv"][l, j].rearrange("(m p) -> p m", p=128), allow_slow_non_contiguous=True)
        sp(self.pb[:, 0:4], prm["gdn_a_log"][l].partition_broadcast(128))
        sp(self.pb[:, 4:8], prm["gdn_dt_bias"][l].partition_broadcast(128))
        sp(self.gon[:], prm["gdn_out_norm"][l].partition_broadcast(128))
        sp(self.mon[:], prm["mlstm_out_norm"][l].partition_broadcast(128))
        sp(self.gKVb[:], prm["dsa_kv_norm"][l].partition_broadcast(128))
        sp(self.gQ[:], prm["dsa_q_norm"][l].rearrange("(k p) -> p k", p=128), allow_slow_non_contiguous=True)
        sp(self.p4[:, 0:1], prm["mlstm_i_bias"][l].rearrange("(h o) -> h o", o=1))
        sp(self.p4[:, 1:2], prm["mlstm_f_bias"][l].rearrange("(h o) -> h o", o=1))
        sp(self.wuk_n[:], prm["dsa_w_uk"][l].rearrange("h c d -> c h d"))
        P.dma("pool", self.Wuq[:], prm["dsa_w_uq"][l].rearrange("(k p) n -> p k n", p=128), [], lp)
        P.dma("pool", self.Wqi[:], prm["dsa_w_qidx"][l].rearrange("(k p) n -> p k n", p=128), [], lp)
        P.dma("pool", self.wuv[:], prm["dsa_w_uv"][l].rearrange("h c d -> c h d"), [], lp)
        P.v("act", "activation", [lp], [lp], out=self.pb[:, 8:12], in_=self.pb[:, 0:4], func=AF.Exp)
        P.v("dve", "tensor_scalar", [lp], [lp], out=self.pb[:, 8:12], in0=self.pb[:, 8:12], scalar1=-1.0,
            scalar2=None, op0=ALU.mult)
        P.v("dve", "tensor_scalar", [lp], [lp], out=self.p4[:, 2:3], in0=self.p4[:, 1:2], scalar1=-1.0,
            scalar2=None, op0=ALU.mult)
        pt, b_pt = self.bank()
        wn, identf = self.wuk_n, self.identf

        def tr(e):
            for m in range(2):
                i = e.transpose(out=pt[:, m * 128:(m + 1) * 128],
                                in_=wn[:, 2 * m:2 * m + 2, :], identity=identf[:])
            return i
        P.op("pe", tr, [lp, self.b_c], [b_pt])
        P.v("dve", "tensor_scalar", [b_pt], [lp], out=self.wukT[:], in0=pt[:, 0:256].rearrange("p (m c) -> p m c", m=2),
            scalar1=0.125, scalar2=None, op0=ALU.mult)
        for g in range(4):
            P.v("pool", "tensor_copy", [self.b_Wi], [lp], out=self.Wik4[:, :, g * 32:(g + 1) * 32],
                in_=self.Wi[:, :, IK:IK + 32])
        P.v("dve", "memset", [], [self.b_S32], ap=self.S32[:], constant=0.0)
        P.v("dve", "memset", [], [self.b_Sbf], ap=self.Sbf[:], constant=0.0)
        P.v("dve", "memset", [], [self.b_C32], ap=self.C32[:], constant=0.0)
        P.v("dve", "memset", [], [self.b_Cbf], ap=self.Cbf[:], constant=0.0)
        P.v("dve", "memset", [], [self.b_car], ap=self.car[:], constant=0.0)
        P.v("dve", "memset", [], [self.b_halo], ap=self.halo[:], constant=0.0)
        P.v("dve", "memset", [], [self.b_vnew], ap=self.vnew[:], constant=0.0)
        P.v("dve", "memset", [], [self.b_vaug], ap=self.vaug[:], constant=1.0)
        P.v("dve", "memset", [], [self.b_kidx], ap=self.kidx, constant=0.0)

    def mixer_tile(self, li, ti):
        P = self.P
        xt, b_xt = self.load_x(li, ti)
        if "nomix" in self.km:
            P.dma("sp", self.xres[ti * 128:(ti + 1) * 128, :], xt[:], [b_xt], self.b_xres)
            return
        self.norm_T(xt, b_xt, self.gA, 0)
        self.projections()
        if "noilv" in self.km:
            self.gdn()
            self.mlstm()
        else:
            self.interleave(self.gdn, self.mlstm, [0, 1, 2, 3, 4], [5, 6, 7])
        self.dsa(ti)
        pt, b_pt = self.bank()
        ptb = pt[:].bitcast(BF16)
        mix, identb, mixT, Wo = self.mix, self.identb, self.mixT, self.Wo

        def tr(e):
            for k in range(KC):
                i = e.transpose(out=ptb[:, k * 128:(k + 1) * 128], in_=mix[:, k * 128:(k + 1) * 128],
                                identity=identb[:])
            return i
        P.op("pe", tr, [self.b_mix, self.b_c], [b_pt])
        P.v("act", "copy", [b_pt], [self.b_mixT], out=mixT[:], in_=ptb.rearrange("p (k t) -> p k t", k=KC))
        for n in range(2):
            po, b_po = self.bank()

            def mm(e, n=n, po=po):
                for k in range(KC):
                    i = e.matmul(po[:], lhsT=mixT[:, k, :], rhs=Wo[:, k, n * 512:(n + 1) * 512],
                                 start=(k == 0), stop=(k == KC - 1))
                return i
            P.op("pe", mm, [self.b_mixT, self.b_Wo], [b_po])
            P.v("dve", "tensor_tensor", [b_xt, b_po], [b_xt], out=xt[:, n * 512:(n + 1) * 512],
                in0=xt[:, n * 512:(n + 1) * 512], in1=po[:], op=ALU.add)
        P.dma("sp", self.xres[ti * 128:(ti + 1) * 128, :], xt[:], [b_xt], self.b_xres)
        if self.dbg and li == 0:
            P.v("pool", "tensor_copy", [self.b_mix], [self.b_junk], out=self.junk[:, 0:D], in_=mix[:])
            P.dma("sp", self.dbg_out[ti * 128:(ti + 1) * 128, :], self.junk[:, 0:D], [self.b_junk], self.b_dbg)

    def proj_fm(self, col0, nm, dst_fn):
        P, Wi, hT = self.P, self.Wi, self.hT
        pt, b_pt = self.bank()

        def mm(e):
            for m in range(nm):
                for k in range(KC):
                    i = e.matmul(pt[:, m * 128:(m + 1) * 128], lhsT=Wi[:, k, col0 + m * 128:col0 + (m + 1) * 128],
                                 rhs=hT[:, k, 0:128], start=(k == 0), stop=(k == KC - 1))
            return i
        P.op("pe", mm, [self.b_hT, self.b_Wi], [b_pt])
        dst_fn(pt[:, 0:nm * 128].rearrange("p (m t) -> p m t", m=nm), b_pt)

    def proj_tm(self, col0, n, dst_fn):
        P, Wi, hT = self.P, self.Wi, self.hT
        pt, b_pt = self.bank()

        def mm(e):
            for k in range(KC):
                i = e.matmul(pt[:, 0:n], lhsT=hT[:, k, 0:128], rhs=Wi[:, k, col0:col0 + n],
                             start=(k == 0), stop=(k == KC - 1))
            return i
        P.op("pe", mm, [self.b_hT, self.b_Wi], [b_pt])
        dst_fn(pt, b_pt)

    def projections(self):
        P = self.P
        cb = self.cbuf
        km = self.km
        if "p1" not in km:
            for g, c0 in enumerate((GQ, GK, GV)):
                self.proj_fm(c0, 4, lambda ps, b, g=g: P.v("act", "copy", [b], [self.b_cbuf],
                                                           out=cb[:, 4 * g:4 * g + 4, 3:131], in_=ps))

        if "p2" not in km:
            stg, b_stg = self.mqk_st, self.b_mqk_st
            self.proj_tm(MQ, 512, lambda pt, b: P.v("act", "copy", [b], [b_stg], out=stg[:], in_=pt[:]))
            pq_, b_pq = self.bank()
            pqb = pq_[:].bitcast(BF16)
            identb = self.identb

            def trq(e):
                for m in range(4):
                    i = e.transpose(out=pqb[:, m * 128:(m + 1) * 128], in_=stg[:, m * 128:(m + 1) * 128], identity=identb[:])
                return i
            P.op("pe", trq, [b_stg, self.b_c], [b_pq])
            pq4 = pqb[:, 0:512].rearrange("p (m t) -> p m t", m=4)
            if "p2b" not in km:
                P.v("act", "copy", [b_pq], [self.b_mqT], out=self.mqT[:], in_=pq4[:, 0:2, :])
            if "p2a" not in km:
                P.v("dve", "tensor_scalar", [b_pq], [self.b_mkT], out=self.mkT[:], in0=pq4[:, 2:4, :], scalar1=0.125,
                    scalar2=None, op0=ALU.mult)
        if "p3" not in km:
            self.proj_tm(GZ, 512, lambda pt, b: P.v("act", "activation", [b], [self.b_zs], out=self.zs[:],
                                                    in_=pt[:], func=AF.Silu))
        if "p4" not in km:
            self.proj_tm(GA, 432, lambda pt, b: P.v("act", "copy", [b], [self.b_tb], out=self.tb[:], in_=pt[:, 0:432]))
        if "p5" not in km:
            self.proj_tm(MK, 512, lambda pt, b: P.v("dve", "tensor_copy", [b], [self.b_mkv], out=self.mkv[:], in_=pt[:]))
        if "p6" not in km:
            def mo_(pt, b):
                P.v("act", "activation", [b], [self.b_mos], out=self.mos[:], in_=pt[:, 0:256], func=AF.Sigmoid)
                P.v("dve", "tensor_copy", [b], [self.b_g8], out=self.g8[:, :, 0:4],
                    in_=pt[:, 256:264].rearrange("p (q h) -> p q h", q=2))
            self.proj_tm(MO, 264, mo_)

    def gdn(self):
        P = self.P
        c = [self.b_c]
        lp = self.b_lp
        gt, b_gt, tb = self.gt, self.b_gt, self.tb
        f4 = (128, 4, 128)
        P.v("dve", "tensor_tensor", [self.b_tb, lp], [b_gt], out=gt[:, 40:44], in0=tb[:, 0:4], in1=self.pb[:, 4:8], op=ALU.add)
        P.v("act", "activation", [b_gt], [b_gt], out=gt[:, 40:44], in_=gt[:, 40:44], func=AF.Exp)
        P.v("act", "activation", [b_gt], [b_gt], out=gt[:, 40:44], in_=gt[:, 40:44], func=AF.Ln, bias=1.0)
        P.v("dve", "tensor_tensor", [b_gt, lp], [b_gt], out=gt[:, 0:4], in0=gt[:, 40:44], in1=self.pb[:, 8:12], op=ALU.mult)
        P.v("act", "activation", [self.b_tb], [b_gt], out=gt[:, 4:8], in_=tb[:, 4:8], func=AF.Sigmoid)
        P.v("dve", "tensor_scalar", [b_gt], [b_gt], out=gt[:, 8:12], in0=gt[:, 4:8], scalar1=-1.0, scalar2=None, op0=ALU.mult)
        pg, b_pg = self.bank()
        utri, bones = self.utriC, self.bonesC

        def mm(e):
            e.matmul(pg[:, 0:4], lhsT=utri[:], rhs=gt[:, 0:4], start=True, stop=True)
            return e.matmul(pg[:, 4:8], lhsT=bones[:], rhs=gt[:, 0:4], start=True, stop=True)
        P.op("pe", mm, [b_gt] + c, [b_pg])
        P.v("dve", "tensor_copy", [b_pg], [b_gt], out=gt[:, 12:20], in_=pg[:, 0:8])
        P.v("act", "activation", [b_gt], [b_gt], out=gt[:, 20:24], in_=gt[:, 12:16], func=AF.Exp)
        P.v("dve", "tensor_tensor", [b_gt], [b_gt], out=gt[:, 40:44], in0=gt[:, 16:20], in1=gt[:, 12:16], op=ALU.subtract)
        P.v("act", "activation", [b_gt], [b_gt], out=gt[:, 24:28], in_=gt[:, 40:44], func=AF.Exp)
        P.v("dve", "tensor_tensor", [b_gt], [b_gt], out=gt[:, 28:32], in0=gt[:, 4:8], in1=gt[:, 20:24], op=ALU.mult)
        P.v("dve", "tensor_scalar", [b_gt] + c, [b_gt], out=gt[:, 32:36], in0=gt[:, 24:28], scalar1=self.rowm[:, 0:1], scalar2=None, op0=ALU.mult)
        P.v("dve", "tensor_scalar", [b_gt] + c, [b_gt], out=gt[:, 36:40], in0=gt[:, 24:28], scalar1=self.rowm[:, 1:2], scalar2=None, op0=ALU.mult)
        P.v("dve", "tensor_tensor", [b_gt] + c, [self.b_G4], out=self.G4[:], in0=bc(self.utriC[:].unsqueeze(1), f4),
            in1=bc(gt[:, 0:4].unsqueeze(2), f4), op=ALU.mult)
        pB, b_pB = self.bank()
        G4, onesf = self.G4, self.onesf
        P.op("pe", lambda e: e.matmul(pB[:], lhsT=onesf[:], rhs=G4[:].rearrange("p h i -> p (h i)"), start=True, stop=True),
             [self.b_G4] + c, [b_pB])
        pB4 = pB[:].rearrange("p (h i) -> p h i", h=4)
        P.v("act", "activation", [b_pB], [self.b_egcB], out=self.egcB[:], in_=pB4, func=AF.Exp)
        P.v("act", "activation", [b_pB], [self.b_edec], out=self.edec[:],
            in_=pB[:].rearrange("p (h c i) -> p h c i", h=4, c=2)[:, :, :, 63], func=AF.Exp)
        for h in range(4):
            P.v("dve", "scalar_tensor_tensor", [b_pB, b_gt] + c, [self.b_dcT], out=self.dcT[:, h, :], in0=pB4[:, h, :],
                scalar=gt[:, 12 + h:13 + h], in1=self.mnegT[:], op0=ALU.subtract, op1=ALU.add)
            P.v("dve", "scalar_tensor_tensor", [b_pB, b_gt] + c, [self.b_dcL], out=self.dcL[:, h, :], in0=pB4[:, h, :],
                scalar=gt[:, 12 + h:13 + h], in1=self.mposL[:], op0=ALU.subtract, op1=ALU.add)
        P.v("act", "activation", [self.b_dcT], [self.b_dcT], out=self.dcT[:], in_=self.dcT[:], func=AF.Exp)
        P.v("act", "activation", [self.b_dcL], [self.b_dcL], out=self.dcL[:], in_=self.dcL[:], func=AF.Exp, scale=-1.0)
        cb, ca, ct, wc = self.cbuf, self.cact, self.ctmp, self.wconv
        s12 = (128, 12, 128)
        P.v("pool", "tensor_copy", [self.b_halo], [self.b_cbuf], out=cb[:, :, 0:3], in_=self.halo[:])
        P.v("pool", "tensor_tensor", [self.b_cbuf, lp], [self.b_cact], out=ca[:], in0=cb[:, :, 0:128],
            in1=bc(wc[:, :, 0:1], s12), op=ALU.mult)
        for j in range(1, 4):
            P.v("pool", "tensor_tensor", [self.b_cbuf, lp], [self.b_ctmp], out=ct[:], in0=cb[:, :, j:j + 128],
                in1=bc(wc[:, :, j:j + 1], s12), op=ALU.mult)
            P.v("pool", "tensor_tensor", [self.b_cact, self.b_ctmp], [self.b_cact], out=ca[:], in0=ca[:], in1=ct[:], op=ALU.add)
        P.v("pool", "tensor_copy", [self.b_cbuf], [self.b_halo], out=self.halo[:], in_=cb[:, :, 128:131])
        P.v("act", "activation", [self.b_cact], [self.b_cact], out=ca[:], in_=ca[:], func=AF.Silu)
        P.v("pool", "tensor_tensor", [self.b_cact], [self.b_sq], out=self.sq[:], in0=ca[:, 0:8, :], in1=ca[:, 0:8, :], op=ALU.mult)
        sq, o128, ob = self.sq, self.ones128b, self.onesb
        for half, lh in ((0, o128), (1, ob)):
            pn, b_pn = self.bank()
            P.op("pe", lambda e, half=half, lh=lh, pn=pn: e.matmul(
                pn[:], lhsT=lh[:], rhs=sq[:, 4 * half:4 * half + 4, :].rearrange("p h i -> p (h i)"), start=True, stop=True),
                [self.b_sq] + c, [b_pn])
            rnv = self.rn[:, 4 * half:4 * half + 4, :].rearrange("p h i -> p (h i)")
            P.v("act", "activation", [b_pn] + c, [self.b_rn], out=rnv, in_=pn[:], func=AF.Ln,
                bias=self.epsc[:, 1 - half:2 - half])
            P.v("act", "activation", [self.b_rn], [self.b_rn], out=rnv, in_=rnv, func=AF.Exp, scale=-0.5)
        P.v("dve", "tensor_tensor", [self.b_cact, self.b_rn], [self.b_qT], out=self.qT[:], in0=ca[:, 0:4, :], in1=self.rn[:, 0:4, :], op=ALU.mult)
        P.v("pool", "tensor_tensor", [self.b_cact, self.b_rn], [self.b_kT], out=self.kT[:], in0=ca[:, 4:8, :], in1=self.rn[:, 4:8, :], op=ALU.mult)
        P.v("dve", "tensor_tensor", [self.b_qT, self.b_egcB], [self.b_qdT], out=self.qdT[:], in0=self.qT[:], in1=self.egcB[:], op=ALU.mult)
        pv, b_pv = self.bank()
        identf, identb, kT, qT = self.identf, self.identb, self.kT, self.qT

        def trv(e):
            for h in range(4):
                i = e.transpose(out=pv[:, h * 128:(h + 1) * 128], in_=ca[:, 8 + h, :], identity=identf[:])
            return i
        P.op("pe", trv, [self.b_cact] + c, [b_pv])
        P.v("dve", "tensor_tensor", [b_pv, b_gt], [self.b_vb], out=self.vb[:], in0=pv[:].rearrange("p (h d) -> p h d", h=4),
            in1=bc(gt[:, 4:8].unsqueeze(2), f4), op=ALU.mult)
        pk_, b_pk = self.bank()
        pkb = pk_[:].bitcast(BF16)

        def trk(e):
            for h in range(4):
                i = e.transpose(out=pkb[:, h * 128:(h + 1) * 128], in_=kT[:, h, :], identity=identb[:])
            return i
        P.op("pe", trk, [self.b_kT] + c, [b_pk])
        pk4 = pkb[:, 0:512].rearrange("p (h d) -> p h d", h=4)
        P.v("dve", "tensor_tensor", [b_pk, b_gt], [self.b_kbg], out=self.kbg[:], in0=pk4, in1=bc(gt[:, 28:32].unsqueeze(2), f4), op=ALU.mult)
        P.v("dve", "tensor_tensor", [b_pk, b_gt], [self.b_kd0], out=self.kd0[:], in0=pk4, in1=bc(gt[:, 32:36].unsqueeze(2), f4), op=ALU.mult)
        P.v("dve", "tensor_tensor", [b_pk, b_gt], [self.b_kd1], out=self.kd1[:], in0=pk4, in1=bc(gt[:, 36:40].unsqueeze(2), f4), op=ALU.mult)
        pkk, b_pkk = self.bank()
        pkq, b_pkq = self.bank()

        def mmk(e):
            for h in range(4):
                e.matmul(pkk[:, h * 128:(h + 1) * 128], lhsT=kT[:, h, :], rhs=kT[:, h, :], start=True, stop=True)
            for h in range(4):
                i = e.matmul(pkq[:, h * 128:(h + 1) * 128], lhsT=kT[:, h, :], rhs=qT[:, h, :], start=True, stop=True)
            return i
        P.op("pe", mmk, [self.b_kT, self.b_qT], [b_pkk, b_pkq])
        Nm, Bm, Qm = self.Nm, self.Bm, self.Qm
        for h in range(4):
            P.v("dve", "scalar_tensor_tensor", [b_pkk, b_gt, self.b_dcL], [self.b_Nm], out=Nm[:, h, :],
                in0=pkk[:, h * 128:(h + 1) * 128], scalar=gt[:, 8 + h:9 + h], in1=self.dcL[:, h, :], op0=ALU.mult, op1=ALU.mult)
        P.v("dve", "tensor_tensor", [b_pkq, self.b_dcT], [self.b_qkT], out=self.qkT[:], in0=pkq[:].rearrange("p (h i) -> p h i", h=4),
            in1=self.dcT[:], op=ALU.mult)
        pb0, b_pb0 = self.bank()

        def trn(e):
            for h in range(4):
                i = e.transpose(out=pb0[:, h * 128:(h + 1) * 128], in_=Nm[:, h, :], identity=identf[:])
            return i
        P.op("pe", trn, [self.b_Nm] + c, [b_pb0])
        pb04 = pb0[:].rearrange("p (h i) -> p h i", h=4)
        P.v("act", "copy", [b_pb0], [self.b_Bm], out=Bm[:], in_=pb04)
        P.v("dve", "tensor_tensor", [b_pb0] + c, [self.b_Qm], out=Qm[:], in0=pb04, in1=bc(identf[:].unsqueeze(1), f4), op=ALU.add)
        for lev in range(6):
            sqr = lev < 5
            upd = lev >= 1
            pN, b_pN = self.bank() if sqr else (None, None)
            pBb, b_pBb = self.bank() if sqr else (None, None)
            pQ, b_pQ = self.bank() if upd else (None, None)

            def mml(e, sqr=sqr, upd=upd, pN=pN, pBb=pBb, pQ=pQ):
                i = None
                for h in range(4):
                    sl = slice(h * 128, (h + 1) * 128)
                    if sqr:
                        e.matmul(pN[:, sl], lhsT=Bm[:, h, :], rhs=Nm[:, h, :], start=True, stop=True)
                        i = e.matmul(pBb[:, sl], lhsT=Nm[:, h, :], rhs=Bm[:, h, :], start=True, stop=True)
                    if upd:
                        i = e.matmul(pQ[:, sl], lhsT=Nm[:, h, :], rhs=Qm[:, h, :], start=True, stop=True)
                return i
            wr = [b for b in (b_pN, b_pBb, b_pQ) if b is not None]
            P.op("pe", mml, [self.b_Nm, self.b_Bm, self.b_Qm], wr)
            v4 = lambda t: t[:].rearrange("p (h i) -> p h i", h=4)
            if upd:
                if lev < 5:
                    P.v("dve", "tensor_tensor", [b_pQ, self.b_Qm], [self.b_Qm], out=Qm[:], in0=v4(pQ), in1=Qm[:], op=ALU.add)
                else:
                    P.v("dve", "tensor_tensor", [b_pQ, self.b_Qm], [self.b_Tt], out=self.Tt[:], in0=v4(pQ), in1=Qm[:], op=ALU.add)
            if sqr:
                P.v("act", "copy", [b_pN], [self.b_Nm], out=Nm[:], in_=v4(pN))
                P.v("dve", "tensor_copy", [b_pBb], [self.b_Bm], out=Bm[:], in_=v4(pBb))
        pu, b_pu = self.bank()
        pw, b_pw = self.bank()
        Tt, vb, kbg = self.Tt, self.vb, self.kbg

        def mmu(e):
            for h in range(4):
                e.matmul(pu[:, h * 128:(h + 1) * 128], lhsT=Tt[:, h, :], rhs=vb[:, h, :], start=True, stop=True)
            for h in range(4):
                i = e.matmul(pw[:, h * 128:(h + 1) * 128], lhsT=kbg[:, h, :], rhs=Tt[:, h, :], start=True, stop=True)
            return i
        P.op("pe", mmu, [self.b_Tt, self.b_vb, self.b_kbg], [b_pu, b_pw])
        P.v("act", "copy", [b_pu], [self.b_u], out=self.u[:], in_=pu[:].rearrange("p (h i) -> p h i", h=4))
        P.v("dve", "tensor_copy", [b_pw], [self.b_wT], out=self.wT[:], in_=pw[:].rearrange("p (h i) -> p h i", h=4))
        wT, Sbf, qdT, qkT, vnew, S32 = self.wT, self.Sbf, self.qdT, self.qkT, self.vnew, self.S32
        for cc in range(2):
            kd = self.kd0 if cc == 0 else self.kd1
            b_kd = self.b_kd0 if cc == 0 else self.b_kd1
            pV, b_pV = self.bank()

            def mmv(e, pV=pV):
                for h in range(4):
                    i = e.matmul(pV[:, h * 128:(h + 1) * 128], lhsT=wT[:, h, :], rhs=Sbf[:, h, :], start=True, stop=True)
                return i
            P.op("pe", mmv, [self.b_wT, self.b_Sbf], [b_pV])
            P.v("dve", "tensor_tensor", [self.b_u, b_pV], [self.b_vnew], out=vnew[:], in0=self.u[:],
                in1=pV[:].rearrange("p (h i) -> p h i", h=4), op=ALU.subtract)
            pO, b_pO = self.bank()
            pS, b_pS = self.bank()

            def mmo(e, pO=pO, pS=pS, kd=kd):
                for h in range(4):
                    sl = slice(h * 128, (h + 1) * 128)
                    e.matmul(pO[:, sl], lhsT=qdT[:, h, :], rhs=Sbf[:, h, :], start=True, stop=False)
                    e.matmul(pO[:, sl], lhsT=qkT[:, h, :], rhs=vnew[:, h, :], start=False, stop=True)
                for h in range(4):
                    i = e.matmul(pS[:, h * 128:(h + 1) * 128], lhsT=kd[:, h, :], rhs=vnew[:, h, :], start=True, stop=True)
                return i
            P.op("pe", mmo, [self.b_qdT, self.b_Sbf, self.b_qkT, self.b_vnew, b_kd], [b_pO, b_pS])
            rows = slice(cc * 64, (cc + 1) * 64)
            P.v("act", "copy", [b_pO], [self.b_osb], out=self.osb[rows], in_=pO[rows].rearrange("p (h i) -> p h i", h=4))
            P.v("dve", "tensor_tensor", [self.b_S32, self.b_edec], [self.b_S32], out=S32[:], in0=S32[:],
                in1=bc(self.edec[:, :, cc:cc + 1], f4), op=ALU.mult)
            P.v("dve", "tensor_tensor", [self.b_S32, b_pS], [self.b_S32], out=S32[:], in0=S32[:],
                in1=pS[:].rearrange("p (h i) -> p h i", h=4), op=ALU.add)
            P.v("act", "copy", [self.b_S32], [self.b_Sbf], out=Sbf[:], in_=S32[:])
        self.out_norm(self.osb, self.b_osb, 128, self.gon, self.zs, self.b_zs, 0)

    def out_norm(self, src, b_src, dv, gain, gate, b_gate, col0):
        P = self.P
        sh = (128, 4, dv)
        jv = self.junk[:, 0:4 * dv].rearrange("p (h d) -> p h d", h=4)
        st, b_st = self.st1[0]
        P.v("pool", "tensor_tensor", [b_src], [self.b_junk], out=jv, in0=src[:], in1=src[:], op=ALU.mult)
        P.v("dve", "tensor_reduce", [self.b_junk], [b_st], out=st[:, 0:4], in_=jv, axis=AX.X, op=ALU.add)
        P.v("act", "activation", [b_st, self.b_c], [b_st], out=st[:, 0:4], in_=st[:, 0:4], func=AF.Ln, scale=1.0 / dv,
            bias=self.epsc[:, 0:1])
        P.v("act", "activation", [b_st], [b_st], out=st[:, 0:4], in_=st[:, 0:4], func=AF.Exp, scale=-0.5)
        P.v("dve", "tensor_tensor", [b_src, b_st], [self.b_junk], out=jv, in0=src[:], in1=bc(st[:, 0:4].unsqueeze(2), sh), op=ALU.mult)
        P.v("pool", "tensor_tensor", [self.b_junk, self.b_lp], [self.b_junk], out=jv, in0=jv, in1=bc(gain[:].unsqueeze(1), sh), op=ALU.mult)
        P.v("dve", "tensor_tensor", [self.b_junk, b_gate], [self.b_mix], out=self.mix[:, col0:col0 + 4 * dv],
            in0=self.junk[:, 0:4 * dv], in1=gate[:, 0:4 * dv], op=ALU.mult)

    def mlstm(self):
        P = self.P
        c = [self.b_c]
        lp = self.b_lp
        f4 = (128, 4, 128)
        Wi, hT = self.Wi, self.hT
        pk, b_pk, fm, b_fm, car, b_car = self.pk, self.b_pk, self.fm, self.b_fm, self.car, self.b_car
        pg, b_pg = self.bank()

        g8, identf_ = self.g8, self.identf

        def mmg(e):
            for q in range(2):
                i = e.transpose(out=pg[0:4, q * 128:(q + 1) * 128], in_=g8[:, q, 0:4], identity=identf_[:])
            return i
        P.op("pe", mmg, [self.b_g8, self.b_c], [b_pg])
        P.v("act", "activation", [b_pg, lp], [b_fm], out=fm[:, 0, :], in_=pg[0:4, 128:256], func=AF.Exp, scale=-1.0,
            bias=self.p4[:, 2:3])
        P.v("act", "activation", [b_fm], [b_fm], out=fm[:, 0, :], in_=fm[:, 0, :], func=AF.Ln, bias=1.0)
        P.v("dve", "tensor_scalar", [b_pg, lp], [b_fm], out=fm[:, 1, :], in0=pg[0:4, 0:128], scalar1=self.p4[:, 0:1],
            scalar2=None, op0=ALU.add)
        P.v("dve", "tensor_tensor_scan", [b_fm, b_car] + c, [b_pk], out=pk[:, 2, :], data0=self.onesf[0:4, :],
            data1=fm[:, 0, :], initial=car[:, 0:1], op0=ALU.mult, op1=ALU.subtract)
        P.v("dve", "tensor_tensor", [b_fm, b_pk], [b_pk], out=pk[:, 0, :], in0=fm[:, 1, :], in1=pk[:, 2, :], op=ALU.subtract)
        P.v("dve", "tensor_tensor_scan", [b_pk, b_car], [b_pk], out=pk[:, 1, :], data0=pk[:, 0, :],
            data1=pk[:, 0, :], initial=car[:, 1:2], op0=ALU.max, op1=ALU.max)
        P.v("dve", "tensor_copy", [b_pk], [b_pk], out=pk[:, 3, :].rearrange("p (c i) -> p c i", c=2),
            in_=bc(pk[:, 1, :].rearrange("p (c i) -> p c i", c=2)[:, :, 63:64], (4, 2, 64)))
        P.v("dve", "tensor_copy", [b_car], [b_pk], out=pk[:, 4, 0:64], in_=bc(car[:, 1:2], (4, 64)))
        P.v("dve", "tensor_copy", [b_pk], [b_pk], out=pk[:, 4, 64:128], in_=bc(pk[:, 1, 63:64], (4, 64)))
        bsrc, b_bsrc, bd, b_bd = self.bsrc, self.b_bsrc, self.bd, self.b_bd
        P.v("dve", "tensor_scalar", [b_pk], [b_bsrc], out=bsrc[:, 0:128], in0=pk[:, 1, :], scalar1=-1.0, scalar2=None, op0=ALU.mult)
        P.v("dve", "tensor_tensor", [b_car, b_pk], [b_bsrc], out=bsrc[:, 128:129], in0=car[:, 1:2], in1=pk[:, 1, 63:64], op=ALU.subtract)
        P.v("dve", "tensor_tensor", [b_pk], [b_bsrc], out=bsrc[:, 129:130], in0=pk[:, 1, 63:64], in1=pk[:, 1, 127:128], op=ALU.subtract)
        P.v("dve", "tensor_tensor", [b_bsrc] + c, [b_bd], out=bd[:], in0=bc(bsrc[:].unsqueeze(1), (4, 4, 130)),
            in1=bc(self.identf[0:4, 0:4].unsqueeze(2), (4, 4, 130)), op=ALU.mult)
        P.v("dve", "tensor_copy", [b_pk], [b_car], out=car[:, 0:1], in_=pk[:, 2, 127:128])
        P.v("dve", "tensor_copy", [b_pk], [b_car], out=car[:, 1:2], in_=pk[:, 1, 127:128])
        ptm, b_ptm = self.bank()
        identf, onesf = self.identf, self.onesf

        def trt(e):
            for q in range(5):
                i = e.transpose(out=ptm[:, q * 4:(q + 1) * 4], in_=pk[:, q, :], identity=identf[0:4, 0:4])
            return i
        P.op("pe", trt, [b_pk] + c, [b_ptm])
        tm, b_tm, ex, b_ex = self.tm, self.b_tm, self.ex, self.b_ex
        P.v("dve", "tensor_copy", [b_ptm], [b_tm], out=tm[:], in_=ptm[:, 0:20].rearrange("p (q h) -> p q h", q=5))
        P.v("dve", "tensor_tensor", [b_tm], [b_ex], out=ex[:, 0, :], in0=tm[:, 0, :], in1=tm[:, 3, :], op=ALU.subtract)
        P.v("dve", "tensor_tensor", [b_tm], [b_ex], out=ex[:, 1, :], in0=tm[:, 4, :], in1=tm[:, 1, :], op=ALU.subtract)
        P.v("dve", "scalar_tensor_tensor", [b_tm], [b_ex], out=ex[:, 2, :], in0=tm[:, 2, :], scalar=-1.0, in1=tm[:, 1, :],
            op0=ALU.mult, op1=ALU.subtract)
        P.v("act", "activation", [b_ex], [b_ex], out=ex[:, 0:3, :], in_=ex[:, 0:3, :], func=AF.Exp)
        for cc in range(2):
            P.v("dve", "tensor_scalar", [b_ex] + c, [b_ex], out=ex[:, 3 + cc, :], in0=ex[:, 0, :], scalar1=self.rowm[:, cc:cc + 1],
                scalar2=0.125, op0=ALU.mult, op1=ALU.mult)
        pM, b_pM = self.bank()
        pD, b_pD = self.bank()

        def mmb(e):
            for h in range(4):
                e.matmul(pM[:, h * 128:(h + 1) * 128], lhsT=onesf[0:4, :], rhs=bd[:, h, 0:128], start=True, stop=True)
            return e.matmul(pD[:, 0:8], lhsT=onesf[0:4, :], rhs=bd[:, :, 128:130], start=True, stop=True)
        P.op("pe", mmb, [b_bd] + c, [b_pM, b_pD])
        P.v("act", "activation", [b_pD], [self.b_edm], out=self.edm[:], in_=pD[:, 0:8].rearrange("p (h c) -> p h c", h=4), func=AF.Exp)
        ET, b_ET = self.ET, self.b_ET
        for h in range(4):
            P.v("dve", "scalar_tensor_tensor", [b_pM, b_tm] + c, [b_ET], out=ET[:, h, :], in0=pM[:, h * 128:(h + 1) * 128],
                scalar=tm[:, 0, h:h + 1], in1=self.mnegT[:], op0=ALU.add, op1=ALU.add)
        P.v("act", "activation", [b_ET], [b_ET], out=ET[:], in_=ET[:], func=AF.Exp)
        mkT, mqT = self.mkT, self.mqT
        pqs = [self.bank(), self.bank()]

        def mmq(e):
            for h in range(4):
                r = slice((h % 2) * 64, (h % 2) * 64 + 64)
                i = e.matmul(pqs[h % 2][0][:, (h // 2) * 128:(h // 2 + 1) * 128], lhsT=mkT[r, h // 2, :], rhs=mqT[r, h // 2, :],
                             start=True, stop=True)
            return i
        P.op("pe", mmq, [self.b_mkT, self.b_mqT], [pqs[0][1], pqs[1][1]])
        for h in range(4):
            P.v("dve", "tensor_tensor", [pqs[h % 2][1], b_ET], [self.b_sT], out=self.sT[:, h, :],
                in0=pqs[h % 2][0][:, (h // 2) * 128:(h // 2 + 1) * 128], in1=ET[:, h, :], op=ALU.mult)
        mk4 = self.mkv[:, 0:256].rearrange("p (h d) -> p h d", h=4)
        mv4 = self.mkv[:, 256:512].rearrange("p (h d) -> p h d", h=4)
        s64 = (128, 4, 64)
        P.v("pool", "tensor_copy", [self.b_mkv], [self.b_vaug], out=self.vaug[:, :, 0:64], in_=mv4)
        P.v("dve", "tensor_tensor", [self.b_mkv, b_ex], [self.b_kwk0], out=self.kwk0[:], in0=mk4, in1=bc(ex[:, 3, :].unsqueeze(2), s64), op=ALU.mult)
        P.v("dve", "tensor_tensor", [self.b_mkv, b_ex], [self.b_kwk1], out=self.kwk1[:], in0=mk4, in1=bc(ex[:, 4, :].unsqueeze(2), s64), op=ALU.mult)
        psv, b_psv = self.bank()
        sT, vaug = self.sT, self.vaug

        def mms(e):
            for h in range(4):
                i = e.matmul(psv[:, h * 65:(h + 1) * 65], lhsT=sT[:, h, :], rhs=vaug[:, h, :], start=True, stop=True)
            return i
        P.op("pe", mms, [self.b_sT, self.b_vaug], [b_psv])
        P.v("act", "copy", [b_psv], [self.b_sv], out=self.sv[:], in_=psv[:, 0:260].rearrange("p (h d) -> p h d", h=4))
        C32, Cbf, rsb = self.C32, self.Cbf, self.rsb
        for cc in range(2):
            kwk = self.kwk0 if cc == 0 else self.kwk1
            b_kwk = self.b_kwk0 if cc == 0 else self.b_kwk1
            pAs = [self.bank(), self.bank()]

            def mma(e, pAs=pAs):
                for h in range(4):
                    r = slice((h % 2) * 64, (h % 2) * 64 + 64)
                    i = e.matmul(pAs[h % 2][0][:, (h // 2) * 65:(h // 2 + 1) * 65], lhsT=mqT[r, h // 2, :], rhs=Cbf[r, h // 2, :],
                                 start=True, stop=True)
                return i
            P.op("pe", mma, [self.b_mqT, self.b_Cbf], [pAs[0][1], pAs[1][1]])
            rows = slice(cc * 64, (cc + 1) * 64)
            for h in range(4):
                P.v("dve", "scalar_tensor_tensor", [pAs[h % 2][1], b_ex, self.b_sv], [self.b_rsb], out=rsb[rows, h, :],
                    in0=pAs[h % 2][0][rows, (h // 2) * 65:(h // 2 + 1) * 65], scalar=ex[rows, 1, h:h + 1], in1=self.sv[rows, h, :],
                    op0=ALU.mult, op1=ALU.add)
            pC, b_pC = self.bank()

            def mmc(e, pC=pC, kwk=kwk):
                for m in range(2):
                    i = e.matmul(pC[:, m * 130:(m + 1) * 130], lhsT=kwk[:, 2 * m:2 * m + 2, :],
                                 rhs=vaug[:, 2 * m:2 * m + 2, :], start=True, stop=True)
                return i
            P.op("pe", mmc, [b_kwk, self.b_vaug], [b_pC])
            for m in range(2):
                for hh in range(2):
                    r = slice(hh * 64, hh * 64 + 64)
                    h = 2 * m + hh
                    P.v("dve", "scalar_tensor_tensor", [self.b_C32, self.b_edm, b_pC], [self.b_C32], out=C32[r, m, :], in0=C32[r, m, :],
                        scalar=self.edm[r, h, cc:cc + 1], in1=pC[r, m * 130 + hh * 65:m * 130 + hh * 65 + 65], op0=ALU.mult, op1=ALU.add)
            P.v("act", "copy", [self.b_C32], [self.b_Cbf], out=Cbf[:], in_=C32[:])
        st, b_st = self.st1[1]
        P.v("dve", "tensor_scalar", [self.b_rsb], [b_ex], out=ex[:, 5, :], in0=rsb[:, :, 64], scalar1=-1.0, scalar2=None, op0=ALU.mult)
        P.v("dve", "tensor_tensor", [self.b_rsb, b_ex], [b_ex], out=ex[:, 5, :], in0=rsb[:, :, 64], in1=ex[:, 5, :], op=ALU.max)
        P.v("dve", "tensor_tensor", [b_ex], [b_st], out=st[:, 0:4], in0=ex[:, 5, :], in1=ex[:, 2, :], op=ALU.max)
        P.v("dve", "reciprocal", [b_st], [b_st], out=st[:, 0:4], in_=st[:, 0:4])
        P.v("dve", "tensor_tensor", [self.b_rsb, b_st], [self.b_hout], out=self.hout[:], in0=rsb[:, :, 0:64],
            in1=bc(st[:, 0:4].unsqueeze(2), s64), op=ALU.mult)
        self.out_norm(self.hout, self.b_hout, 64, self.mon, self.mos, self.b_mos, 768)

    def dsa(self, ti):
        P = self.P
        c = [self.b_c]
        lp = self.b_lp
        tb, b_tb = self.tb, self.b_tb
        qt = ti
        S = 128 * (qt + 1)
        T0 = qt * 128
        identb, identf = self.identb, self.identf
        st, b_st = self.st1[0]
        self.rstd(tb[:, 8:264], [b_tb], 256, st, b_st, 2)
        P.v("act", "activation", [b_tb, b_st], [self.b_cqn], out=self.cqn[:], in_=tb[:, 8:264], func=AF.Copy, scale=st[:, 2:3])
        p1, b_p1 = self.bank()
        p1b = p1[:].bitcast(BF16)
        cqn = self.cqn

        def tr1(e):
            for k in range(2):
                i = e.transpose(out=p1b[:, k * 128:(k + 1) * 128], in_=cqn[:, k * 128:(k + 1) * 128], identity=identb[:])
            return i
        P.op("pe", tr1, [self.b_cqn] + c, [b_p1])
        P.v("dve", "tensor_tensor", [b_p1, lp], [self.b_cqT], out=self.cqT[:], in0=p1b[:, 0:256].rearrange("p (k t) -> p k t", k=2),
            in1=bc(self.gQ[:].unsqueeze(2), (128, 2, 128)), op=ALU.mult)
        self.rstd(tb[:, 264:392], [b_tb], 128, st, b_st, 3)
        P.v("dve", "scalar_tensor_tensor", [b_tb, b_st, lp], [self.b_ckvK], out=self.ckvK[:, qt, :], in0=tb[:, 264:392],
            scalar=st[:, 3:4], in1=self.gKVb[:], op0=ALU.mult, op1=ALU.mult)
        p2, b_p2 = self.bank()
        p2b = p2[:].bitcast(BF16)
        ckvK = self.ckvK
        P.op("pe", lambda e: e.transpose(out=p2b[:, 0:128], in_=ckvK[:, qt, :], identity=identb[:]), [self.b_ckvK] + c, [b_p2])
        P.v("act", "copy", [b_p2], [self.b_ckvT], out=self.ckvT[:, T0:T0 + 128], in_=p2b[:, 0:128])
        p3, b_p3 = self.bank()
        Wik4, hT = self.Wik4, self.hT

        def mmk(e):
            for k in range(KC):
                i = e.matmul(p3[:, 0:128], lhsT=Wik4[:, k, :], rhs=hT[:, k, 0:128], start=(k == 0), stop=(k == KC - 1))
            return i
        P.op("pe", mmk, [self.b_hT, lp], [b_p3])
        P.v("act", "copy", [b_p3], [self.b_kidx], out=self.kidx[:, T0:T0 + 128], in_=p3[:, 0:128])
        p4_, b_p4 = self.bank()
        Wuq, Wqi, cqT = self.Wuq, self.Wqi, self.cqT

        def mmq(e):
            for m in range(2):
                for k in range(2):
                    i = e.matmul(p4_[:, m * 128:(m + 1) * 128], lhsT=Wuq[:, k, m * 128:(m + 1) * 128],
                                 rhs=cqT[:, k, :], start=(k == 0), stop=(k == 1))
            return i
        P.op("pe", mmq, [self.b_cqT, lp], [b_p4])
        P.v("act", "copy", [b_p4], [self.b_dqT], out=self.dqT[:], in_=p4_[:, 0:256].rearrange("p (m t) -> p m t", m=2))
        p6, b_p6 = self.bank()

        def mmqi(e):
            for m in range(3):
                w = 96 if m < 2 else 64
                for k in range(2):
                    i = e.matmul(p6[0:w, m * 128:(m + 1) * 128], lhsT=Wqi[:, k, m * 96:m * 96 + w],
                                 rhs=cqT[:, k, :], start=(k == 0), stop=(k == 1))
            return i
        P.op("pe", mmqi, [self.b_cqT, lp], [b_p6])
        P.v("dve", "tensor_copy", [b_p6], [self.b_qidxT], out=self.qidxT[0:96, 0:2, :], in_=p6[0:96, 0:256].rearrange("p (m t) -> p m t", m=2))
        P.v("dve", "tensor_copy", [b_p6], [self.b_qidxT], out=self.qidxT[0:64, 2, :], in_=p6[0:64, 256:384])
        wukT, dqT = self.wukT, self.dqT
        p5s = [self.bank(), self.bank()]

        def mml(e):
            for h in range(4):
                r = slice((h % 2) * 64, (h % 2) * 64 + 64)
                i = e.matmul(p5s[h % 2][0][:, (h // 2) * 128:(h // 2 + 1) * 128], lhsT=wukT[r, h // 2, :], rhs=dqT[r, h // 2, :],
                             start=True, stop=True)
            return i
        P.op("pe", mml, [self.b_dqT, lp], [p5s[0][1], p5s[1][1]])
        for h in range(4):
            P.v("act", "copy", [p5s[h % 2][1]], [self.b_qlatT], out=self.qlatT[:, h, :],
                in_=p5s[h % 2][0][:, (h // 2) * 128:(h // 2 + 1) * 128])
        P.v("dve", "tensor_scalar", [b_tb], [b_st], out=self.rs4[:, 0:8], in0=tb[:, 424:432], scalar1=0.0625, scalar2=None, op0=ALU.mult)
        P.v("dve", "tensor_tensor", [b_st] + c, [self.b_Dw], out=self.Dw[:], in0=bc(identf[:].unsqueeze(1), (128, 8, 128)),
            in1=bc(self.rs4[:, 0:8].unsqueeze(2), (128, 8, 128)), op=ALU.mult)
        score, b_score = self.score, self.b_score
        qidxT, kidx, Dw = self.qidxT, self.kidx, self.Dw
        nblk = (S + 511) // 512
        seq = [(kb, h) for kb in range(nblk) for h in range(8)]
        paccs = {}
        phs = {}

        def blk(kb):
            k0 = kb * 512
            return k0, min(512, S - k0)

        def idx_mm(kb, h):
            if kb not in paccs:
                paccs[kb] = self.bank()
            k0, n = blk(kb)
            ph, b_ph = self.bank(tuple(p[0] for p in paccs.values()))
            r = slice((h % 3) * 32, (h % 3) * 32 + 32)
            P.op("pe", lambda e, ph=ph, r=r, h=h, k0=k0, n=n: e.matmul(
                ph[:, 0:n], lhsT=qidxT[r, h // 3, :], rhs=kidx[r, k0:k0 + n], start=True, stop=True),
                [self.b_qidxT, self.b_kidx], [b_ph])
            phs[(kb, h)] = (ph, b_ph)
        idx_mm(*seq[0])
        for si, (kb, h) in enumerate(seq):
            if si + 1 < len(seq):
                idx_mm(*seq[si + 1])
            k0, n = blk(kb)
            pacc, b_pacc = paccs[kb]
            ph, b_ph = phs.pop((kb, h))
            rr, b_rr = self.rr[h % 2]
            if h % 2 == 0:
                P.v("act", "activation", [b_ph], [b_rr], out=rr[:, 0:n], in_=ph[:, 0:n], func=AF.Relu)
            else:
                P.v("dve", "tensor_scalar", [b_ph], [b_rr], out=rr[:, 0:n], in0=ph[:, 0:n], scalar1=0.0, scalar2=None, op0=ALU.max)
            P.op("pe", lambda e, pacc=pacc, rr=rr, h=h, n=n: e.matmul(
                pacc[:, 0:n], lhsT=Dw[:, h, :], rhs=rr[:, 0:n], start=(h == 0), stop=(h == 7)),
                [self.b_Dw, b_rr], [b_pacc])
            if h == 7:
                P.v("act", "copy", [b_pacc], [b_score], out=score[:, k0:k0 + n], in_=pacc[:, 0:n])
                del paccs[kb]
        bs, b_bs = self.bs, self.b_bs
        maskb, b_maskb = self.maskb, self.b_maskb
        if S > self.ksel:
            P.v("dve", "tensor_reduce", [b_score], [b_bs], out=bs[:, 0:1], in_=score[:, 0:S], axis=AX.X, op=ALU.min)
            P.v("dve", "tensor_reduce", [b_score], [b_bs], out=bs[:, 1:2], in_=score[:, 0:S], axis=AX.X, op=ALU.max)
            P.v("dve", "tensor_tensor", [b_score] + c, [b_score], out=score[:, T0:S], in0=score[:, T0:S], in1=self.caus[:], op=ALU.add)
            P.v("dve", "tensor_tensor", [b_bs], [b_bs], out=bs[:, 2:3], in0=bs[:, 1:2], in1=bs[:, 0:1], op=ALU.subtract)
            P.v("dve", "tensor_scalar", [b_bs] + c, [self.b_wk], out=self.wk[:], in0=self.pows[:], scalar1=bs[:, 2:3], scalar2=None, op0=ALU.mult)
            jb = self.Pm
            P.v("dve", "tensor_tensor", [b_bs, self.b_wk], [b_bs], out=bs[:, 3:4], in0=bs[:, 0:1], in1=self.wk[:, 0:1], op=ALU.add)
            for k in range(NBIS):
                P.v("dve", "tensor_scalar", [b_score, b_bs], [self.b_Pm, b_bs], out=jb[:, 0:S], in0=score[:, 0:S], scalar1=bs[:, 3:4],
                    scalar2=None, op0=ALU.is_ge, op1=ALU.add, accum_out=bs[:, 4:5])
                P.v("dve", "tensor_scalar", [b_bs], [b_bs], out=bs[:, 5:6], in0=bs[:, 4:5], scalar1=float(self.ksel) - 0.5, scalar2=0.5,
                    op0=ALU.is_ge, op1=ALU.subtract)
                P.v("dve", "scalar_tensor_tensor", [b_bs, self.b_wk], [b_bs], out=bs[:, 3:4], in0=bs[:, 5:6], scalar=self.wk[:, k:k + 1],
                    in1=bs[:, 3:4], op0=ALU.mult, op1=ALU.add)
            P.v("dve", "tensor_tensor", [b_bs, self.b_wk], [b_bs], out=bs[:, 0:1], in0=bs[:, 3:4], in1=self.wk[:, NBIS:NBIS + 1], op=ALU.subtract)
            P.v("dve", "tensor_scalar", [b_score, b_bs], [b_maskb], out=maskb[:, 0:S], in0=score[:, 0:S], scalar1=bs[:, 0:1], scalar2=NEG,
                op0=ALU.is_lt, op1=ALU.mult)
        else:
            if T0 > 0:
                P.v("pool", "memset", [], [b_maskb], ap=maskb[:, 0:T0], constant=0.0)
            P.v("pool", "tensor_copy", c, [b_maskb], out=maskb[:, T0:S], in_=self.causb[:])
        qlatT, ckvT, Pm, b_Pm = self.qlatT, self.ckvT, self.Pm, self.b_Pm
        PT, b_PT = self.PT[0]
        pol, b_pol = self.bank()
        nb = S // 128
        rs4, b_rs4 = self.rs4, self.b_rs4
        for h in range(4):
            for kb in range(nblk):
                k0 = kb * 512
                n = min(512, S - k0)
                pl, b_pl = self.bank((pol,))
                def mmlg(e, pl=pl, h=h, k0=k0, n=n):
                    e.matmul(pl[:, 0:n], lhsT=qlatT[:, h, :], rhs=ckvT[:, k0:k0 + n], start=True, stop=False)
                    return e.matmul(pl[:, 0:n], lhsT=identb[:], rhs=maskb[:, k0:k0 + n], start=False, stop=True)
                P.op("pe", mmlg, [self.b_qlatT, self.b_ckvT, b_maskb] + c, [b_pl])
                P.v("act", "copy", [b_pl], [b_score], out=score[:, k0:k0 + n], in_=pl[:, 0:n])
            P.v("dve", "tensor_reduce", [b_score], [b_bs], out=bs[:, 8:9], in_=score[:, 0:S], axis=AX.X, op=ALU.max, negate=True)
            P.v("act", "activation", [b_score, b_bs], [b_Pm, b_rs4], out=Pm[:, 0:S], in_=score[:, 0:S], func=AF.Exp, bias=bs[:, 8:9],
                accum_out=rs4[:, 8 + h - 8 + 0:8 + h - 8 + 1] if False else self.bs[:, 10 + h:11 + h])
            groups = [(g0, min(nb, g0 + 8)) for g0 in range(0, nb, 8)]
            tps = {}

            def issue_tr(gi):
                g0, g1 = groups[gi]
                ptp, b_ptp = self.bank((pol,))
                ptb = ptp[:].bitcast(BF16)

                def trp(e, g0=g0, g1=g1, ptb=ptb):
                    for b_ in range(g0, g1):
                        i = e.transpose(out=ptb[:, (b_ - g0) * 128:(b_ - g0 + 1) * 128], in_=Pm[:, b_ * 128:(b_ + 1) * 128], identity=identb[:])
                    return i
                P.op("pe", trp, [b_Pm] + c, [b_ptp])
                tps[gi] = (ptb, b_ptp)
            issue_tr(0)
            for gi, (g0, g1) in enumerate(groups):
                if gi + 1 < len(groups):
                    issue_tr(gi + 1)
                ptb, b_ptp = tps[gi]
                ng = g1 - g0
                P.v("act", "copy", [b_ptp], [b_PT], out=PT[:, 0:ng, :], in_=ptb[:, 0:ng * 128].rearrange("p (b i) -> p b i", b=ng))

                def mmpv(e, g0=g0, g1=g1, h=h):
                    for b_ in range(g0, g1):
                        i = e.matmul(pol[:, h * 128:(h + 1) * 128], lhsT=ckvK[:, b_, :], rhs=PT[:, b_ - g0, :],
                                     start=(b_ == 0), stop=(b_ == nb - 1))
                    return i
                P.op("pe", mmpv, [self.b_ckvK, b_PT], [b_pol])
        P.v("act", "copy", [b_pol], [self.b_olatT], out=self.olatT[:], in_=pol[:].rearrange("p (h i) -> p h i", h=4))
        py, b_py = self.bank()
        olatT, wuv = self.olatT, self.wuv

        def mmy(e):
            for h in range(4):
                i = e.matmul(py[:, h * 64:(h + 1) * 64], lhsT=olatT[:, h, :], rhs=wuv[:, h, :], start=True, stop=True)
            return i
        P.op("pe", mmy, [self.b_olatT, lp], [b_py])
        P.v("dve", "reciprocal", [b_bs], [b_bs], out=bs[:, 10:14], in_=bs[:, 10:14])
        P.v("dve", "tensor_tensor", [b_py, b_bs], [self.b_mix], out=self.mix[:, 512:768].rearrange("p (h d) -> p h d", h=4),
            in0=py[:, 0:256].rearrange("p (h d) -> p h d", h=4), in1=bc(bs[:, 10:14].unsqueeze(2), (128, 4, 64)), op=ALU.mult)


PARAM_NAMES = ["attn_norm", "w_in", "gdn_conv", "gdn_a_log", "gdn_dt_bias", "gdn_out_norm", "dsa_q_norm",
               "dsa_kv_norm", "dsa_w_uq", "dsa_w_qidx", "dsa_w_uk", "dsa_w_uv", "mlstm_i_bias",
               "mlstm_f_bias", "mlstm_out_norm", "w_out", "ffn_norm", "w_gate", "w_up", "w_down"]


FUSED = True
_CACHE = {}


def _builder(T, L, final, ksel):
    key = (T, L, final, ksel)
    if key not in _CACHE:
        _CACHE[key] = Builder(T, list(range(L)), final, ksel)
    return _CACHE[key]


def kernel(**inputs):
    x = np.ascontiguousarray(inputs["x"], dtype=np.float32)
    Bn, T, _ = x.shape
    depth = inputs["w_in"].shape[0]
    ksel = min(256, T // 4)
    prm = {k: np.ascontiguousarray(inputs[k], dtype=np.float32) for k in PARAM_NAMES}
    fin = np.ascontiguousarray(inputs["final_norm"], dtype=np.float32)
    cores = list(range(Bn))
    if FUSED:
        b = _builder(T, depth, True, ksel)
        in_maps = [dict(prm, final_norm=fin, x=x[i]) for i in range(Bn)]
        res = run_bass_kernel_spmd(b.nc, in_maps, core_ids=cores)
        return np.stack([np.asarray(r["y"]) for r in res.results], axis=0).astype(np.float32)
    cur = [x[i] for i in range(Bn)]
    for l in range(depth):
        last = l == depth - 1
        b = _builder(T, 1, last, ksel)
        lw = {k: np.ascontiguousarray(v[l:l + 1]) for k, v in prm.items()}
        in_maps = [dict(lw, final_norm=fin, x=np.ascontiguousarray(cur[i])) for i in range(Bn)]
        res = run_bass_kernel_spmd(b.nc, in_maps, core_ids=cores)
        cur = [np.asarray(r["y"]) for r in res.results]
    return np.stack(cur, axis=0).astype(np.float32)
```
